# Optimizing a Trainium2 kernel written in Bass

```python
import jax
import jax.numpy as jnp
from jax import lax
import numpy as np


D_MODEL = 1024
BATCH = 4
SEQ = 4096
DEPTH = 2

N_A_LAYERS = DEPTH // 2
N_B_LAYERS = DEPTH - N_A_LAYERS
HEAD_DIM = 64
MIX_WIDTH = D_MODEL
MEM_HEADS = 4
MEM_WIDTH = MEM_HEADS * HEAD_DIM
MAIN_WIDTH = MIX_WIDTH - MEM_WIDTH
CHUNK = 128
A_GROUPS = 6
A_GROUP_DIM = MAIN_WIDTH // A_GROUPS
FOX_HEADS = MAIN_WIDTH // HEAD_DIM
Q_BLOCK = 128
N_MEM = 256
D_FF = -(-8 * D_MODEL // (3 * 256)) * 256
RMS_EPS = 1e-6
LN_EPS = 1e-5
FORGET_BIAS = 4.0

kernel_name = "yoco_gmlp_fox_memory_hybrid"


def rmsnorm(x, g):
    xf = x.astype(jnp.float32)
    y = xf * lax.rsqrt(jnp.mean(xf * xf, axis=-1, keepdims=True) + RMS_EPS)
    return (y * g.astype(jnp.float32)).astype(x.dtype)


def layernorm(x, g, b):
    xf = x.astype(jnp.float32)
    mu = jnp.mean(xf, axis=-1, keepdims=True)
    xc = xf - mu
    y = xc * lax.rsqrt(jnp.mean(xc * xc, axis=-1, keepdims=True) + LN_EPS)
    return (y * g.astype(jnp.float32) + b.astype(jnp.float32)).astype(x.dtype)


def memory_attention(q_mem, mem_n, w_mem_kv):
    B, S, _ = q_mem.shape
    M = mem_n.shape[1]
    k, v = jnp.split(mem_n @ w_mem_kv, 2, axis=-1)
    q = q_mem.reshape(B, S, MEM_HEADS, HEAD_DIM)
    k = k.reshape(B, M, MEM_HEADS, HEAD_DIM)
    v = v.reshape(B, M, MEM_HEADS, HEAD_DIM)
    logits = jnp.einsum('bshd,bmhd->bhsm', q, k).astype(jnp.float32) * (HEAD_DIM ** -0.5)
    p = jax.nn.softmax(logits, axis=-1).astype(v.dtype)
    o = jnp.einsum('bhsm,bmhd->bshd', p, v)
    return o.reshape(B, S, MEM_WIDTH)


def chunked_spatial_gating(u, v, w_s, b_s, ln_g, ln_b):
    B, S, _ = u.shape
    u = jax.nn.gelu(u)
    v = layernorm(jax.nn.gelu(v), ln_g, ln_b)
    nc = S // CHUNK
    vc = v.reshape(B, nc, CHUNK, A_GROUPS, A_GROUP_DIM)
    mask = jnp.tril(jnp.ones((CHUNK, CHUNK), dtype=bool))
    w = jnp.where(mask[None], w_s, jnp.zeros((), w_s.dtype))
    s = jnp.einsum('gts,bcsgd->bctgd', w, vc) + b_s.T[None, None, :, :, None]
    return u * s.reshape(B, S, MAIN_WIDTH)


def forgetting_attention(q, k, v, log_f):
    B, S, H, Dh = q.shape
    c = jnp.cumsum(log_f, axis=1)
    cT = c.transpose(0, 2, 1)
    nb = S // Q_BLOCK
    qb = q.reshape(B, nb, Q_BLOCK, H, Dh).transpose(1, 0, 2, 3, 4)
    cb = cT.reshape(B, H, nb, Q_BLOCK).transpose(2, 0, 1, 3)
    kpos = jnp.arange(S)
    scale = Dh ** -0.5

    def block(args):
        i, qi, ci = args
        qpos = i * Q_BLOCK + jnp.arange(Q_BLOCK)
        logits = jnp.einsum('bthd,bshd->bhts', qi, k).astype(jnp.float32) * scale
        logits = logits + ci[..., :, None] - cT[..., None, :]
        mask = kpos[None, :] <= qpos[:, None]
        logits = jnp.where(mask, logits, -jnp.inf)
        p = jax.nn.softmax(logits, axis=-1).astype(v.dtype)
        return jnp.einsum('bhts,bshd->bthd', p, v)

    o = lax.map(block, (jnp.arange(nb), qb, cb))
    return o.transpose(1, 0, 2, 3, 4).reshape(B, S, H * Dh)


def setup_inputs(seed: int = 0) -> dict:
    key = jax.random.key(seed)
    ks = jax.random.split(key, 24)
    f32 = jnp.float32

    def dense(k, shape, fan_in):
        return jax.random.normal(k, shape, f32) * (fan_in ** -0.5)

    def gain(k, shape):
        return 1.0 + 0.05 * jax.random.normal(k, shape, f32)

    x = jax.random.normal(ks[0], (BATCH, SEQ, D_MODEL), f32)
    mem = jax.random.normal(ks[1], (BATCH, N_MEM, D_MODEL), f32)
    w_shared_kv = jnp.concatenate([
        dense(ks[19], (D_MODEL, 2 * MAIN_WIDTH), D_MODEL),
        0.1 * dense(ks[20], (D_MODEL, FOX_HEADS), D_MODEL)], axis=-1)
    return {
        'x': x,
        'mem': mem,
        'ln_mix_pre': gain(ks[2], (DEPTH, D_MODEL)),
        'ln_mix_post': gain(ks[3], (DEPTH, D_MODEL)),
        'ln_ffn_pre': gain(ks[4], (DEPTH, D_MODEL)),
        'ln_ffn_post': gain(ks[5], (DEPTH, D_MODEL)),
        'ln_mem': gain(ks[6], (DEPTH, D_MODEL)),
        'w_mem_kv': dense(ks[7], (DEPTH, D_MODEL, 2 * MEM_WIDTH), D_MODEL),
        'w_out': dense(ks[8], (DEPTH, MIX_WIDTH, D_MODEL), MIX_WIDTH),
        'w_ffn_gate': dense(ks[9], (DEPTH, D_MODEL, D_FF), D_MODEL),
        'w_ffn_up': dense(ks[10], (DEPTH, D_MODEL, D_FF), D_MODEL),
        'w_ffn_down': dense(ks[11], (DEPTH, D_FF, D_MODEL), D_FF),
        'w_in_a': dense(ks[12], (N_A_LAYERS, D_MODEL, 2 * MAIN_WIDTH + MEM_WIDTH), D_MODEL),
        'w_spatial': dense(ks[13], (N_A_LAYERS, A_GROUPS, CHUNK, CHUNK), CHUNK),
        'b_spatial': 1.0 + 0.05 * jax.random.normal(ks[14], (N_A_LAYERS, A_GROUPS, CHUNK), f32),
        'ln_v_g': gain(ks[15], (N_A_LAYERS, MAIN_WIDTH)),
        'ln_v_b': 0.02 * jax.random.normal(ks[16], (N_A_LAYERS, MAIN_WIDTH), f32),
        'ln_shared': gain(ks[17], (D_MODEL,)),
        'w_shared_kv': w_shared_kv,
        'b_forget': FORGET_BIAS + 0.1 * jax.random.normal(ks[18], (FOX_HEADS,), f32),
        'w_in_b': dense(ks[21], (N_B_LAYERS, D_MODEL, MAIN_WIDTH + MEM_WIDTH), D_MODEL),
    }


def reference(x, mem, ln_mix_pre, ln_mix_post, ln_ffn_pre, ln_ffn_post, ln_mem,
              w_mem_kv, w_out, w_ffn_gate, w_ffn_up, w_ffn_down, w_in_a,
              w_spatial, b_spatial, ln_v_g, ln_v_b, ln_shared, w_shared_kv,
              b_forget, w_in_b):
    B, S, _ = x.shape
    h = x
    k_s = v_s = log_f_s = None
    for layer in range(DEPTH):
        a = rmsnorm(h, ln_mix_pre[layer])
        mem_n = rmsnorm(mem, ln_mem[layer])
        if layer < N_A_LAYERS:
            proj = a @ w_in_a[layer]
            u = proj[..., :MAIN_WIDTH]
            v = proj[..., MAIN_WIDTH:2 * MAIN_WIDTH]
            q_mem = proj[..., 2 * MAIN_WIDTH:]
            main = chunked_spatial_gating(u, v, w_spatial[layer], b_spatial[layer],
                                          ln_v_g[layer], ln_v_b[layer])
        else:
            if layer == N_A_LAYERS:
                s_in = rmsnorm(h, ln_shared)
                kvf = s_in @ w_shared_kv
                k_s = kvf[..., :MAIN_WIDTH].reshape(B, S, FOX_HEADS, HEAD_DIM)
                v_s = kvf[..., MAIN_WIDTH:2 * MAIN_WIDTH].reshape(B, S, FOX_HEADS, HEAD_DIM)
                log_f_s = jax.nn.log_sigmoid(kvf[..., 2 * MAIN_WIDTH:].astype(jnp.float32)
                                             + b_forget.astype(jnp.float32))
            proj = a @ w_in_b[layer - N_A_LAYERS]
            q = proj[..., :MAIN_WIDTH].reshape(B, S, FOX_HEADS, HEAD_DIM)
            q_mem = proj[..., MAIN_WIDTH:]
            main = forgetting_attention(q, k_s, v_s, log_f_s)
        mixed = jnp.concatenate([main, memory_attention(q_mem, mem_n, w_mem_kv[layer])], axis=-1)
        h = h + rmsnorm(mixed @ w_out[layer], ln_mix_post[layer])
        f = rmsnorm(h, ln_ffn_pre[layer])
        f = (jax.nn.silu(f @ w_ffn_gate[layer]) * (f @ w_ffn_up[layer])) @ w_ffn_down[layer]
        h = h + rmsnorm(f, ln_ffn_post[layer])
    return h
```

```python
import contextlib
import numpy as np
import concourse.bass as bass
import concourse.mybir as mybir
from concourse.bass_utils import run_bass_kernel_spmd

F32 = mybir.dt.float32
BF16 = mybir.dt.bfloat16
ALU = mybir.AluOpType
AF = mybir.ActivationFunctionType
AX = mybir.AxisListType

N_DMA_SEMS = 12


class View:
    __slots__ = ("buf", "ap", "lo", "hi")

    def __init__(self, buf, ap, lo, hi):
        self.buf, self.ap, self.lo, self.hi = buf, ap, lo, hi


class Buf:
    def __init__(self, name, full_ap, shape, kind):
        self.name, self.full, self.shape, self.kind = name, full_ap, list(shape), kind
        first = 1 if kind != "dram" else 0
        self.first = first
        st = [0] * len(shape)
        s = 1
        for i in range(len(shape) - 1, first - 1, -1):
            st[i] = s
            s *= shape[i]
        self.strides = st
        self.size = s
        self.recs = []

    def __getitem__(self, idx):
        if not isinstance(idx, tuple):
            idx = (idx,)
        idx = tuple(idx) + (slice(None),) * (len(self.shape) - len(idx))
        lo, hi = 0, 1
        for d, (i, n) in enumerate(zip(idx, self.shape)):
            if isinstance(i, slice):
                a = 0 if i.start is None else i.start
                b = n if i.stop is None else i.stop
                assert i.step in (None, 1) and 0 <= a < b <= n, (self.name, idx)
            else:
                assert 0 <= i < n, (self.name, idx)
                a, b = i, i + 1
            if d >= self.first:
                lo += a * self.strides[d]
                hi += (b - 1) * self.strides[d]
        return View(self, self.full[idx], lo, hi)

    def all(self):
        return self[tuple(slice(None) for _ in self.shape)]


class Op:
    __slots__ = ("eng", "fn", "deps", "is_dma", "pos", "need_inc", "count", "waits", "sem", "semval", "guard", "idx", "inc")


class Prog:
    def __init__(self, nc):
        self.nc = nc
        self.stack = contextlib.ExitStack()
        self.ops = []
        self.eng_ops = {e: [] for e in ("pe", "act", "dve", "pool", "sp")}
        self.n_dma = {e: 0 for e in self.eng_ops}
        self.nbuf = 0

    def sbuf(self, name, shape, dtype):
        t = self.stack.enter_context(self.nc.sbuf_tensor("sb_" + name, list(shape), dtype))
        return Buf(name, t[:] if not hasattr(t, "ap") else t.ap(), shape, "sbuf")

    def psum(self, name, shape, dtype):
        t = self.stack.enter_context(self.nc.psum_tensor("ps_" + name, list(shape), dtype))
        b = Buf(name, t[:] if not hasattr(t, "ap") else t.ap(), shape, "psum")
        b.bank_elems = 512 if dtype == F32 else 1024
        return b

    def dram(self, name, shape, dtype, kind="Internal"):
        t = self.nc.dram_tensor(name, list(shape), dtype, kind=kind)
        return Buf(name, t.ap(), shape, "dram")

    def op(self, eng, fn, reads=(), writes=(), dma=False, cc=False):
        o = Op()
        o.eng, o.fn, o.is_dma = eng, fn, dma
        o.deps = {}
        o.need_inc = False
        o.idx = len(self.ops)

        def norm(v):
            be = getattr(v.buf, "bank_elems", None)
            if v is None or be is None:
                return v
            return View(v.buf, v.ap, v.lo // be * be, -(-v.hi // be) * be)
        reads = [norm(v) for v in reads if v is not None]
        writes = [norm(v) for v in writes if v is not None]
        for v in reads:
            if v is None:
                continue
            for r in v.buf.recs:
                if r[0] < v.hi and v.lo < r[1]:
                    if r[2] is not None:
                        o.deps.setdefault(r[2], "raw")
                    r[3].append(o)
        for v in writes:
            keep = []
            for r in v.buf.recs:
                if r[0] < v.hi and v.lo < r[1]:
                    if r[2] is not None:
                        if o.deps.get(r[2]) != "raw":
                            o.deps[r[2]] = "waw"
                    for rd in r[3]:
                        if rd is not o:
                            o.deps.setdefault(rd, "war")
                    if v.lo <= r[0] and r[1] <= v.hi:
                        continue
                keep.append(r)
            keep.append([v.lo, v.hi, o, []])
            v.buf.recs = keep
        o.pos = len(self.eng_ops[eng])
        self.eng_ops[eng].append(o)
        self.ops.append(o)
        o.inc = 16
        if cc:
            self.n_cc = getattr(self, "n_cc", 0) + 1
            o.is_dma = True
            o.sem = ("cc", 0)
            o.semval = self.n_cc
            o.guard = 0
            o.inc = 1
        elif dma:
            k = self.n_dma[eng]
            self.n_dma[eng] += 1
            o.sem = (eng, k % N_DMA_SEMS)
            o.semval = 16 * (k // N_DMA_SEMS + 1)
            o.guard = 16 * (k // N_DMA_SEMS) if k >= N_DMA_SEMS else 0
        return o

    def mm(self, out, lhsT, rhs, start=True, stop=True, **kw):
        return self.op("pe", lambda e: e.matmul(out.ap, lhsT.ap, rhs.ap, start=start, stop=stop, **kw),
                       reads=[lhsT, rhs] + ([] if start else []), writes=[out])

    def transpose(self, out, in_, ident):
        return self.op("pe", lambda e: e.transpose(out.ap, in_.ap, ident.ap), reads=[in_, ident], writes=[out])

    def act(self, out, in_, func, bias=None, scale=1.0, accum=None, eng="act"):
        def fn(e):
            kw = {}
            if bias is not None:
                kw["bias"] = bias.ap if isinstance(bias, View) else bias
            if accum is not None:
                kw["accum_out"] = accum.ap
            sc = scale.ap if isinstance(scale, View) else scale
            return e.activation(out.ap, in_.ap, func, scale=sc, **kw)
        rd = [in_] + [x for x in (bias, scale) if isinstance(x, View)]
        wr = [out] + ([accum] if accum is not None else [])
        return self.op(eng, fn, reads=rd, writes=wr)

    def tt(self, eng, out, a, b, op):
        return self.op(eng, lambda e: e.tensor_tensor(out.ap, a.ap, b.ap, op), reads=[a, b], writes=[out])

    def ts(self, eng, out, a, s1, s2, op0, op1=None, accum=None):
        def fn(e):
            x1 = s1.ap if isinstance(s1, View) else s1
            x2 = s2.ap if isinstance(s2, View) else s2
            kw = {}
            if accum is not None:
                kw["accum_out"] = accum.ap
            if op1 is None:
                return e.tensor_scalar(out.ap, a.ap, x1, x2, op0, **kw)
            return e.tensor_scalar(out.ap, a.ap, x1, x2, op0, op1, **kw)
        rd = [a] + [x for x in (s1, s2) if isinstance(x, View)]
        wr = [out] + ([accum] if accum is not None else [])
        return self.op(eng, fn, reads=rd, writes=wr)

    def stt(self, eng, out, a, s, b, op0, op1):
        def fn(e):
            x = s.ap if isinstance(s, View) else s
            return e.scalar_tensor_tensor(out.ap, a.ap, x, b.ap, op0, op1)
        rd = [a, b] + ([s] if isinstance(s, View) else [])
        return self.op(eng, fn, reads=rd, writes=[out])

    def copy(self, eng, out, in_):
        if eng == "act":
            return self.op(eng, lambda e: e.copy(out.ap, in_.ap), reads=[in_], writes=[out])
        return self.op(eng, lambda e: e.tensor_copy(out.ap, in_.ap), reads=[in_], writes=[out])

    def memset(self, eng, out, val):
        return self.op(eng, lambda e: e.memset(out.ap, val), writes=[out])

    def dma(self, eng, out, in_, **kw):
        return self.op(eng, lambda e: e.dma_start(out=out.ap, in_=in_.ap, **kw), reads=[in_], writes=[out], dma=True)

    def finalize(self, final_waits=()):
        nc = self.nc
        seen = {e: {f: -1 for f in self.eng_ops} for e in self.eng_ops}
        seen_dma = {e: {} for e in self.eng_ops}
        for o in self.ops:
            o.waits = []
            E = o.eng
            for y, kind in sorted(o.deps.items(), key=lambda kv: -kv[0].idx):
                if y.is_dma:
                    if seen_dma[E].get(y.sem, 0) >= y.semval:
                        continue
                    seen_dma[E][y.sem] = y.semval
                    o.waits.append(("dma", y.sem, y.semval))
                    continue
                F = y.eng
                if F == E and not o.is_dma:
                    if E == "pe":
                        continue
                if seen[E][F] >= y.pos:
                    continue
                seen[E][F] = y.pos
                y.need_inc = True
                o.waits.append(("eng", y))
            if o.is_dma and o.guard:
                if seen_dma[E].get(o.sem, 0) < o.guard:
                    seen_dma[E][o.sem] = o.guard
                    o.waits.append(("dma", o.sem, o.guard))
        for e, lst in self.eng_ops.items():
            c = 0
            for o in lst:
                if o.need_inc:
                    c += 1
                    o.count = c
        esem = {e: self.stack.enter_context(nc.semaphore("s_" + e)) for e in self.eng_ops}
        dsem = {}
        for e in self.eng_ops:
            if self.n_dma[e]:
                for i in range(min(N_DMA_SEMS, self.n_dma[e])):
                    dsem[(e, i)] = self.stack.enter_context(nc.semaphore("d_%s_%d" % (e, i)))
        if getattr(self, "n_cc", 0):
            dsem[("cc", 0)] = self.stack.enter_context(nc.semaphore("d_cc"))
        self.stats = {e: (len(l), sum(1 for o in l if o.need_inc), sum(len(o.waits) for o in l)) for e, l in self.eng_ops.items()}
        block = self.stack.enter_context(nc.Block())

        def emit(engobj, lst, extra):
            for o in lst:
                for w in o.waits:
                    if w[0] == "dma":
                        engobj.wait_ge(dsem[w[1]], w[2])
                    else:
                        engobj.wait_ge(esem[w[1].eng], w[1].count)
                ins = o.fn(engobj)
                if o.is_dma:
                    ins.then_inc(dsem[o.sem], o.inc)
                elif o.need_inc:
                    ins.then_inc(esem[o.eng], 1)
            for o in extra:
                if o.is_dma:
                    engobj.wait_ge(dsem[o.sem], o.semval)
                else:
                    engobj.wait_ge(esem[o.eng], o.count)

        fw = list(final_waits)
        for o in fw:
            if not o.is_dma and not o.need_inc:
                raise RuntimeError("final wait on op without inc")

        @block.tensor
        def _(e):
            emit(e, self.eng_ops["pe"], [])

        @block.scalar
        def _(e):
            emit(e, self.eng_ops["act"], [])

        @block.vector
        def _(e):
            emit(e, self.eng_ops["dve"], [])

        @block.gpsimd
        def _(e):
            emit(e, self.eng_ops["pool"], [])

        @block.sync
        def _(e):
            emit(e, self.eng_ops["sp"], fw)

        self.stack.close()


D = 1024
T = 2048
TG = 1024
SG = 512
NT = T // 128
DFF = 2816
NJ = DFF // 128
EPS = 1e-6
LN_EPS = 1e-5


def hull(buf, ap, lo, hi):
    return View(buf, ap, lo, hi)


def build(ncores=8, stop=None):
    nc = bass.Bass("TRN2", target_bir_lowering=False)
    P = Prog(nc)

    def din(name, shape, dt=F32):
        return P.dram(name, shape, dt, kind="ExternalInput")

    ident_d = din("ident", [128, 128])
    tri_d = din("tri", [128, 128])
    m0_d = din("m0", [128, 128])
    m1_d = din("m1", [128, 128])
    x_d = din("x", [T, D])
    mem_d = din("mem", [256, D])
    g_d = [din(n, [2, D]) for n in ("ln_mix_pre", "ln_mix_post", "ln_ffn_pre", "ln_ffn_post")]
    g_mem_d = din("ln_mem", [2, D])
    g_sh_d = din("ln_shared", [D])
    w_mkv_d = din("w_mem_kv", [2, D, 512])
    w_out_d = din("w_out", [2, D, D])
    w_g_d = din("w_ffn_gate", [2, D, DFF])
    w_u_d = din("w_ffn_up", [2, D, DFF])
    w_d_d = din("w_ffn_down", [2, DFF, D])
    w_ina_d = din("w_in_a", [1, D, 1792])
    w_sp_d = din("w_spatial", [1, 6, 128, 128])
    b_sp_d = din("b_spatial", [1, 6, 128])
    lnvg_d = din("ln_v_g", [1, 768])
    lnvb_d = din("ln_v_b", [1, 768])
    w_skv_d = din("w_shared_kv", [D, 1548])
    bfor_d = din("b_forget", [12])
    w_inb_d = din("w_in_b", [1, D, D])
    out_d = P.dram("out", [T, D], F32, kind="ExternalOutput")
    kT_loc = [P.dram("kT_loc%d" % g, [768, TG], BF16) for g in range(2)]
    v_loc = [P.dram("v_loc%d" % g, [TG, 768], BF16) for g in range(2)]
    lf_loc = P.dram("lf_loc", [T, 12], F32)
    kT_all = [P.dram("kT_all%d" % g, [2 * 768, TG], BF16) for g in range(2)]
    v_all = [P.dram("v_all%d" % g, [2 * TG, 768], BF16) for g in range(2)]
    lf_all = P.dram("lf_all", [2 * T, 12], F32)
    rg = [[2 * i, 2 * i + 1] for i in range(ncores // 2)]

    def gather(src, dst):
        P.op("pool", lambda e, s_=src, d_=dst: e.collective_compute("AllGather", ALU.bypass, replica_groups=rg,
                                                                     ins=[s_.full.opt()], outs=[d_.full.opt()]),
             reads=[src.all()], writes=[dst.all()], cc=True)

    hT = P.sbuf("hTs", [128, 8, T], F32)
    xn = P.sbuf("xn", [128, 8 * TG], BF16)
    ARENA_B = 88 * 1024
    arena = P.sbuf("arena", [128, ARENA_B // 2], BF16)
    arena32 = arena.full.bitcast(F32)
    xn32 = xn.full.bitcast(F32)
    psF = P.psum("psF", [128, 7 * 512], F32)
    psT = P.psum("psT", [128, 1024], BF16)

    def bank(b, n=512, o=0):
        return psF[:, 512 * b + o:512 * b + o + n]

    class AV:
        def __init__(self, off_b, shape, dt=BF16, base=None, base32=None):
            base = arena if base is None else base
            base32 = arena32 if base32 is None else base32
            self.base = base
            n = 1
            for s_ in shape:
                n *= s_
            if dt == BF16:
                e0 = off_b // 2
                ap = base.full[:, e0:e0 + n]
                self.lo, self.hi, self.esz = e0, e0 + n, 1
            else:
                e0 = off_b // 4
                ap = base32[:, e0:e0 + n]
                self.lo, self.hi, self.esz = off_b // 2, off_b // 2 + 2 * n, 2
            self.shape = list(shape)
            if len(shape) > 1:
                names = " ".join("a%d" % i for i in range(len(shape)))
                kw = {"a%d" % i: s_ for i, s_ in enumerate(shape)}
                ap = ap.rearrange("p (%s) -> p %s" % (names, names), **kw)
            self.ap = ap
            st = [0] * len(shape)
            s_ = 1
            for i in range(len(shape) - 1, -1, -1):
                st[i] = s_
                s_ *= shape[i]
            self.st = st

        def __getitem__(self, idx):
            if not isinstance(idx, tuple):
                idx = (idx,)
            idx = tuple(idx) + (slice(None),) * (1 + len(self.shape) - len(idx))
            lo, hi = 0, 1
            for d, (i, n) in enumerate(zip(idx[1:], self.shape)):
                if isinstance(i, slice):
                    a = 0 if i.start is None else i.start
                    b = n if i.stop is None else i.stop
                else:
                    a, b = i, i + 1
                assert 0 <= a < b <= n
                lo += a * self.st[d]
                hi += (b - 1) * self.st[d]
            return View(self.base, self.ap[idx], self.lo + lo * self.esz, self.lo + hi * self.esz)

    io = AV(32768, [2, D], F32)
    gmem = AV(24576, [D], F32)

    class YV:
        def __getitem__(self, idx):
            p, c, cols = idx
            a = 0 if cols.start is None else cols.start
            b = 512 if cols.stop is None else cols.stop
            e0 = c * 512 + a
            return View(xn, xn32[p, e0:e0 + (b - a)], 2 * e0, 2 * (e0 + b - a))
    y = YV()

    def xnv(c, t0, n, p=slice(None)):
        return xn[p, c * TG + t0:c * TG + t0 + n]

    ident = P.sbuf("ident", [128, 128], F32)
    identb = P.sbuf("identb", [128, 128], BF16)
    tri = P.sbuf("tri", [128, 128], F32)
    onesS = P.sbuf("onesS", [128, 128], BF16)
    ones32 = P.sbuf("ones32", [128, 128], F32)
    gains = P.sbuf("gains", [128, 9, 8], F32)
    KmT = P.sbuf("KmT", [128, 2, 256], BF16)
    Vm = P.sbuf("Vm", [128, 2, 4, 128], BF16)
    sq = P.sbuf("sq", [128, 2, 512], BF16)
    rstd = P.sbuf("rstd", [128, 512], F32)
    tmpn = P.sbuf("tmpn", [128, 2, 512], F32)
    st = P.sbuf("st", [128, 16], F32)
    PT = P.sbuf("PT", [128, 6, 512], BF16)
    Rb = P.sbuf("Rb", [128, 2, 512], F32)
    WmT = P.sbuf("WmT", [128, 6, 128], BF16)
    bsp = P.sbuf("bsp", [128, 6], F32)
    bfor = P.sbuf("bfor", [128, 12], F32)
    lfst = P.sbuf("lfst", [128, NT, 12], F32)
    msk = P.sbuf("msk", [128, 2, 128], BF16)
    lshare = P.sbuf("lshare", [128, 3072], BF16)
    lshare32 = lshare.full.bitcast(F32)
    lnvg = AV(0, [768], F32, lshare, lshare32)
    lnvb = AV(3072, [768], F32, lshare, lshare32)
    BT = AV(0, [4, 2, NT, 12], F32, lshare, lshare32)

    P.dma("sp", ident.all(), ident_d.all())
    P.dma("sp", tri.all(), tri_d.all())
    P.copy("dve", identb.all(), ident.all())
    P.memset("dve", onesS.all(), 1.0 / 1024.0)
    P.memset("dve", ones32.all(), 1.0)
    P.memset("dve", Vm.all(), 1.0)

    def gcol(dst_i, src, ap):
        P.dma("sp", gains[:, dst_i, :], hull(src, ap.rearrange("(c p) -> p c", p=128), 0, src.size),
              allow_slow_non_contiguous=True)
    for L_ in range(2):
        for q in range(4):
            gcol(L_ * 4 + q, g_d[q], g_d[q].full[L_])
    gcol(8, g_sh_d, g_sh_d.full)

    def wload(dst, wd, wap, c0, c1, eng="pool"):
        src = wap[:, c0:c1].rearrange("(k p) c -> p k c", p=128)
        return P.dma(eng, dst, hull(wd, src, 0, wd.size))

    WA = AV(0, [8, 1792])
    WB = AV(28672, [8, 1024])
    mixT = AV(45056, [8, TG])
    qmT = AV(61440, [2, TG])
    TMP0 = 65536
    actb = AV(0, [NJ, TG])
    GU = [AV(45056 + i * 4096, [2, 8, 128]) for i in range(4)]
    DN = [AV(61440 + i * 5632, [NJ, 128]) for i in range(3)]
    QT = AV(0, [6, TG])
    WIN = AV(12288, [8, 1024])
    KTp = AV(65536, [2, T])
    VP = AV(73728, [2, NT, 2, 128])

    def mem_prep(L):
        wmk = AV(0, [8, 512])
        memT = AV(8192, [8, 256])
        memn = AV(8192 + 4096, [2, D])
        P.dma("sp", gmem[:, :], hull(g_mem_d, g_mem_d.full[L].partition_broadcast(128), 0, g_mem_d.size))
        wload(wmk[:, :, :], w_mkv_d, w_mkv_d.full[L], 0, 512)
        for mt in range(2):
            mtile = io[:, mt, :]
            P.dma("sp", mtile, mem_d[mt * 128:(mt + 1) * 128, :])
            P.act(tmpn[:, 0, :], io[:, mt, 0:512], AF.Square, accum=st[:, 0:1])
            P.act(tmpn[:, 1, :], io[:, mt, 512:1024], AF.Square, accum=st[:, 1:2])
            P.tt("dve", st[:, 2:3], st[:, 0:1], st[:, 1:2], ALU.add)
            P.act(st[:, 3:4], st[:, 2:3], AF.Sqrt, bias=EPS, scale=1.0 / D)
            P.op("dve", lambda e, o=st[:, 4:5], i=st[:, 3:4]: e.reciprocal(o.ap, i.ap), reads=[st[:, 3:4]], writes=[st[:, 4:5]])
            P.stt("dve", memn[:, mt, :], mtile, st[:, 4:5], gmem[:, :], ALU.mult, ALU.mult)
            for c in range(8):
                P.transpose(psT[:, c * 128:(c + 1) * 128], memn[:, mt, c * 128:(c + 1) * 128], identb.all())
            P.copy("dve", memT[:, :, mt * 128:(mt + 1) * 128],
                   hull(psT, psT.full.rearrange("p (c t) -> p c t", t=128), 0, 1024))
        for m in range(2):
            for k in range(8):
                P.mm(bank(4 + m, 256), wmk[:, k, m * 128:(m + 1) * 128], memT[:, k, :], start=(k == 0), stop=(k == 7))
            P.copy("act", KmT[:, m, :], bank(4 + m, 256))
        for mt in range(2):
            for k in range(8):
                P.mm(bank(mt, 256), memT[:, k, mt * 128:(mt + 1) * 128], wmk[:, k, 256:512], start=(k == 0), stop=(k == 7))
            P.copy("dve", Vm[:, mt, :, 0:64],
                   hull(psF, psF.full[:, 512 * mt:512 * mt + 256].rearrange("p (h d) -> p h d", d=64), 512 * mt, 512 * mt + 256))

    def rms_stats(src_fn):
        for c in range(8):
            P.act(sq[:, c % 2, :], src_fn(c), AF.Square)
            P.mm(bank(6), onesS.all(), sq[:, c % 2, :], start=(c == 0), stop=(c == 7))
        P.act(tmpn[:, 0, :], bank(6), AF.Sqrt, bias=EPS)
        P.op("dve", lambda e: e.reciprocal(rstd.all().ap, tmpn[:, 0, :].ap), reads=[tmpn[:, 0, :]], writes=[rstd.all()])

    def norm_to_xn(gi, g0, s):
        t0 = g0 + s * SG
        rms_stats(lambda c: hT[:, c, t0:t0 + SG])
        for c in range(8):
            P.stt("dve", xnv(c, s * SG, SG), hT[:, c, t0:t0 + SG], gains[:, gi, c:c + 1], rstd.all(), ALU.mult, ALU.mult)

    def resid_update(gi, g0, s):
        t0 = g0 + s * SG
        rms_stats(lambda c: y[:, c, :])
        for c in range(8):
            P.tt("pool", tmpn[:, c % 2, :], y[:, c, :], rstd.all(), ALU.mult)
            P.stt("dve", hT[:, c, t0:t0 + SG], tmpn[:, c % 2, :], gains[:, gi, c:c + 1], hT[:, c, t0:t0 + SG], ALU.mult, ALU.add)

    def mem_attn(s):
        c0 = s * SG
        for hm in range(4):
            m, r0 = hm // 2, (hm % 2) * 64
            for mt in range(2):
                P.mm(bank(mt), KmT[r0:r0 + 64, m, mt * 128:(mt + 1) * 128], qmT[r0:r0 + 64, m, c0:c0 + SG])
                P.act(PT[:, mt, :], bank(mt), AF.Exp, scale=0.125)
            ob = 2 + hm % 2
            for mt in range(2):
                P.mm(bank(ob), Vm[:, mt, hm, :], PT[:, mt, :], start=(mt == 0), stop=(mt == 1))
            rv = Rb[64:128, hm % 2, :]
            ov = psF[64:128, 512 * ob:512 * ob + 512]
            P.op("dve", lambda e, o=rv, i=ov: e.reciprocal(o.ap, i.ap), reads=[ov], writes=[rv])
            P.tt("dve", mixT[r0:r0 + 64, 6 + m, c0:c0 + SG], psF[0:64, 512 * ob:512 * ob + 512], rv, ALU.mult)

    def out_proj_and_ffn(L, g0):
        for s in range(2):
            c0 = s * SG
            for m in range(8):
                b = 4 + m % 2
                for k in range(8):
                    P.mm(bank(b), WB[:, k, m * 128:(m + 1) * 128], mixT[:, k, c0:c0 + SG], start=(k == 0), stop=(k == 7))
                P.copy("act", y[:, m, :], bank(b))
            resid_update(L * 4 + 1, g0, s)
        for s in range(2):
            norm_to_xn(L * 4 + 2, g0, s)

        def load_gu(j):
            slot = GU[j % 4]
            wload(slot[:, 0, :, :], w_g_d, w_g_d.full[L], j * 128, (j + 1) * 128)
            wload(slot[:, 1, :, :], w_u_d, w_u_d.full[L], j * 128, (j + 1) * 128)
        for j in range(3):
            load_gu(j)
        for j in range(NJ):
            if j + 3 < NJ:
                load_gu(j + 3)
            slot = GU[j % 4]
            for s in range(2):
                c0 = s * SG
                gb, ub = s, 2 + s
                for k in range(8):
                    P.mm(bank(gb), slot[:, 0, k, :], xnv(k, c0, SG), start=(k == 0), stop=(k == 7))
                for k in range(8):
                    P.mm(bank(ub), slot[:, 1, k, :], xnv(k, c0, SG), start=(k == 0), stop=(k == 7))
                P.act(tmpn[:, s, :], bank(gb), AF.Silu)
                P.tt("dve", actb[:, j, c0:c0 + SG], bank(ub), tmpn[:, s, :], ALU.mult)

        def load_dn(q):
            m = q % 8
            wload(DN[q % 3][:, :, :], w_d_d, w_d_d.full[L], m * 128, (m + 1) * 128)
        load_dn(0)
        load_dn(1)
        for s in range(2):
            c0 = s * SG
            for m in range(8):
                q = s * 8 + m
                if q + 2 < 16:
                    load_dn(q + 2)
                b = 4 + m % 2
                for j in range(NJ):
                    P.mm(bank(b), DN[q % 3][:, j, :], actb[:, j, c0:c0 + SG], start=(j == 0), stop=(j == NJ - 1))
                P.copy("act", y[:, m, :], bank(b))
            resid_update(L * 4 + 3, g0, s)

    mem_prep(0)
    wsp32 = AV(16384, [6, 128], F32)
    P.dma("sp", wsp32[:, :, :], hull(w_sp_d, w_sp_d.full[0].rearrange("g t s -> t g s"), 0, w_sp_d.size))
    for g in range(6):
        P.transpose(bank(g % 4, 128), wsp32[:, g, :], ident.all())
        P.tt("dve", WmT[:, g, :], bank(g % 4, 128), tri.all(), ALU.mult)
    P.dma("sp", bsp.all(), hull(b_sp_d, b_sp_d.full[0].rearrange("g t -> t g"), 0, b_sp_d.size),
          allow_slow_non_contiguous=True)
    P.dma("sp", lnvg[:, :], hull(lnvg_d, lnvg_d.full[0].partition_broadcast(128), 0, 768))
    P.dma("sp", lnvb[:, :], hull(lnvb_d, lnvb_d.full[0].partition_broadcast(128), 0, 768))
    P.dma("sp", bfor.all(), hull(bfor_d, bfor_d.full.partition_broadcast(128), 0, 12))
    for t in range(NT):
        xt = io[:, t % 2, :]
        P.dma("sp", xt, x_d[t * 128:(t + 1) * 128, :])
        for c in range(8):
            P.transpose(bank(c // 4, 128, (c % 4) * 128), io[:, t % 2, c * 128:(c + 1) * 128], ident.all())
        for hb in range(2):
            P.copy("act" if hb == 0 else "dve", hT[:, 4 * hb:4 * hb + 4, t * 128:(t + 1) * 128],
                   hull(psF, psF.full[:, 512 * hb:512 * hb + 512].rearrange("p (c t) -> p c t", t=128), 512 * hb, 512 * hb + 512))

    for g in range(2):
        g0 = g * TG
        wload(WA[:, :, :], w_ina_d, w_ina_d.full[0], 0, 1792)
        wload(WB[:, :, :], w_out_d, w_out_d.full[0], 0, D)
        for s in range(2):
            norm_to_xn(0, g0, s)
        for s in range(2):
            c0 = s * SG
            for m in range(2):
                for k in range(8):
                    P.mm(bank(4 + m), WA[:, k, 1536 + m * 128:1536 + (m + 1) * 128], xnv(k, c0, SG), start=(k == 0), stop=(k == 7))
                P.copy("act", qmT[:, m, c0:c0 + SG], bank(4 + m))
        for tl in range(TG // 128):
            o = TMP0 + (tl % 2) * 10752
            gu = AV(o, [768])
            gv = AV(o + 1536, [768], F32)
            vn = AV(o + 4608, [768], F32)
            vln = AV(o + 7680, [768])
            mainb = AV(o + 9216, [768])
            for n in range(3):
                for k in range(8):
                    P.mm(bank(n), xnv(k, tl * 128, 128), WA[:, k, n * 512:(n + 1) * 512], start=(k == 0), stop=(k == 7))
            P.act(gu[:, :], psF[:, 0:768], AF.Gelu_apprx_tanh)
            P.act(gv[:, :], psF[:, 768:1536], AF.Gelu_apprx_tanh, accum=st[:, 8:9])
            P.act(vn[:, :], gv[:, :], AF.Square, accum=st[:, 9:10])
            P.ts("dve", st[:, 10:11], st[:, 8:9], -1.0 / 768, None, ALU.mult)
            P.ts("dve", st[:, 11:12], st[:, 9:10], 1.0 / 768, None, ALU.mult)
            P.stt("dve", st[:, 12:13], st[:, 10:11], st[:, 10:11], st[:, 11:12], ALU.mult, ALU.subtract)
            P.act(st[:, 13:14], st[:, 12:13], AF.Sqrt, bias=LN_EPS, scale=-1.0)
            P.op("dve", lambda e, o_=st[:, 14:15], i_=st[:, 13:14]: e.reciprocal(o_.ap, i_.ap), reads=[st[:, 13:14]], writes=[st[:, 14:15]])
            P.ts("dve", vn[:, :], gv[:, :], st[:, 10:11], st[:, 14:15], ALU.add, ALU.mult)
            P.tt("pool", gv[:, :], vn[:, :], lnvg[:, :], ALU.mult)
            P.tt("pool", vln[:, :], gv[:, :], lnvb[:, :], ALU.add)
            for gg in range(6):
                P.mm(psF[:, 1536 + gg * 128:1536 + (gg + 1) * 128], WmT[:, gg, :], vln[:, gg * 128:(gg + 1) * 128])
            for gg in range(6):
                P.stt("dve", mainb[:, gg * 128:(gg + 1) * 128], psF[:, 1536 + gg * 128:1536 + (gg + 1) * 128],
                      bsp[:, gg:gg + 1], gu[:, gg * 128:(gg + 1) * 128], ALU.add, ALU.mult)
            for gg in range(6):
                P.transpose(psT[:, gg * 128:(gg + 1) * 128], mainb[:, gg * 128:(gg + 1) * 128], identb.all())
            P.copy("act", mixT[:, 0:6, tl * 128:(tl + 1) * 128],
                   hull(psT, psT.full[:, 0:768].rearrange("p (c t) -> p c t", t=128), 0, 768))
        for s in range(2):
            mem_attn(s)
        out_proj_and_ffn(0, g0)
        wload(WA[:, :, 0:1548], w_skv_d, w_skv_d.full, 0, 1548)
        for s in range(2):
            norm_to_xn(8, g0, s)
        kst = AV(45056, [2, 6, SG])
        vst = AV(45056 + 12288, [2, 768])
        for s in range(2):
            c0 = s * SG
            for m in range(6):
                b = 4 + m % 2
                for k in range(8):
                    P.mm(bank(b), WA[:, k, m * 128:(m + 1) * 128], xnv(k, c0, SG), start=(k == 0), stop=(k == 7))
                P.copy("act", kst[:, s, m, :], bank(b))
            P.dma("sp", hull(kT_loc[g], kT_loc[g].full.rearrange("(m p) t -> p m t", p=128)[:, :, c0:c0 + SG], 0, kT_loc[g].size),
                  kst[:, s, :, :])
        for tl in range(TG // 128):
            tg = g * 8 + tl
            for hh in range(2):
                for k in range(8):
                    P.mm(bank(hh, 384), xnv(k, tl * 128, 128), WA[:, k, 768 + hh * 384:768 + (hh + 1) * 384], start=(k == 0), stop=(k == 7))
                P.copy("act" if hh == 0 else "dve", vst[:, tl % 2, hh * 384:(hh + 1) * 384], bank(hh, 384))
            P.dma("sp", hull(v_loc[g], v_loc[g].full[tl * 128:(tl + 1) * 128, :], 0, v_loc[g].size), vst[:, tl % 2, :])
            for k in range(8):
                P.mm(bank(6, 12), xnv(k, tl * 128, 128), WA[:, k, 1536:1548], start=(k == 0), stop=(k == 7))
            P.tt("dve", st[:, 0:12], bank(6, 12), bfor.all(), ALU.add)
            P.act(tmpn[:, 0, 0:12], st[:, 0:12], AF.Exp, scale=-1.0)
            P.act(tmpn[:, 0, 16:28], tmpn[:, 0, 0:12], AF.Ln, bias=1.0)
            P.ts("dve", lfst[:, tg, :], tmpn[:, 0, 16:28], -1.0, None, ALU.mult)
        gather(kT_loc[g], kT_all[g])
        gather(v_loc[g], v_all[g])
    P.dma("sp", hull(lf_loc, lf_loc.full.rearrange("(i p) h -> p i h", p=128), 0, lf_loc.size), lfst.all(),
          allow_slow_non_contiguous=True)

    gather(lf_loc, lf_all)

    mem_prep(1)
    m32 = AV(47200, [2, 128], F32)
    P.dma("sp", m32[:, 0, :], m0_d.all())
    P.dma("sp", m32[:, 1, :], m1_d.all())
    P.copy("dve", msk.all(), m32[:, :, :])
    LF = AV(40960, [2, NT, 12], F32)
    LFf = AV(40960, [2 * NT * 12], F32)
    CT = AV(42496, [2, NT, 12], F32)
    CTf = AV(42496, [2 * NT * 12], F32)
    TOT = AV(44032, [2, NT, 12], F32)
    TOTf = AV(44032, [2 * NT * 12], F32)
    OFF = AV(45568, [2, NT + 1, 12], F32)
    P.dma("sp", LF[:, :, :, :], hull(lf_all, lf_all.full.rearrange("(r i p) h -> p r i h", p=128, i=NT), 0, lf_all.size),
          allow_slow_non_contiguous=True)
    P.mm(bank(0, 384), tri.all(), LFf[:, :])
    P.mm(bank(1, 384), ones32.all(), LFf[:, :])
    P.copy("dve", CTf[:, :], bank(0, 384))
    P.copy("act", TOTf[:, :], bank(1, 384))
    P.memset("dve", OFF[:, 0, 0, :], 0.0)
    for G in range(2 * NT):
        r, i = G % 2, G // 2
        r2, i2 = (G + 1) % 2, (G + 1) // 2
        P.tt("dve", OFF[:, r2, i2, :], OFF[:, r, i, :], TOT[:, r, i, :], ALU.add)
    for r in range(2):
        P.tt("dve", CT[:, r, :, :], CT[:, r, :, :], OFF[:, r, 0:NT, :], ALU.add)
    for j in range(4):
        for r in range(2):
            ni = 4 * j + 4
            lo_ = OFF.lo + 2 * (4 * j * 12)
            cref = View(arena, OFF.ap[:, 0, 4 * j:4 * j + 1, :].to_broadcast([128, ni, 12]), lo_, lo_ + 24)
            P.tt("dve", BT[:, j, r, 0:ni, :], cref, CT[:, r, 0:ni, :], ALU.subtract)

    for g in range(2):
        if stop == 1 or (stop == 2 and g == 1):
            break
        g0 = g * TG
        P.memset("pool", VP[:, :, :, :, 64:128], 1.0)
        wload(WIN[:, :, :], w_inb_d, w_inb_d.full[0], 0, D)
        wload(WB[:, :, :], w_out_d, w_out_d.full[1], 0, D)
        for s in range(2):
            norm_to_xn(4, g0, s)
        for s in range(2):
            c0 = s * SG
            for m in range(8):
                b = 4 + m % 2
                for k in range(8):
                    P.mm(bank(b), WIN[:, k, m * 128:(m + 1) * 128], xnv(k, c0, SG), start=(k == 0), stop=(k == 7))
                if m < 6:
                    P.copy("act", QT[:, m, c0:c0 + SG], bank(b))
                else:
                    P.copy("act", qmT[:, m - 6, c0:c0 + SG], bank(b))
        nS = 0
        for hp in range(6):
            for g2 in range(2):
                P.dma("sp", KTp[:, :, g2 * TG:(g2 + 1) * TG],
                      hull(kT_all[g2], kT_all[g2].full.rearrange("(r f) t -> f r t", r=2)[hp * 128:(hp + 1) * 128], 0, kT_all[g2].size))
                for r_ in range(2):
                    for e2 in range(2):
                        cc = hp * 128 + e2 * 64
                        P.dma("sp", VP[:, r_, g2 * 8:(g2 + 1) * 8, e2, 0:64],
                              hull(v_all[g2], v_all[g2].full[r_ * TG:(r_ + 1) * TG, cc:cc + 64].rearrange("(i p) d -> p i d", p=128), 0, v_all[g2].size))
            for e_ in range(2):
                h = 2 * hp + e_
                r0 = 64 * e_
                for jj in range(2):
                    j = 2 * g + jj
                    c0 = jj * SG
                    ob = 4 + (2 * e_ + jj) % 2
                    first = True
                    for i in range(4 * j + 4):
                        o_ = max(0, i - 4 * j)
                        ncol = SG - 128 * o_
                        for r in range(2):
                            sb = nS % 4
                            pt = PT[:, nS % 6, 0:ncol]
                            nS += 1
                            P.mm(bank(sb, ncol), KTp[r0:r0 + 64, r, i * 128:(i + 1) * 128],
                                 QT[r0:r0 + 64, hp, c0 + 128 * o_:c0 + SG])
                            P.act(pt, bank(sb, ncol), AF.Exp, bias=BT[:, j, r, i, h:h + 1], scale=0.125)
                            if i >= 4 * j:
                                ptd = View(PT, pt.ap[:, 0:128], pt.lo, pt.lo + 128)
                                P.tt("pool", ptd, ptd, msk[:, r, :], ALU.mult)
                            P.mm(bank(ob, ncol, 128 * o_), VP[:, r, i, e_, :], pt, start=first,
                                 stop=(i == 4 * j + 3 and r == 1))
                            first = False
                    rv = Rb[64:128, jj, :]
                    ov = psF[64:128, 512 * ob:512 * ob + 512]
                    P.op("dve", lambda e, o=rv, i_=ov: e.reciprocal(o.ap, i_.ap), reads=[ov], writes=[rv])
                    P.tt("dve", mixT[r0:r0 + 64, hp, c0:c0 + SG], psF[0:64, 512 * ob:512 * ob + 512], rv, ALU.mult)
        for s in range(2):
            mem_attn(s)
        out_proj_and_ffn(1, g0)
    final = []
    for t in range(NT):
        for c in range(8):
            P.transpose(bank(c // 4, 128, (c % 4) * 128), hT[:, c, t * 128:(t + 1) * 128], ident.all())
        P.copy("act", io[:, t % 2, 0:512], bank(0))
        P.copy("dve", io[:, t % 2, 512:1024], bank(1))
        final.append(P.dma("sp", out_d[t * 128:(t + 1) * 128, :], io[:, t % 2, :]))
    P.finalize(final_waits=final)
    return nc, P


PARAM_NAMES = ["ln_mix_pre", "ln_mix_post", "ln_ffn_pre", "ln_ffn_post", "ln_mem", "ln_shared", "w_mem_kv", "w_out",
               "w_ffn_gate", "w_ffn_up", "w_ffn_down", "w_in_a", "w_spatial", "b_spatial", "ln_v_g", "ln_v_b",
               "w_shared_kv", "b_forget", "w_in_b"]


def kernel(**inp):
    inp = {k: np.asarray(v) for k, v in inp.items()}
    x, mem = inp["x"], inp["mem"]
    B = x.shape[0]
    ident = np.eye(128, dtype=np.float32)
    tri = np.triu(np.ones((128, 128), np.float32))
    ones = np.ones((128, 128), np.float32)
    zeros = np.zeros((128, 128), np.float32)
    cores = [(b, r) for b in range(B) for r in range(2)]
    params = {k: np.ascontiguousarray(inp[k], dtype=np.float32) for k in PARAM_NAMES}
    nc, _ = build(len(cores))
    maps = []
    for (b, r) in cores:
        m = dict(params)
        m.update({"ident": ident, "tri": tri,
                  "m0": tri if r == 0 else ones, "m1": zeros if r == 0 else tri,
                  "x": np.ascontiguousarray(x[b].reshape(NT, 2, 128, D)[:, r].reshape(T, D)),
                  "mem": np.ascontiguousarray(mem[b])})
        maps.append(m)
    res = run_bass_kernel_spmd(nc, maps, core_ids=list(range(len(cores)))).results
    out = np.empty((B, NT, 2, 128, D), np.float32)
    for ci, (b, r) in enumerate(cores):
        out[b, :, r] = np.asarray(res[ci]["out"]).reshape(NT, 128, D)
    return out.reshape(B, NT * 2 * 128, D)
```

```python
import contextlib
import numpy as np
import concourse.bass as bass
import concourse.mybir as mybir
from concourse.bass_utils import run_bass_kernel_spmd

F32 = mybir.dt.float32
BF16 = mybir.dt.bfloat16
ALU = mybir.AluOpType
AF = mybir.ActivationFunctionType
AX = mybir.AxisListType

N_DMA_SEMS = 12


class View:
    __slots__ = ("buf", "ap", "lo", "hi")

    def __init__(self, buf, ap, lo, hi):
        self.buf, self.ap, self.lo, self.hi = buf, ap, lo, hi


class Buf:
    def __init__(self, name, full_ap, shape, kind):
        self.name, self.full, self.shape, self.kind = name, full_ap, list(shape), kind
        first = 1 if kind != "dram" else 0
        self.first = first
        st = [0] * len(shape)
        s = 1
        for i in range(len(shape) - 1, first - 1, -1):
            st[i] = s
            s *= shape[i]
        self.strides = st
        self.size = s
        self.recs = []

    def __getitem__(self, idx):
        if not isinstance(idx, tuple):
            idx = (idx,)
        idx = tuple(idx) + (slice(None),) * (len(self.shape) - len(idx))
        lo, hi = 0, 1
        for d, (i, n) in enumerate(zip(idx, self.shape)):
            if isinstance(i, slice):
                a = 0 if i.start is None else i.start
                b = n if i.stop is None else i.stop
                assert i.step in (None, 1) and 0 <= a < b <= n, (self.name, idx)
            else:
                assert 0 <= i < n, (self.name, idx)
                a, b = i, i + 1
            if d >= self.first:
                lo += a * self.strides[d]
                hi += (b - 1) * self.strides[d]
        return View(self, self.full[idx], lo, hi)

    def all(self):
        return self[tuple(slice(None) for _ in self.shape)]


class Op:
    __slots__ = ("eng", "fn", "deps", "is_dma", "pos", "need_inc", "count", "waits", "sem", "semval", "guard", "idx", "inc")


class Prog:
    def __init__(self, nc):
        self.nc = nc
        self.stack = contextlib.ExitStack()
        self.ops = []
        self.eng_ops = {e: [] for e in ("pe", "act", "dve", "pool", "sp")}
        self.n_dma = {e: 0 for e in self.eng_ops}
        self.nbuf = 0

    def sbuf(self, name, shape, dtype):
        t = self.stack.enter_context(self.nc.sbuf_tensor("sb_" + name, list(shape), dtype))
        return Buf(name, t[:] if not hasattr(t, "ap") else t.ap(), shape, "sbuf")

    def psum(self, name, shape, dtype):
        t = self.stack.enter_context(self.nc.psum_tensor("ps_" + name, list(shape), dtype))
        b = Buf(name, t[:] if not hasattr(t, "ap") else t.ap(), shape, "psum")
        b.bank_elems = 512 if dtype == F32 else 1024
        return b

    def dram(self, name, shape, dtype, kind="Internal"):
        t = self.nc.dram_tensor(name, list(shape), dtype, kind=kind)
        return Buf(name, t.ap(), shape, "dram")

    def op(self, eng, fn, reads=(), writes=(), dma=False, cc=False):
        o = Op()
        o.eng, o.fn, o.is_dma = eng, fn, dma
        o.deps = {}
        o.need_inc = False
        o.idx = len(self.ops)

        def norm(v):
            be = getattr(v.buf, "bank_elems", None)
            if v is None or be is None:
                return v
            return View(v.buf, v.ap, v.lo // be * be, -(-v.hi // be) * be)
        reads = [norm(v) for v in reads if v is not None]
        writes = [norm(v) for v in writes if v is not None]
        for v in reads:
            if v is None:
                continue
            for r in v.buf.recs:
                if r[0] < v.hi and v.lo < r[1]:
                    if r[2] is not None:
                        o.deps.setdefault(r[2], "raw")
                    r[3].append(o)
        for v in writes:
            keep = []
            for r in v.buf.recs:
                if r[0] < v.hi and v.lo < r[1]:
                    if r[2] is not None:
                        if o.deps.get(r[2]) != "raw":
                            o.deps[r[2]] = "waw"
                    for rd in r[3]:
                        if rd is not o:
                            o.deps.setdefault(rd, "war")
                    if v.lo <= r[0] and r[1] <= v.hi:
                        continue
                keep.append(r)
            keep.append([v.lo, v.hi, o, []])
            v.buf.recs = keep
        o.pos = len(self.eng_ops[eng])
        self.eng_ops[eng].append(o)
        self.ops.append(o)
        o.inc = 16
        if cc:
            self.n_cc = getattr(self, "n_cc", 0) + 1
            o.is_dma = True
            o.sem = ("cc", 0)
            o.semval = self.n_cc
            o.guard = 0
            o.inc = 1
        elif dma:
            k = self.n_dma[eng]
            self.n_dma[eng] += 1
            o.sem = (eng, k % N_DMA_SEMS)
            o.semval = 16 * (k // N_DMA_SEMS + 1)
            o.guard = 16 * (k // N_DMA_SEMS) if k >= N_DMA_SEMS else 0
        return o

    def mm(self, out, lhsT, rhs, start=True, stop=True, **kw):
        return self.op("pe", lambda e: e.matmul(out.ap, lhsT.ap, rhs.ap, start=start, stop=stop, **kw),
                       reads=[lhsT, rhs] + ([] if start else []), writes=[out])

    def transpose(self, out, in_, ident):
        return self.op("pe", lambda e: e.transpose(out.ap, in_.ap, ident.ap), reads=[in_, ident], writes=[out])

    def act(self, out, in_, func, bias=None, scale=1.0, accum=None, eng="act"):
        def fn(e):
            kw = {}
            if bias is not None:
                kw["bias"] = bias.ap if isinstance(bias, View) else bias
            if accum is not None:
                kw["accum_out"] = accum.ap
            sc = scale.ap if isinstance(scale, View) else scale
            return e.activation(out.ap, in_.ap, func, scale=sc, **kw)
        rd = [in_] + [x for x in (bias, scale) if isinstance(x, View)]
        wr = [out] + ([accum] if accum is not None else [])
        return self.op(eng, fn, reads=rd, writes=wr)

    def tt(self, eng, out, a, b, op):
        return self.op(eng, lambda e: e.tensor_tensor(out.ap, a.ap, b.ap, op), reads=[a, b], writes=[out])

    def ts(self, eng, out, a, s1, s2, op0, op1=None, accum=None):
        def fn(e):
            x1 = s1.ap if isinstance(s1, View) else s1
            x2 = s2.ap if isinstance(s2, View) else s2
            kw = {}
            if accum is not None:
                kw["accum_out"] = accum.ap
            if op1 is None:
                return e.tensor_scalar(out.ap, a.ap, x1, x2, op0, **kw)
            return e.tensor_scalar(out.ap, a.ap, x1, x2, op0, op1, **kw)
        rd = [a] + [x for x in (s1, s2) if isinstance(x, View)]
        wr = [out] + ([accum] if accum is not None else [])
        return self.op(eng, fn, reads=rd, writes=wr)

    def stt(self, eng, out, a, s, b, op0, op1):
        def fn(e):
            x = s.ap if isinstance(s, View) else s
            return e.scalar_tensor_tensor(out.ap, a.ap, x, b.ap, op0, op1)
        rd = [a, b] + ([s] if isinstance(s, View) else [])
        return self.op(eng, fn, reads=rd, writes=[out])

    def copy(self, eng, out, in_):
        if eng == "act":
            return self.op(eng, lambda e: e.copy(out.ap, in_.ap), reads=[in_], writes=[out])
        return self.op(eng, lambda e: e.tensor_copy(out.ap, in_.ap), reads=[in_], writes=[out])

    def memset(self, eng, out, val):
        return self.op(eng, lambda e: e.memset(out.ap, val), writes=[out])

    def dma(self, eng, out, in_, **kw):
        return self.op(eng, lambda e: e.dma_start(out=out.ap, in_=in_.ap, **kw), reads=[in_], writes=[out], dma=True)

    def finalize(self, final_waits=()):
        nc = self.nc
        seen = {e: {f: -1 for f in self.eng_ops} for e in self.eng_ops}
        seen_dma = {e: {} for e in self.eng_ops}
        for o in self.ops:
            o.waits = []
            E = o.eng
            for y, kind in sorted(o.deps.items(), key=lambda kv: -kv[0].idx):
                if y.is_dma:
                    if seen_dma[E].get(y.sem, 0) >= y.semval:
                        continue
                    seen_dma[E][y.sem] = y.semval
                    o.waits.append(("dma", y.sem, y.semval))
                    continue
                F = y.eng
                if F == E and not o.is_dma:
                    if E == "pe":
                        continue
                if seen[E][F] >= y.pos:
                    continue
                seen[E][F] = y.pos
                y.need_inc = True
                o.waits.append(("eng", y))
            if o.is_dma and o.guard:
                if seen_dma[E].get(o.sem, 0) < o.guard:
                    seen_dma[E][o.sem] = o.guard
                    o.waits.append(("dma", o.sem, o.guard))
        for e, lst in self.eng_ops.items():
            c = 0
            for o in lst:
                if o.need_inc:
                    c += 1
                    o.count = c
        esem = {e: self.stack.enter_context(nc.semaphore("s_" + e)) for e in self.eng_ops}
        dsem = {}
        for e in self.eng_ops:
            if self.n_dma[e]:
                for i in range(min(N_DMA_SEMS, self.n_dma[e])):
                    dsem[(e, i)] = self.stack.enter_context(nc.semaphore("d_%s_%d" % (e, i)))
        if getattr(self, "n_cc", 0):
            dsem[("cc", 0)] = self.stack.enter_context(nc.semaphore("d_cc"))
        self.stats = {e: (len(l), sum(1 for o in l if o.need_inc), sum(len(o.waits) for o in l)) for e, l in self.eng_ops.items()}
        block = self.stack.enter_context(nc.Block())

        def emit(engobj, lst, extra):
            for o in lst:
                for w in o.waits:
                    if w[0] == "dma":
                        engobj.wait_ge(dsem[w[1]], w[2])
                    else:
                        engobj.wait_ge(esem[w[1].eng], w[1].count)
                ins = o.fn(engobj)
                if o.is_dma:
                    ins.then_inc(dsem[o.sem], o.inc)
                elif o.need_inc:
                    ins.then_inc(esem[o.eng], 1)
            for o in extra:
                if o.is_dma:
                    engobj.wait_ge(dsem[o.sem], o.semval)
                else:
                    engobj.wait_ge(esem[o.eng], o.count)

        fw = list(final_waits)
        for o in fw:
            if not o.is_dma and not o.need_inc:
                raise RuntimeError("final wait on op without inc")

        @block.tensor
        def _(e):
            emit(e, self.eng_ops["pe"], [])

        @block.scalar
        def _(e):
            emit(e, self.eng_ops["act"], [])

        @block.vector
        def _(e):
            emit(e, self.eng_ops["dve"], [])

        @block.gpsimd
        def _(e):
            emit(e, self.eng_ops["pool"], [])

        @block.sync
        def _(e):
            emit(e, self.eng_ops["sp"], fw)

        self.stack.close()


D = 1024
T = 2048
TG = 1024
SG = 512
NT = T // 128
DFF = 2816
NJ = DFF // 128
EPS = 1e-6
LN_EPS = 1e-5


def hull(buf, ap, lo, hi):
    return View(buf, ap, lo, hi)


def build(ncores=8, stop=None):
    nc = bass.Bass("TRN2", target_bir_lowering=False)
    P = Prog(nc)

    def din(name, shape, dt=F32):
        return P.dram(name, shape, dt, kind="ExternalInput")

    ident_d = din("ident", [128, 128])
    tri_d = din("tri", [128, 128])
    m0_d = din("m0", [128, 128])
    m1_d = din("m1", [128, 128])
    x_d = din("x", [T, D])
    mem_d = din("mem", [256, D])
    g_d = [din(n, [2, D]) for n in ("ln_mix_pre", "ln_mix_post", "ln_ffn_pre", "ln_ffn_post")]
    g_mem_d = din("ln_mem", [2, D])
    g_sh_d = din("ln_shared", [D])
    w_mkv_d = din("w_mem_kv", [2, D, 512])
    w_out_d = din("w_out", [2, D, D])
    w_g_d = din("w_ffn_gate", [2, D, DFF])
    w_u_d = din("w_ffn_up", [2, D, DFF])
    w_d_d = din("w_ffn_down", [2, DFF, D])
    w_ina_d = din("w_in_a", [1, D, 1792])
    w_sp_d = din("w_spatial", [1, 6, 128, 128])
    b_sp_d = din("b_spatial", [1, 6, 128])
    lnvg_d = din("ln_v_g", [1, 768])
    lnvb_d = din("ln_v_b", [1, 768])
    w_skv_d = din("w_shared_kv", [D, 1548])
    bfor_d = din("b_forget", [12])
    w_inb_d = din("w_in_b", [1, D, D])
    out_d = P.dram("out", [T, D], F32, kind="ExternalOutput")
    kT_loc = [P.dram("kT_loc%d" % g, [768, TG], BF16) for g in range(2)]
    v_loc = [P.dram("v_loc%d" % g, [TG, 768], BF16) for g in range(2)]
    lf_loc = P.dram("lf_loc", [T, 12], F32)
    kT_all = [P.dram("kT_all%d" % g, [2 * 768, TG], BF16) for g in range(2)]
    v_all = [P.dram("v_all%d" % g, [2 * TG, 768], BF16) for g in range(2)]
    lf_all = P.dram("lf_all", [2 * T, 12], F32)
    rg = [[2 * i, 2 * i + 1] for i in range(ncores // 2)]

    def gather(src, dst):
        P.op("pool", lambda e, s_=src, d_=dst: e.collective_compute("AllGather", ALU.bypass, replica_groups=rg,
                                                                     ins=[s_.full.opt()], outs=[d_.full.opt()]),
             reads=[src.all()], writes=[dst.all()], cc=True)

    hT = P.sbuf("hTs", [128, 8, T], F32)
    xn = P.sbuf("xn", [128, 8 * TG], BF16)
    ARENA_B = 88 * 1024
    arena = P.sbuf("arena", [128, ARENA_B // 2], BF16)
    arena32 = arena.full.bitcast(F32)
    xn32 = xn.full.bitcast(F32)
    psF = P.psum("psF", [128, 7 * 512], F32)
    psT = P.psum("psT", [128, 1024], BF16)

    def bank(b, n=512, o=0):
        return psF[:, 512 * b + o:512 * b + o + n]

    class AV:
        def __init__(self, off_b, shape, dt=BF16, base=None, base32=None):
            base = arena if base is None else base
            base32 = arena32 if base32 is None else base32
            self.base = base
            n = 1
            for s_ in shape:
                n *= s_
            if dt == BF16:
                e0 = off_b // 2
                ap = base.full[:, e0:e0 + n]
                self.lo, self.hi, self.esz = e0, e0 + n, 1
            else:
                e0 = off_b // 4
                ap = base32[:, e0:e0 + n]
                self.lo, self.hi, self.esz = off_b // 2, off_b // 2 + 2 * n, 2
            self.shape = list(shape)
            if len(shape) > 1:
                names = " ".join("a%d" % i for i in range(len(shape)))
                kw = {"a%d" % i: s_ for i, s_ in enumerate(shape)}
                ap = ap.rearrange("p (%s) -> p %s" % (names, names), **kw)
            self.ap = ap
            st = [0] * len(shape)
            s_ = 1
            for i in range(len(shape) - 1, -1, -1):
                st[i] = s_
                s_ *= shape[i]
            self.st = st

        def __getitem__(self, idx):
            if not isinstance(idx, tuple):
                idx = (idx,)
            idx = tuple(idx) + (slice(None),) * (1 + len(self.shape) - len(idx))
            lo, hi = 0, 1
            for d, (i, n) in enumerate(zip(idx[1:], self.shape)):
                if isinstance(i, slice):
                    a = 0 if i.start is None else i.start
                    b = n if i.stop is None else i.stop
                else:
                    a, b = i, i + 1
                assert 0 <= a < b <= n
                lo += a * self.st[d]
                hi += (b - 1) * self.st[d]
            return View(self.base, self.ap[idx], self.lo + lo * self.esz, self.lo + hi * self.esz)

    io = AV(32768, [2, D], F32)
    gmem = AV(24576, [D], F32)

    class YV:
        def __getitem__(self, idx):
            p, c, cols = idx
            a = 0 if cols.start is None else cols.start
            b = 512 if cols.stop is None else cols.stop
            e0 = c * 512 + a
            return View(xn, xn32[p, e0:e0 + (b - a)], 2 * e0, 2 * (e0 + b - a))
    y = YV()

    def xnv(c, t0, n, p=slice(None)):
        return xn[p, c * TG + t0:c * TG + t0 + n]

    ident = P.sbuf("ident", [128, 128], F32)
    identb = P.sbuf("identb", [128, 128], BF16)
    tri = P.sbuf("tri", [128, 128], F32)
    onesS = P.sbuf("onesS", [128, 128], BF16)
    ones32 = P.sbuf("ones32", [128, 128], F32)
    gains = P.sbuf("gains", [128, 9, 8], F32)
    KmT = P.sbuf("KmT", [128, 2, 256], BF16)
    Vm = P.sbuf("Vm", [128, 2, 4, 128], BF16)
    sq = P.sbuf("sq", [128, 2, 512], BF16)
    rstd = P.sbuf("rstd", [128, 512], F32)
    tmpn = P.sbuf("tmpn", [128, 2, 512], F32)
    st = P.sbuf("st", [128, 16], F32)
    PT = P.sbuf("PT", [128, 6, 512], BF16)
    Rb = P.sbuf("Rb", [128, 2, 512], F32)
    WmT = P.sbuf("WmT", [128, 6, 128], BF16)
    bsp = P.sbuf("bsp", [128, 6], F32)
    bfor = P.sbuf("bfor", [128, 12], F32)
    lfst = P.sbuf("lfst", [128, NT, 12], F32)
    msk = P.sbuf("msk", [128, 2, 128], BF16)
    lshare = P.sbuf("lshare", [128, 3072], BF16)
    lshare32 = lshare.full.bitcast(F32)
    lnvg = AV(0, [768], F32, lshare, lshare32)
    lnvb = AV(3072, [768], F32, lshare, lshare32)
    BT = AV(0, [4, 2, NT, 12], F32, lshare, lshare32)

    P.dma("sp", ident.all(), ident_d.all())
    P.dma("sp", tri.all(), tri_d.all())
    P.copy("dve", identb.all(), ident.all())
    P.memset("dve", onesS.all(), 1.0 / 1024.0)
    P.memset("dve", ones32.all(), 1.0)
    P.memset("dve", Vm.all(), 1.0)

    def gcol(dst_i, src, ap):
        P.dma("sp", gains[:, dst_i, :], hull(src, ap.rearrange("(c p) -> p c", p=128), 0, src.size),
              allow_slow_non_contiguous=True)
    for L_ in range(2):
        for q in range(4):
            gcol(L_ * 4 + q, g_d[q], g_d[q].full[L_])
    gcol(8, g_sh_d, g_sh_d.full)

    def wload(dst, wd, wap, c0, c1, eng="pool"):
        src = wap[:, c0:c1].rearrange("(k p) c -> p k c", p=128)
        return P.dma(eng, dst, hull(wd, src, 0, wd.size))

    WA = AV(0, [8, 1792])
    WB = AV(28672, [8, 1024])
    mixT = AV(45056, [8, TG])
    qmT = AV(61440, [2, TG])
    TMP0 = 65536
    actb = AV(0, [NJ, TG])
    GU = [AV(45056 + i * 4096, [2, 8, 128]) for i in range(4)]
    DN = [AV(61440 + i * 5632, [NJ, 128]) for i in range(3)]
    QT = AV(0, [6, TG])
    WIN = AV(12288, [8, 1024])
    KTp = AV(65536, [2, T])
    VP = AV(73728, [2, NT, 2, 128])

    def mem_prep(L):
        wmk = AV(0, [8, 512])
        memT = AV(8192, [8, 256])
        memn = AV(8192 + 4096, [2, D])
        P.dma("sp", gmem[:, :], hull(g_mem_d, g_mem_d.full[L].partition_broadcast(128), 0, g_mem_d.size))
        wload(wmk[:, :, :], w_mkv_d, w_mkv_d.full[L], 0, 512)
        for mt in range(2):
            mtile = io[:, mt, :]
            P.dma("sp", mtile, mem_d[mt * 128:(mt + 1) * 128, :])
            P.act(tmpn[:, 0, :], io[:, mt, 0:512], AF.Square, accum=st[:, 0:1])
            P.act(tmpn[:, 1, :], io[:, mt, 512:1024], AF.Square, accum=st[:, 1:2])
            P.tt("dve", st[:, 2:3], st[:, 0:1], st[:, 1:2], ALU.add)
            P.act(st[:, 3:4], st[:, 2:3], AF.Sqrt, bias=EPS, scale=1.0 / D)
            P.op("dve", lambda e, o=st[:, 4:5], i=st[:, 3:4]: e.reciprocal(o.ap, i.ap), reads=[st[:, 3:4]], writes=[st[:, 4:5]])
            P.stt("dve", memn[:, mt, :], mtile, st[:, 4:5], gmem[:, :], ALU.mult, ALU.mult)
            for c in range(8):
                P.transpose(psT[:, c * 128:(c + 1) * 128], memn[:, mt, c * 128:(c + 1) * 128], identb.all())
            P.copy("dve", memT[:, :, mt * 128:(mt + 1) * 128],
                   hull(psT, psT.full.rearrange("p (c t) -> p c t", t=128), 0, 1024))
        for m in range(2):
            for k in range(8):
                P.mm(bank(4 + m, 256), wmk[:, k, m * 128:(m + 1) * 128], memT[:, k, :], start=(k == 0), stop=(k == 7))
            P.copy("act", KmT[:, m, :], bank(4 + m, 256))
        for mt in range(2):
            for k in range(8):
                P.mm(bank(mt, 256), memT[:, k, mt * 128:(mt + 1) * 128], wmk[:, k, 256:512], start=(k == 0), stop=(k == 7))
            P.copy("dve", Vm[:, mt, :, 0:64],
                   hull(psF, psF.full[:, 512 * mt:512 * mt + 256].rearrange("p (h d) -> p h d", d=64), 512 * mt, 512 * mt + 256))

    def rms_stats(src_fn):
        for c in range(8):
            P.act(sq[:, c % 2, :], src_fn(c), AF.Square)
            P.mm(bank(6), onesS.all(), sq[:, c % 2, :], start=(c == 0), stop=(c == 7))
        P.act(tmpn[:, 0, :], bank(6), AF.Sqrt, bias=EPS)
        P.op("dve", lambda e: e.reciprocal(rstd.all().ap, tmpn[:, 0, :].ap), reads=[tmpn[:, 0, :]], writes=[rstd.all()])

    def norm_to_xn(gi, g0, s):
        t0 = g0 + s * SG
        rms_stats(lambda c: hT[:, c, t0:t0 + SG])
        for c in range(8):
            P.stt("dve", xnv(c, s * SG, SG), hT[:, c, t0:t0 + SG], gains[:, gi, c:c + 1], rstd.all(), ALU.mult, ALU.mult)

    def resid_update(gi, g0, s):
        t0 = g0 + s * SG
        rms_stats(lambda c: y[:, c, :])
        for c in range(8):
            P.tt("pool", tmpn[:, c % 2, :], y[:, c, :], rstd.all(), ALU.mult)
            P.stt("dve", hT[:, c, t0:t0 + SG], tmpn[:, c % 2, :], gains[:, gi, c:c + 1], hT[:, c, t0:t0 + SG], ALU.mult, ALU.add)

    def mem_attn(s):
        c0 = s * SG
        for hm in range(4):
            m, r0 = hm // 2, (hm % 2) * 64
            for mt in range(2):
                P.mm(bank(mt), KmT[r0:r0 + 64, m, mt * 128:(mt + 1) * 128], qmT[r0:r0 + 64, m, c0:c0 + SG])
                P.act(PT[:, mt, :], bank(mt), AF.Exp, scale=0.125)
            ob = 2 + hm % 2
            for mt in range(2):
                P.mm(bank(ob), Vm[:, mt, hm, :], PT[:, mt, :], start=(mt == 0), stop=(mt == 1))
            rv = Rb[64:128, hm % 2, :]
            ov = psF[64:128, 512 * ob:512 * ob + 512]
            P.op("dve", lambda e, o=rv, i=ov: e.reciprocal(o.ap, i.ap), reads=[ov], writes=[rv])
            P.tt("dve", mixT[r0:r0 + 64, 6 + m, c0:c0 + SG], psF[0:64, 512 * ob:512 * ob + 512], rv, ALU.mult)

    def out_proj_and_ffn(L, g0):
        for s in range(2):
            c0 = s * SG
            for m in range(8):
                b = 4 + m % 2
                for k in range(8):
                    P.mm(bank(b), WB[:, k, m * 128:(m + 1) * 128], mixT[:, k, c0:c0 + SG], start=(k == 0), stop=(k == 7))
                P.copy("act", y[:, m, :], bank(b))
            resid_update(L * 4 + 1, g0, s)
        for s in range(2):
            norm_to_xn(L * 4 + 2, g0, s)

        def load_gu(j):
            slot = GU[j % 4]
            wload(slot[:, 0, :, :], w_g_d, w_g_d.full[L], j * 128, (j + 1) * 128)
            wload(slot[:, 1, :, :], w_u_d, w_u_d.full[L], j * 128, (j + 1) * 128)
        for j in range(3):
            load_gu(j)
        for j in range(NJ):
            if j + 3 < NJ:
                load_gu(j + 3)
            slot = GU[j % 4]
            for s in range(2):
                c0 = s * SG
                gb, ub = s, 2 + s
                for k in range(8):
                    P.mm(bank(gb), slot[:, 0, k, :], xnv(k, c0, SG), start=(k == 0), stop=(k == 7))
                for k in range(8):
                    P.mm(bank(ub), slot[:, 1, k, :], xnv(k, c0, SG), start=(k == 0), stop=(k == 7))
                P.act(tmpn[:, s, :], bank(gb), AF.Silu)
                P.tt("dve", actb[:, j, c0:c0 + SG], bank(ub), tmpn[:, s, :], ALU.mult)

        def load_dn(q):
            m = q % 8
            wload(DN[q % 3][:, :, :], w_d_d, w_d_d.full[L], m * 128, (m + 1) * 128)
        load_dn(0)
        load_dn(1)
        for s in range(2):
            c0 = s * SG
            for m in range(8):
                q = s * 8 + m
                if q + 2 < 16:
                    load_dn(q + 2)
                b = 4 + m % 2
                for j in range(NJ):
                    P.mm(bank(b), DN[q % 3][:, j, :], actb[:, j, c0:c0 + SG], start=(j == 0), stop=(j == NJ - 1))
                P.copy("act", y[:, m, :], bank(b))
            resid_update(L * 4 + 3, g0, s)

    mem_prep(0)
    wsp32 = AV(16384, [6, 128], F32)
    P.dma("sp", wsp32[:, :, :], hull(w_sp_d, w_sp_d.full[0].rearrange("g t s -> t g s"), 0, w_sp_d.size))
    for g in range(6):
        P.transpose(bank(g % 4, 128), wsp32[:, g, :], ident.all())
        P.tt("dve", WmT[:, g, :], bank(g % 4, 128), tri.all(), ALU.mult)
    P.dma("sp", bsp.all(), hull(b_sp_d, b_sp_d.full[0].rearrange("g t -> t g"), 0, b_sp_d.size),
          allow_slow_non_contiguous=True)
    P.dma("sp", lnvg[:, :], hull(lnvg_d, lnvg_d.full[0].partition_broadcast(128), 0, 768))
    P.dma("sp", lnvb[:, :], hull(lnvb_d, lnvb_d.full[0].partition_broadcast(128), 0, 768))
    P.dma("sp", bfor.all(), hull(bfor_d, bfor_d.full.partition_broadcast(128), 0, 12))
    for t in range(NT):
        xt = io[:, t % 2, :]
        P.dma("sp", xt, x_d[t * 128:(t + 1) * 128, :])
        for c in range(8):
            P.transpose(bank(c // 4, 128, (c % 4) * 128), io[:, t % 2, c * 128:(c + 1) * 128], ident.all())
        for hb in range(2):
            P.copy("act" if hb == 0 else "dve", hT[:, 4 * hb:4 * hb + 4, t * 128:(t + 1) * 128],
                   hull(psF, psF.full[:, 512 * hb:512 * hb + 512].rearrange("p (c t) -> p c t", t=128), 512 * hb, 512 * hb + 512))

    for g in range(2):
        g0 = g * TG
        wload(WA[:, :, :], w_ina_d, w_ina_d.full[0], 0, 1792)
        wload(WB[:, :, :], w_out_d, w_out_d.full[0], 0, D)
        for s in range(2):
            norm_to_xn(0, g0, s)
        for s in range(2):
            c0 = s * SG
            for m in range(2):
                for k in range(8):
                    P.mm(bank(4 + m), WA[:, k, 1536 + m * 128:1536 + (m + 1) * 128], xnv(k, c0, SG), start=(k == 0), stop=(k == 7))
                P.copy("act", qmT[:, m, c0:c0 + SG], bank(4 + m))
        for tl in range(TG // 128):
            o = TMP0 + (tl % 2) * 10752
            gu = AV(o, [768])
            gv = AV(o + 1536, [768], F32)
            vn = AV(o + 4608, [768], F32)
            vln = AV(o + 7680, [768])
            mainb = AV(o + 9216, [768])
            for n in range(3):
                for k in range(8):
                    P.mm(bank(n), xnv(k, tl * 128, 128), WA[:, k, n * 512:(n + 1) * 512], start=(k == 0), stop=(k == 7))
            P.act(gu[:, :], psF[:, 0:768], AF.Gelu_apprx_tanh)
            P.act(gv[:, :], psF[:, 768:1536], AF.Gelu_apprx_tanh, accum=st[:, 8:9])
            P.act(vn[:, :], gv[:, :], AF.Square, accum=st[:, 9:10])
            P.ts("dve", st[:, 10:11], st[:, 8:9], -1.0 / 768, None, ALU.mult)
            P.ts("dve", st[:, 11:12], st[:, 9:10], 1.0 / 768, None, ALU.mult)
            P.stt("dve", st[:, 12:13], st[:, 10:11], st[:, 10:11], st[:, 11:12], ALU.mult, ALU.subtract)
            P.act(st[:, 13:14], st[:, 12:13], AF.Sqrt, bias=LN_EPS, scale=-1.0)
            P.op("dve", lambda e, o_=st[:, 14:15], i_=st[:, 13:14]: e.reciprocal(o_.ap, i_.ap), reads=[st[:, 13:14]], writes=[st[:, 14:15]])
            P.ts("dve", vn[:, :], gv[:, :], st[:, 10:11], st[:, 14:15], ALU.add, ALU.mult)
            P.tt("pool", gv[:, :], vn[:, :], lnvg[:, :], ALU.mult)
            P.tt("pool", vln[:, :], gv[:, :], lnvb[:, :], ALU.add)
            for gg in range(6):
                P.mm(psF[:, 1536 + gg * 128:1536 + (gg + 1) * 128], WmT[:, gg, :], vln[:, gg * 128:(gg + 1) * 128])
            for gg in range(6):
                P.stt("dve", mainb[:, gg * 128:(gg + 1) * 128], psF[:, 1536 + gg * 128:1536 + (gg + 1) * 128],
                      bsp[:, gg:gg + 1], gu[:, gg * 128:(gg + 1) * 128], ALU.add, ALU.mult)
            for gg in range(6):
                P.transpose(psT[:, gg * 128:(gg + 1) * 128], mainb[:, gg * 128:(gg + 1) * 128], identb.all())
            P.copy("act", mixT[:, 0:6, tl * 128:(tl + 1) * 128],
                   hull(psT, psT.full[:, 0:768].rearrange("p (c t) -> p c t", t=128), 0, 768))
        for s in range(2):
            mem_attn(s)
        out_proj_and_ffn(0, g0)
        wload(WA[:, :, 0:1548], w_skv_d, w_skv_d.full, 0, 1548)
        for s in range(2):
            norm_to_xn(8, g0, s)
        kst = AV(45056, [2, 6, SG])
        vst = AV(45056 + 12288, [2, 768])
        for s in range(2):
            c0 = s * SG
            for m in range(6):
                b = 4 + m % 2
                for k in range(8):
                    P.mm(bank(b), WA[:, k, m * 128:(m + 1) * 128], xnv(k, c0, SG), start=(k == 0), stop=(k == 7))
                P.copy("act", kst[:, s, m, :], bank(b))
            P.dma("sp", hull(kT_loc[g], kT_loc[g].full.rearrange("(m p) t -> p m t", p=128)[:, :, c0:c0 + SG], 0, kT_loc[g].size),
                  kst[:, s, :, :])
        for tl in range(TG // 128):
            tg = g * 8 + tl
            for hh in range(2):
                for k in range(8):
                    P.mm(bank(hh, 384), xnv(k, tl * 128, 128), WA[:, k, 768 + hh * 384:768 + (hh + 1) * 384], start=(k == 0), stop=(k == 7))
                P.copy("act" if hh == 0 else "dve", vst[:, tl % 2, hh * 384:(hh + 1) * 384], bank(hh, 384))
            P.dma("sp", hull(v_loc[g], v_loc[g].full[tl * 128:(tl + 1) * 128, :], 0, v_loc[g].size), vst[:, tl % 2, :])
            for k in range(8):
                P.mm(bank(6, 12), xnv(k, tl * 128, 128), WA[:, k, 1536:1548], start=(k == 0), stop=(k == 7))
            P.tt("dve", st[:, 0:12], bank(6, 12), bfor.all(), ALU.add)
            P.act(tmpn[:, 0, 0:12], st[:, 0:12], AF.Exp, scale=-1.0)
            P.act(tmpn[:, 0, 16:28], tmpn[:, 0, 0:12], AF.Ln, bias=1.0)
            P.ts("dve", lfst[:, tg, :], tmpn[:, 0, 16:28], -1.0, None, ALU.mult)
        gather(kT_loc[g], kT_all[g])
        gather(v_loc[g], v_all[g])
    P.dma("sp", hull(lf_loc, lf_loc.full.rearrange("(i p) h -> p i h", p=128), 0, lf_loc.size), lfst.all(),
          allow_slow_non_contiguous=True)

    gather(lf_loc, lf_all)

    mem_prep(1)
    m32 = AV(47200, [2, 128], F32)
    P.dma("sp", m32[:, 0, :], m0_d.all())
    P.dma("sp", m32[:, 1, :], m1_d.all())
    P.copy("dve", msk.all(), m32[:, :, :])
    LF = AV(40960, [2, NT, 12], F32)
    LFf = AV(40960, [2 * NT * 12], F32)
    CT = AV(42496, [2, NT, 12], F32)
    CTf = AV(42496, [2 * NT * 12], F32)
    TOT = AV(44032, [2, NT, 12], F32)
    TOTf = AV(44032, [2 * NT * 12], F32)
    OFF = AV(45568, [2, NT + 1, 12], F32)
    P.dma("sp", LF[:, :, :, :], hull(lf_all, lf_all.full.rearrange("(r i p) h -> p r i h", p=128, i=NT), 0, lf_all.size),
          allow_slow_non_contiguous=True)
    P.mm(bank(0, 384), tri.all(), LFf[:, :])
    P.mm(bank(1, 384), ones32.all(), LFf[:, :])
    P.copy("dve", CTf[:, :], bank(0, 384))
    P.copy("act", TOTf[:, :], bank(1, 384))
    P.memset("dve", OFF[:, 0, 0, :], 0.0)
    for G in range(2 * NT):
        r, i = G % 2, G // 2
        r2, i2 = (G + 1) % 2, (G + 1) // 2
        P.tt("dve", OFF[:, r2, i2, :], OFF[:, r, i, :], TOT[:, r, i, :], ALU.add)
    for r in range(2):
        P.tt("dve", CT[:, r, :, :], CT[:, r, :, :], OFF[:, r, 0:NT, :], ALU.add)
    for j in range(4):
        for r in range(2):
            ni = 4 * j + 4
            lo_ = OFF.lo + 2 * (4 * j * 12)
            cref = View(arena, OFF.ap[:, 0, 4 * j:4 * j + 1, :].to_broadcast([128, ni, 12]), lo_, lo_ + 24)
            P.tt("dve", BT[:, j, r, 0:ni, :], cref, CT[:, r, 0:ni, :], ALU.subtract)

    for g in range(2):
        if stop == 1 or (stop == 2 and g == 1):
            break
        g0 = g * TG
        P.memset("pool", VP[:, :, :, :, :], 1.0)
        wload(WIN[:, :, :], w_inb_d, w_inb_d.full[0], 0, D)
        wload(WB[:, :, :], w_out_d, w_out_d.full[1], 0, D)
        for s in range(2):
            norm_to_xn(4, g0, s)
        for s in range(2):
            c0 = s * SG
            for m in range(8):
                b = 4 + m % 2
                for k in range(8):
                    P.mm(bank(b), WIN[:, k, m * 128:(m + 1) * 128], xnv(k, c0, SG), start=(k == 0), stop=(k == 7))
                if m < 6:
                    P.copy("act", QT[:, m, c0:c0 + SG], bank(b))
                else:
                    P.copy("act", qmT[:, m - 6, c0:c0 + SG], bank(b))
        KTb = [KTp, AV(12288, [2, T])]
        VPb = [AV(73728 + 8192 * i_, [2, NT, 128]) for i_ in range(2)]

        def load_k(hp):
            for g2 in range(2):
                P.dma("sp", KTb[hp % 2][:, :, g2 * TG:(g2 + 1) * TG],
                      hull(kT_all[g2], kT_all[g2].full.rearrange("(r f) t -> f r t", r=2)[hp * 128:(hp + 1) * 128], 0, kT_all[g2].size))

        def load_v(h):
            cc = h * 64
            for g2 in range(2):
                for r_ in range(2):
                    P.dma("sp", VPb[h % 2][:, r_, g2 * 8:(g2 + 1) * 8, 0:64],
                          hull(v_all[g2], v_all[g2].full[r_ * TG:(r_ + 1) * TG, cc:cc + 64].rearrange("(i p) d -> p i d", p=128), 0, v_all[g2].size))

        tiles = []
        for h in range(12):
            for jj in range(2):
                j = 2 * g + jj
                for i in range(4 * j + 4):
                    for r in range(2):
                        tiles.append((h, jj, j, i, r, i == 0 and r == 0, i == 4 * j + 3 and r == 1))
        LA = 3
        load_k(0)
        load_v(0)
        load_v(1)
        for n in range(len(tiles) + LA):
            if n < len(tiles):
                h, jj, j, i, r, first, last = tiles[n]
                hp, e_ = h // 2, h % 2
                r0 = 64 * e_
                if first and jj == 0 and e_ == 0 and hp + 1 < 6:
                    load_k(hp + 1)
                c0 = jj * SG
                o_ = max(0, i - 4 * j)
                ncol = SG - 128 * o_
                pt = PT[:, n % 6, 0:ncol]
                P.mm(bank(n % 4, ncol), KTb[hp % 2][r0:r0 + 64, r, i * 128:(i + 1) * 128],
                     QT[r0:r0 + 64, hp, c0 + 128 * o_:c0 + SG])
                P.act(pt, bank(n % 4, ncol), AF.Exp, bias=BT[:, j, r, i, h:h + 1], scale=0.125)
                if i >= 4 * j:
                    ptd = View(PT, pt.ap[:, 0:128], pt.lo, pt.lo + 128)
                    P.tt("pool", ptd, ptd, msk[:, r, :], ALU.mult)
            if n >= LA:
                m_ = n - LA
                h, jj, j, i, r, first, last = tiles[m_]
                hp, e_ = h // 2, h % 2
                r0 = 64 * e_
                c0 = jj * SG
                o_ = max(0, i - 4 * j)
                ncol = SG - 128 * o_
                ob = 4 + (2 * h + jj) % 2
                pt = PT[:, m_ % 6, 0:ncol]
                P.mm(bank(ob, ncol, 128 * o_), VPb[h % 2][:, r, i, :], pt, start=first, stop=last)
                if last:
                    rv = Rb[64:128, jj, :]
                    ov = psF[64:128, 512 * ob:512 * ob + 512]
                    P.op("dve", lambda e, o=rv, i_=ov: e.reciprocal(o.ap, i_.ap), reads=[ov], writes=[rv])
                    P.tt("dve", mixT[r0:r0 + 64, hp, c0:c0 + SG], psF[0:64, 512 * ob:512 * ob + 512], rv, ALU.mult)
                    if jj == 1 and h + 2 < 12:
                        load_v(h + 2)
        for s in range(2):
            mem_attn(s)
        out_proj_and_ffn(1, g0)
    final = []
    for t in range(NT):
        for c in range(8):
            P.transpose(bank(c // 4, 128, (c % 4) * 128), hT[:, c, t * 128:(t + 1) * 128], ident.all())
        P.copy("act", io[:, t % 2, 0:512], bank(0))
        P.copy("dve", io[:, t % 2, 512:1024], bank(1))
        final.append(P.dma("sp", out_d[t * 128:(t + 1) * 128, :], io[:, t % 2, :]))
    P.finalize(final_waits=final)
    return nc, P


PARAM_NAMES = ["ln_mix_pre", "ln_mix_post", "ln_ffn_pre", "ln_ffn_post", "ln_mem", "ln_shared", "w_mem_kv", "w_out",
               "w_ffn_gate", "w_ffn_up", "w_ffn_down", "w_in_a", "w_spatial", "b_spatial", "ln_v_g", "ln_v_b",
               "w_shared_kv", "b_forget", "w_in_b"]


def kernel(**inp):
    inp = {k: np.asarray(v) for k, v in inp.items()}
    x, mem = inp["x"], inp["mem"]
    B = x.shape[0]
    ident = np.eye(128, dtype=np.float32)
    tri = np.triu(np.ones((128, 128), np.float32))
    ones = np.ones((128, 128), np.float32)
    zeros = np.zeros((128, 128), np.float32)
    cores = [(b, r) for b in range(B) for r in range(2)]
    params = {k: np.ascontiguousarray(inp[k], dtype=np.float32) for k in PARAM_NAMES}
    nc, _ = build(len(cores))
    maps = []
    for (b, r) in cores:
        m = dict(params)
        m.update({"ident": ident, "tri": tri,
                  "m0": tri if r == 0 else ones, "m1": zeros if r == 0 else tri,
                  "x": np.ascontiguousarray(x[b].reshape(NT, 2, 128, D)[:, r].reshape(T, D)),
                  "mem": np.ascontiguousarray(mem[b])})
        maps.append(m)
    res = run_bass_kernel_spmd(nc, maps, core_ids=list(range(len(cores)))).results
    out = np.empty((B, NT, 2, 128, D), np.float32)
    for ci, (b, r) in enumerate(cores):
        out[b, :, r] = np.asarray(res[ci]["out"]).reshape(NT, 128, D)
    return out.reshape(B, NT * 2 * 128, D)
```

```python
import contextlib
import numpy as np
import concourse.bass as bass
import concourse.mybir as mybir
from concourse.bass_utils import run_bass_kernel_spmd

F32 = mybir.dt.float32
BF16 = mybir.dt.bfloat16
ALU = mybir.AluOpType
AF = mybir.ActivationFunctionType
AX = mybir.AxisListType

N_DMA_SEMS = 12


class View:
    __slots__ = ("buf", "ap", "lo", "hi")

    def __init__(self, buf, ap, lo, hi):
        self.buf, self.ap, self.lo, self.hi = buf, ap, lo, hi


class Buf:
    def __init__(self, name, full_ap, shape, kind):
        self.name, self.full, self.shape, self.kind = name, full_ap, list(shape), kind
        first = 1 if kind != "dram" else 0
        self.first = first
        st = [0] * len(shape)
        s = 1
        for i in range(len(shape) - 1, first - 1, -1):
            st[i] = s
            s *= shape[i]
        self.strides = st
        self.size = s
        self.recs = []

    def __getitem__(self, idx):
        if not isinstance(idx, tuple):
            idx = (idx,)
        idx = tuple(idx) + (slice(None),) * (len(self.shape) - len(idx))
        lo, hi = 0, 1
        for d, (i, n) in enumerate(zip(idx, self.shape)):
            if isinstance(i, slice):
                a = 0 if i.start is None else i.start
                b = n if i.stop is None else i.stop
                assert i.step in (None, 1) and 0 <= a < b <= n, (self.name, idx)
            else:
                assert 0 <= i < n, (self.name, idx)
                a, b = i, i + 1
            if d >= self.first:
                lo += a * self.strides[d]
                hi += (b - 1) * self.strides[d]
        return View(self, self.full[idx], lo, hi)

    def all(self):
        return self[tuple(slice(None) for _ in self.shape)]


class Op:
    __slots__ = ("eng", "fn", "deps", "is_dma", "pos", "need_inc", "count", "waits", "sem", "semval", "guard", "idx", "inc")


class Prog:
    def __init__(self, nc):
        self.nc = nc
        self.stack = contextlib.ExitStack()
        self.ops = []
        self.eng_ops = {e: [] for e in ("pe", "act", "dve", "pool", "sp")}
        self.n_dma = {e: 0 for e in self.eng_ops}
        self.nbuf = 0

    def sbuf(self, name, shape, dtype):
        t = self.stack.enter_context(self.nc.sbuf_tensor("sb_" + name, list(shape), dtype))
        return Buf(name, t[:] if not hasattr(t, "ap") else t.ap(), shape, "sbuf")

    def psum(self, name, shape, dtype):
        t = self.stack.enter_context(self.nc.psum_tensor("ps_" + name, list(shape), dtype))
        b = Buf(name, t[:] if not hasattr(t, "ap") else t.ap(), shape, "psum")
        b.bank_elems = 512 if dtype == F32 else 1024
        return b

    def dram(self, name, shape, dtype, kind="Internal"):
        t = self.nc.dram_tensor(name, list(shape), dtype, kind=kind)
        return Buf(name, t.ap(), shape, "dram")

    def op(self, eng, fn, reads=(), writes=(), dma=False, cc=False):
        o = Op()
        o.eng, o.fn, o.is_dma = eng, fn, dma
        o.deps = {}
        o.need_inc = False
        o.idx = len(self.ops)

        def norm(v):
            be = getattr(v.buf, "bank_elems", None)
            if v is None or be is None:
                return v
            return View(v.buf, v.ap, v.lo // be * be, -(-v.hi // be) * be)
        reads = [norm(v) for v in reads if v is not None]
        writes = [norm(v) for v in writes if v is not None]
        for v in reads:
            if v is None:
                continue
            for r in v.buf.recs:
                if r[0] < v.hi and v.lo < r[1]:
                    if r[2] is not None:
                        o.deps.setdefault(r[2], "raw")
                    r[3].append(o)
        for v in writes:
            keep = []
            for r in v.buf.recs:
                if r[0] < v.hi and v.lo < r[1]:
                    if r[2] is not None:
                        if o.deps.get(r[2]) != "raw":
                            o.deps[r[2]] = "waw"
                    for rd in r[3]:
                        if rd is not o:
                            o.deps.setdefault(rd, "war")
                    if v.lo <= r[0] and r[1] <= v.hi:
                        continue
                keep.append(r)
            keep.append([v.lo, v.hi, o, []])
            v.buf.recs = keep
        o.pos = len(self.eng_ops[eng])
        self.eng_ops[eng].append(o)
        self.ops.append(o)
        o.inc = 16
        if cc:
            self.n_cc = getattr(self, "n_cc", 0) + 1
            o.is_dma = True
            o.sem = ("cc", 0)
            o.semval = self.n_cc
            o.guard = 0
            o.inc = 1
        elif dma:
            k = self.n_dma[eng]
            self.n_dma[eng] += 1
            o.sem = (eng, k % N_DMA_SEMS)
            o.semval = 16 * (k // N_DMA_SEMS + 1)
            o.guard = 16 * (k // N_DMA_SEMS) if k >= N_DMA_SEMS else 0
        return o

    def mm(self, out, lhsT, rhs, start=True, stop=True, **kw):
        return self.op("pe", lambda e: e.matmul(out.ap, lhsT.ap, rhs.ap, start=start, stop=stop, **kw),
                       reads=[lhsT, rhs] + ([] if start else []), writes=[out])

    def transpose(self, out, in_, ident):
        return self.op("pe", lambda e: e.transpose(out.ap, in_.ap, ident.ap), reads=[in_, ident], writes=[out])

    def act(self, out, in_, func, bias=None, scale=1.0, accum=None, eng="act"):
        def fn(e):
            kw = {}
            if bias is not None:
                kw["bias"] = bias.ap if isinstance(bias, View) else bias
            if accum is not None:
                kw["accum_out"] = accum.ap
            sc = scale.ap if isinstance(scale, View) else scale
            return e.activation(out.ap, in_.ap, func, scale=sc, **kw)
        rd = [in_] + [x for x in (bias, scale) if isinstance(x, View)]
        wr = [out] + ([accum] if accum is not None else [])
        return self.op(eng, fn, reads=rd, writes=wr)

    def tt(self, eng, out, a, b, op):
        return self.op(eng, lambda e: e.tensor_tensor(out.ap, a.ap, b.ap, op), reads=[a, b], writes=[out])

    def ts(self, eng, out, a, s1, s2, op0, op1=None, accum=None):
        def fn(e):
            x1 = s1.ap if isinstance(s1, View) else s1
            x2 = s2.ap if isinstance(s2, View) else s2
            kw = {}
            if accum is not None:
                kw["accum_out"] = accum.ap
            if op1 is None:
                return e.tensor_scalar(out.ap, a.ap, x1, x2, op0, **kw)
            return e.tensor_scalar(out.ap, a.ap, x1, x2, op0, op1, **kw)
        rd = [a] + [x for x in (s1, s2) if isinstance(x, View)]
        wr = [out] + ([accum] if accum is not None else [])
        return self.op(eng, fn, reads=rd, writes=wr)

    def stt(self, eng, out, a, s, b, op0, op1):
        def fn(e):
            x = s.ap if isinstance(s, View) else s
            return e.scalar_tensor_tensor(out.ap, a.ap, x, b.ap, op0, op1)
        rd = [a, b] + ([s] if isinstance(s, View) else [])
        return self.op(eng, fn, reads=rd, writes=[out])

    def copy(self, eng, out, in_):
        if eng == "act":
            return self.op(eng, lambda e: e.copy(out.ap, in_.ap), reads=[in_], writes=[out])
        return self.op(eng, lambda e: e.tensor_copy(out.ap, in_.ap), reads=[in_], writes=[out])

    def memset(self, eng, out, val):
        return self.op(eng, lambda e: e.memset(out.ap, val), writes=[out])

    def dma(self, eng, out, in_, **kw):
        return self.op(eng, lambda e: e.dma_start(out=out.ap, in_=in_.ap, **kw), reads=[in_], writes=[out], dma=True)

    def finalize(self, final_waits=()):
        nc = self.nc
        seen = {e: {f: -1 for f in self.eng_ops} for e in self.eng_ops}
        seen_dma = {e: {} for e in self.eng_ops}
        for o in self.ops:
            o.waits = []
            E = o.eng
            for y, kind in sorted(o.deps.items(), key=lambda kv: -kv[0].idx):
                if y.is_dma:
                    if seen_dma[E].get(y.sem, 0) >= y.semval:
                        continue
                    seen_dma[E][y.sem] = y.semval
                    o.waits.append(("dma", y.sem, y.semval))
                    continue
                F = y.eng
                if F == E and not o.is_dma:
                    if E == "pe":
                        continue
                if seen[E][F] >= y.pos:
                    continue
                seen[E][F] = y.pos
                y.need_inc = True
                o.waits.append(("eng", y))
            if o.is_dma and o.guard:
                if seen_dma[E].get(o.sem, 0) < o.guard:
                    seen_dma[E][o.sem] = o.guard
                    o.waits.append(("dma", o.sem, o.guard))
        for e, lst in self.eng_ops.items():
            c = 0
            for o in lst:
                if o.need_inc:
                    c += 1
                    o.count = c
        esem = {e: self.stack.enter_context(nc.semaphore("s_" + e)) for e in self.eng_ops}
        dsem = {}
        for e in self.eng_ops:
            if self.n_dma[e]:
                for i in range(min(N_DMA_SEMS, self.n_dma[e])):
                    dsem[(e, i)] = self.stack.enter_context(nc.semaphore("d_%s_%d" % (e, i)))
        if getattr(self, "n_cc", 0):
            dsem[("cc", 0)] = self.stack.enter_context(nc.semaphore("d_cc"))
        self.stats = {e: (len(l), sum(1 for o in l if o.need_inc), sum(len(o.waits) for o in l)) for e, l in self.eng_ops.items()}
        block = self.stack.enter_context(nc.Block())

        def emit(engobj, lst, extra):
            for o in lst:
                for w in o.waits:
                    if w[0] == "dma":
                        engobj.wait_ge(dsem[w[1]], w[2])
                    else:
                        engobj.wait_ge(esem[w[1].eng], w[1].count)
                ins = o.fn(engobj)
                if o.is_dma:
                    ins.then_inc(dsem[o.sem], o.inc)
                elif o.need_inc:
                    ins.then_inc(esem[o.eng], 1)
            for o in extra:
                if o.is_dma:
                    engobj.wait_ge(dsem[o.sem], o.semval)
                else:
                    engobj.wait_ge(esem[o.eng], o.count)

        fw = list(final_waits)
        for o in fw:
            if not o.is_dma and not o.need_inc:
                raise RuntimeError("final wait on op without inc")

        @block.tensor
        def _(e):
            emit(e, self.eng_ops["pe"], [])

        @block.scalar
        def _(e):
            emit(e, self.eng_ops["act"], [])

        @block.vector
        def _(e):
            emit(e, self.eng_ops["dve"], [])

        @block.gpsimd
        def _(e):
            emit(e, self.eng_ops["pool"], [])

        @block.sync
        def _(e):
            emit(e, self.eng_ops["sp"], fw)

        self.stack.close()


D = 1024
T = 2048
TG = 1024
SG = 512
NT = T // 128
DFF = 2816
NJ = DFF // 128
EPS = 1e-6
LN_EPS = 1e-5


def hull(buf, ap, lo, hi):
    return View(buf, ap, lo, hi)


def build(ncores=8, stop=None):
    nc = bass.Bass("TRN2", target_bir_lowering=False)
    P = Prog(nc)

    def din(name, shape, dt=F32):
        return P.dram(name, shape, dt, kind="ExternalInput")

    ident_d = din("ident", [128, 128])
    tri_d = din("tri", [128, 128])
    m0_d = din("m0", [128, 128])
    m1_d = din("m1", [128, 128])
    x_d = din("x", [T, D])
    mem_d = din("mem", [256, D])
    g_d = [din(n, [2, D]) for n in ("ln_mix_pre", "ln_mix_post", "ln_ffn_pre", "ln_ffn_post")]
    g_mem_d = din("ln_mem", [2, D])
    g_sh_d = din("ln_shared", [D])
    w_mkv_d = din("w_mem_kv", [2, D, 512])
    w_out_d = din("w_out", [2, D, D])
    w_g_d = din("w_ffn_gate", [2, D, DFF])
    w_u_d = din("w_ffn_up", [2, D, DFF])
    w_d_d = din("w_ffn_down", [2, DFF, D])
    w_ina_d = din("w_in_a", [1, D, 1792])
    w_sp_d = din("w_spatial", [1, 6, 128, 128])
    b_sp_d = din("b_spatial", [1, 6, 128])
    lnvg_d = din("ln_v_g", [1, 768])
    lnvb_d = din("ln_v_b", [1, 768])
    w_skv_d = din("w_shared_kv", [D, 1548])
    bfor_d = din("b_forget", [12])
    w_inb_d = din("w_in_b", [1, D, D])
    out_d = P.dram("out", [T, D], F32, kind="ExternalOutput")
    kT_loc = [P.dram("kT_loc%d" % g, [768, TG], BF16) for g in range(2)]
    v_loc = [P.dram("v_loc%d" % g, [TG, 768], BF16) for g in range(2)]
    lf_loc = P.dram("lf_loc", [T, 12], F32)
    kT_all = [P.dram("kT_all%d" % g, [2 * 768, TG], BF16) for g in range(2)]
    v_all = [P.dram("v_all%d" % g, [2 * TG, 768], BF16) for g in range(2)]
    lf_all = P.dram("lf_all", [2 * T, 12], F32)
    rg = [[2 * i, 2 * i + 1] for i in range(ncores // 2)]

    def gather(src, dst):
        P.op("pool", lambda e, s_=src, d_=dst: e.collective_compute("AllGather", ALU.bypass, replica_groups=rg,
                                                                     ins=[s_.full.opt()], outs=[d_.full.opt()]),
             reads=[src.all()], writes=[dst.all()], cc=True)

    hT = P.sbuf("hTs", [128, 8, T], F32)
    xn = P.sbuf("xn", [128, 8 * TG], BF16)
    ARENA_B = 88 * 1024
    arena = P.sbuf("arena", [128, ARENA_B // 2], BF16)
    arena32 = arena.full.bitcast(F32)
    xn32 = xn.full.bitcast(F32)
    psF = P.psum("psF", [128, 7 * 512], F32)
    psT = P.psum("psT", [128, 1024], BF16)

    def bank(b, n=512, o=0):
        return psF[:, 512 * b + o:512 * b + o + n]

    class AV:
        def __init__(self, off_b, shape, dt=BF16, base=None, base32=None):
            base = arena if base is None else base
            base32 = arena32 if base32 is None else base32
            self.base = base
            n = 1
            for s_ in shape:
                n *= s_
            if dt == BF16:
                e0 = off_b // 2
                ap = base.full[:, e0:e0 + n]
                self.lo, self.hi, self.esz = e0, e0 + n, 1
            else:
                e0 = off_b // 4
                ap = base32[:, e0:e0 + n]
                self.lo, self.hi, self.esz = off_b // 2, off_b // 2 + 2 * n, 2
            self.shape = list(shape)
            if len(shape) > 1:
                names = " ".join("a%d" % i for i in range(len(shape)))
                kw = {"a%d" % i: s_ for i, s_ in enumerate(shape)}
                ap = ap.rearrange("p (%s) -> p %s" % (names, names), **kw)
            self.ap = ap
            st = [0] * len(shape)
            s_ = 1
            for i in range(len(shape) - 1, -1, -1):
                st[i] = s_
                s_ *= shape[i]
            self.st = st

        def __getitem__(self, idx):
            if not isinstance(idx, tuple):
                idx = (idx,)
            idx = tuple(idx) + (slice(None),) * (1 + len(self.shape) - len(idx))
            lo, hi = 0, 1
            for d, (i, n) in enumerate(zip(idx[1:], self.shape)):
                if isinstance(i, slice):
                    a = 0 if i.start is None else i.start
                    b = n if i.stop is None else i.stop
                else:
                    a, b = i, i + 1
                assert 0 <= a < b <= n
                lo += a * self.st[d]
                hi += (b - 1) * self.st[d]
            return View(self.base, self.ap[idx], self.lo + lo * self.esz, self.lo + hi * self.esz)

    io = AV(32768, [2, D], F32)
    gmem = AV(24576, [D], F32)

    class YV:
        def __getitem__(self, idx):
            p, c, cols = idx
            a = 0 if cols.start is None else cols.start
            b = 512 if cols.stop is None else cols.stop
            e0 = c * 512 + a
            return View(xn, xn32[p, e0:e0 + (b - a)], 2 * e0, 2 * (e0 + b - a))
    y = YV()

    def xnv(c, t0, n, p=slice(None)):
        return xn[p, c * TG + t0:c * TG + t0 + n]

    ident = P.sbuf("ident", [128, 128], F32)
    identb = P.sbuf("identb", [128, 128], BF16)
    tri = P.sbuf("tri", [128, 128], F32)
    onesS = P.sbuf("onesS", [128, 128], BF16)
    ones32 = P.sbuf("ones32", [128, 128], F32)
    gains = P.sbuf("gains", [128, 9, 8], F32)
    KmT = P.sbuf("KmT", [128, 2, 256], BF16)
    Vm = P.sbuf("Vm", [128, 2, 4, 128], BF16)
    sq = P.sbuf("sq", [128, 2, 512], BF16)
    rstd = P.sbuf("rstd", [128, 512], F32)
    tmpn = P.sbuf("tmpn", [128, 2, 512], F32)
    st = P.sbuf("st", [128, 16], F32)
    PT = P.sbuf("PT", [128, 6, 512], BF16)
    Rb = P.sbuf("Rb", [128, 2, 512], F32)
    WmT = P.sbuf("WmT", [128, 6, 128], BF16)
    bsp = P.sbuf("bsp", [128, 6], F32)
    bfor = P.sbuf("bfor", [128, 12], F32)
    lfst = P.sbuf("lfst", [128, NT, 12], F32)
    msk = P.sbuf("msk", [128, 2, 128], BF16)
    lshare = P.sbuf("lshare", [128, 3072], BF16)
    lshare32 = lshare.full.bitcast(F32)
    lnvg = AV(0, [768], F32, lshare, lshare32)
    lnvb = AV(3072, [768], F32, lshare, lshare32)
    BT = AV(0, [4, 2, NT, 12], F32, lshare, lshare32)

    P.dma("sp", ident.all(), ident_d.all())
    P.dma("sp", tri.all(), tri_d.all())
    P.copy("dve", identb.all(), ident.all())
    P.memset("dve", onesS.all(), 1.0 / 1024.0)
    P.memset("dve", ones32.all(), 1.0)
    P.memset("dve", Vm.all(), 1.0)

    def gcol(dst_i, src, ap):
        P.dma("sp", gains[:, dst_i, :], hull(src, ap.rearrange("(c p) -> p c", p=128), 0, src.size),
              allow_slow_non_contiguous=True)
    for L_ in range(2):
        for q in range(4):
            gcol(L_ * 4 + q, g_d[q], g_d[q].full[L_])
    gcol(8, g_sh_d, g_sh_d.full)

    def wload(dst, wd, wap, c0, c1, eng="pool"):
        src = wap[:, c0:c1].rearrange("(k p) c -> p k c", p=128)
        return P.dma(eng, dst, hull(wd, src, 0, wd.size))

    WA = AV(0, [8, 1792])
    WB = AV(28672, [8, 1024])
    mixT = AV(45056, [8, TG])
    qmT = AV(61440, [2, TG])
    TMP0 = 65536
    actb = AV(0, [NJ, TG])
    GU = [AV(45056 + i * 4096, [2, 8, 128]) for i in range(4)]
    DN = [AV(61440 + i * 5632, [NJ, 128]) for i in range(3)]
    QT = AV(0, [6, TG])
    WIN = AV(12288, [8, 1024])
    KTp = AV(65536, [2, T])
    VP = AV(73728, [2, NT, 2, 128])

    def mem_prep(L):
        wmk = AV(0, [8, 512])
        memT = AV(8192, [8, 256])
        memn = AV(8192 + 4096, [2, D])
        P.dma("sp", gmem[:, :], hull(g_mem_d, g_mem_d.full[L].partition_broadcast(128), 0, g_mem_d.size))
        wload(wmk[:, :, :], w_mkv_d, w_mkv_d.full[L], 0, 512)
        for mt in range(2):
            mtile = io[:, mt, :]
            P.dma("sp", mtile, mem_d[mt * 128:(mt + 1) * 128, :])
            P.act(tmpn[:, 0, :], io[:, mt, 0:512], AF.Square, accum=st[:, 0:1])
            P.act(tmpn[:, 1, :], io[:, mt, 512:1024], AF.Square, accum=st[:, 1:2])
            P.tt("dve", st[:, 2:3], st[:, 0:1], st[:, 1:2], ALU.add)
            P.act(st[:, 3:4], st[:, 2:3], AF.Sqrt, bias=EPS, scale=1.0 / D)
            P.op("dve", lambda e, o=st[:, 4:5], i=st[:, 3:4]: e.reciprocal(o.ap, i.ap), reads=[st[:, 3:4]], writes=[st[:, 4:5]])
            P.stt("dve", memn[:, mt, :], mtile, st[:, 4:5], gmem[:, :], ALU.mult, ALU.mult)
            for c in range(8):
                P.transpose(psT[:, c * 128:(c + 1) * 128], memn[:, mt, c * 128:(c + 1) * 128], identb.all())
            P.copy("dve", memT[:, :, mt * 128:(mt + 1) * 128],
                   hull(psT, psT.full.rearrange("p (c t) -> p c t", t=128), 0, 1024))
        for m in range(2):
            for k in range(8):
                P.mm(bank(4 + m, 256), wmk[:, k, m * 128:(m + 1) * 128], memT[:, k, :], start=(k == 0), stop=(k == 7))
            P.copy("act", KmT[:, m, :], bank(4 + m, 256))
        for mt in range(2):
            for k in range(8):
                P.mm(bank(mt, 256), memT[:, k, mt * 128:(mt + 1) * 128], wmk[:, k, 256:512], start=(k == 0), stop=(k == 7))
            P.copy("dve", Vm[:, mt, :, 0:64],
                   hull(psF, psF.full[:, 512 * mt:512 * mt + 256].rearrange("p (h d) -> p h d", d=64), 512 * mt, 512 * mt + 256))

    def rms_stats(src_fn):
        for c in range(8):
            P.act(sq[:, c % 2, :], src_fn(c), AF.Square)
            P.mm(bank(6), onesS.all(), sq[:, c % 2, :], start=(c == 0), stop=(c == 7))
        P.act(tmpn[:, 0, :], bank(6), AF.Sqrt, bias=EPS)
        P.op("dve", lambda e: e.reciprocal(rstd.all().ap, tmpn[:, 0, :].ap), reads=[tmpn[:, 0, :]], writes=[rstd.all()])

    def norm_to_xn(gi, g0, s):
        t0 = g0 + s * SG
        rms_stats(lambda c: hT[:, c, t0:t0 + SG])
        for c in range(8):
            P.stt("dve", xnv(c, s * SG, SG), hT[:, c, t0:t0 + SG], gains[:, gi, c:c + 1], rstd.all(), ALU.mult, ALU.mult)

    def resid_update(gi, g0, s):
        t0 = g0 + s * SG
        rms_stats(lambda c: y[:, c, :])
        for c in range(8):
            P.tt("pool", tmpn[:, c % 2, :], y[:, c, :], rstd.all(), ALU.mult)
            P.stt("dve", hT[:, c, t0:t0 + SG], tmpn[:, c % 2, :], gains[:, gi, c:c + 1], hT[:, c, t0:t0 + SG], ALU.mult, ALU.add)

    def mem_attn(s):
        c0 = s * SG
        for hm in range(4):
            m, r0 = hm // 2, (hm % 2) * 64
            for mt in range(2):
                P.mm(bank(mt), KmT[r0:r0 + 64, m, mt * 128:(mt + 1) * 128], qmT[r0:r0 + 64, m, c0:c0 + SG])
                P.act(PT[:, mt, :], bank(mt), AF.Exp, scale=0.125)
            ob = 2 + hm % 2
            for mt in range(2):
                P.mm(bank(ob), Vm[:, mt, hm, :], PT[:, mt, :], start=(mt == 0), stop=(mt == 1))
            rv = Rb[64:128, hm % 2, :]
            ov = psF[64:128, 512 * ob:512 * ob + 512]
            P.op("dve", lambda e, o=rv, i=ov: e.reciprocal(o.ap, i.ap), reads=[ov], writes=[rv])
            P.tt("dve", mixT[r0:r0 + 64, 6 + m, c0:c0 + SG], psF[0:64, 512 * ob:512 * ob + 512], rv, ALU.mult)

    def out_proj_and_ffn(L, g0):
        for s in range(2):
            c0 = s * SG
            for m in range(8):
                b = 4 + m % 2
                for k in range(8):
                    P.mm(bank(b), WB[:, k, m * 128:(m + 1) * 128], mixT[:, k, c0:c0 + SG], start=(k == 0), stop=(k == 7))
                P.copy("act", y[:, m, :], bank(b))
            resid_update(L * 4 + 1, g0, s)
        for s in range(2):
            norm_to_xn(L * 4 + 2, g0, s)

        def load_gu(j):
            slot = GU[j % 4]
            wload(slot[:, 0, :, :], w_g_d, w_g_d.full[L], j * 128, (j + 1) * 128)
            wload(slot[:, 1, :, :], w_u_d, w_u_d.full[L], j * 128, (j + 1) * 128)
        for j in range(3):
            load_gu(j)
        for j in range(NJ):
            if j + 3 < NJ:
                load_gu(j + 3)
            slot = GU[j % 4]
            for s in range(2):
                c0 = s * SG
                gb, ub = s, 2 + s
                for k in range(8):
                    P.mm(bank(gb), slot[:, 0, k, :], xnv(k, c0, SG), start=(k == 0), stop=(k == 7))
                for k in range(8):
                    P.mm(bank(ub), slot[:, 1, k, :], xnv(k, c0, SG), start=(k == 0), stop=(k == 7))
                P.act(tmpn[:, s, :], bank(gb), AF.Silu)
                P.tt("dve", actb[:, j, c0:c0 + SG], bank(ub), tmpn[:, s, :], ALU.mult)

        def load_dn(q):
            m = q % 8
            wload(DN[q % 3][:, :, :], w_d_d, w_d_d.full[L], m * 128, (m + 1) * 128)
        load_dn(0)
        load_dn(1)
        for s in range(2):
            c0 = s * SG
            for m in range(8):
                q = s * 8 + m
                if q + 2 < 16:
                    load_dn(q + 2)
                b = 4 + m % 2
                for j in range(NJ):
                    P.mm(bank(b), DN[q % 3][:, j, :], actb[:, j, c0:c0 + SG], start=(j == 0), stop=(j == NJ - 1))
                P.copy("act", y[:, m, :], bank(b))
            resid_update(L * 4 + 3, g0, s)

    mem_prep(0)
    wsp32 = AV(16384, [6, 128], F32)
    P.dma("sp", wsp32[:, :, :], hull(w_sp_d, w_sp_d.full[0].rearrange("g t s -> t g s"), 0, w_sp_d.size))
    for g in range(6):
        P.transpose(bank(g % 4, 128), wsp32[:, g, :], ident.all())
        P.tt("dve", WmT[:, g, :], bank(g % 4, 128), tri.all(), ALU.mult)
    P.dma("sp", bsp.all(), hull(b_sp_d, b_sp_d.full[0].rearrange("g t -> t g"), 0, b_sp_d.size),
          allow_slow_non_contiguous=True)
    P.dma("sp", lnvg[:, :], hull(lnvg_d, lnvg_d.full[0].partition_broadcast(128), 0, 768))
    P.dma("sp", lnvb[:, :], hull(lnvb_d, lnvb_d.full[0].partition_broadcast(128), 0, 768))
    P.dma("sp", bfor.all(), hull(bfor_d, bfor_d.full.partition_broadcast(128), 0, 12))
    for t in range(NT):
        xt = io[:, t % 2, :]
        P.dma("sp", xt, x_d[t * 128:(t + 1) * 128, :])
        for c in range(8):
            P.transpose(bank(c // 4, 128, (c % 4) * 128), io[:, t % 2, c * 128:(c + 1) * 128], ident.all())
        for hb in range(2):
            P.copy("act" if hb == 0 else "dve", hT[:, 4 * hb:4 * hb + 4, t * 128:(t + 1) * 128],
                   hull(psF, psF.full[:, 512 * hb:512 * hb + 512].rearrange("p (c t) -> p c t", t=128), 512 * hb, 512 * hb + 512))

    for g in range(2):
        g0 = g * TG
        wload(WA[:, :, :], w_ina_d, w_ina_d.full[0], 0, 1792)
        wload(WB[:, :, :], w_out_d, w_out_d.full[0], 0, D)
        for s in range(2):
            norm_to_xn(0, g0, s)
        for s in range(2):
            c0 = s * SG
            for m in range(2):
                for k in range(8):
                    P.mm(bank(4 + m), WA[:, k, 1536 + m * 128:1536 + (m + 1) * 128], xnv(k, c0, SG), start=(k == 0), stop=(k == 7))
                P.copy("act", qmT[:, m, c0:c0 + SG], bank(4 + m))
        for tl in range(TG // 128):
            o = TMP0 + (tl % 2) * 10752
            gu = AV(o, [768])
            gv = AV(o + 1536, [768], F32)
            vn = AV(o + 4608, [768], F32)
            vln = AV(o + 7680, [768])
            mainb = AV(o + 9216, [768])
            for n in range(3):
                for k in range(8):
                    P.mm(bank(n), xnv(k, tl * 128, 128), WA[:, k, n * 512:(n + 1) * 512], start=(k == 0), stop=(k == 7))
            P.act(gu[:, :], psF[:, 0:768], AF.Gelu_apprx_tanh)
            P.act(gv[:, :], psF[:, 768:1536], AF.Gelu_apprx_tanh, accum=st[:, 8:9])
            P.act(vn[:, :], gv[:, :], AF.Square, accum=st[:, 9:10])
            P.ts("dve", st[:, 10:11], st[:, 8:9], -1.0 / 768, None, ALU.mult)
            P.ts("dve", st[:, 11:12], st[:, 9:10], 1.0 / 768, None, ALU.mult)
            P.stt("dve", st[:, 12:13], st[:, 10:11], st[:, 10:11], st[:, 11:12], ALU.mult, ALU.subtract)
            P.act(st[:, 13:14], st[:, 12:13], AF.Sqrt, bias=LN_EPS, scale=-1.0)
            P.op("dve", lambda e, o_=st[:, 14:15], i_=st[:, 13:14]: e.reciprocal(o_.ap, i_.ap), reads=[st[:, 13:14]], writes=[st[:, 14:15]])
            P.ts("dve", vn[:, :], gv[:, :], st[:, 10:11], st[:, 14:15], ALU.add, ALU.mult)
            P.tt("pool", gv[:, :], vn[:, :], lnvg[:, :], ALU.mult)
            P.tt("pool", vln[:, :], gv[:, :], lnvb[:, :], ALU.add)
            for gg in range(6):
                P.mm(psF[:, 1536 + gg * 128:1536 + (gg + 1) * 128], WmT[:, gg, :], vln[:, gg * 128:(gg + 1) * 128])
            for gg in range(6):
                P.stt("dve", mainb[:, gg * 128:(gg + 1) * 128], psF[:, 1536 + gg * 128:1536 + (gg + 1) * 128],
                      bsp[:, gg:gg + 1], gu[:, gg * 128:(gg + 1) * 128], ALU.add, ALU.mult)
            for gg in range(6):
                P.transpose(psT[:, gg * 128:(gg + 1) * 128], mainb[:, gg * 128:(gg + 1) * 128], identb.all())
            P.copy("act", mixT[:, 0:6, tl * 128:(tl + 1) * 128],
                   hull(psT, psT.full[:, 0:768].rearrange("p (c t) -> p c t", t=128), 0, 768))
        for s in range(2):
            mem_attn(s)
        out_proj_and_ffn(0, g0)
        wload(WA[:, :, 0:1548], w_skv_d, w_skv_d.full, 0, 1548)
        for s in range(2):
            norm_to_xn(8, g0, s)
        kst = AV(45056, [2, 6, SG])
        vst = AV(45056 + 12288, [2, 768])
        for s in range(2):
            c0 = s * SG
            for m in range(6):
                b = 4 + m % 2
                for k in range(8):
                    P.mm(bank(b), WA[:, k, m * 128:(m + 1) * 128], xnv(k, c0, SG), start=(k == 0), stop=(k == 7))
                P.copy("act", kst[:, s, m, :], bank(b))
            P.dma("sp", hull(kT_loc[g], kT_loc[g].full.rearrange("(m p) t -> p m t", p=128)[:, :, c0:c0 + SG], 0, kT_loc[g].size),
                  kst[:, s, :, :])
        for tl in range(TG // 128):
            tg = g * 8 + tl
            for hh in range(2):
                for k in range(8):
                    P.mm(bank(hh, 384), xnv(k, tl * 128, 128), WA[:, k, 768 + hh * 384:768 + (hh + 1) * 384], start=(k == 0), stop=(k == 7))
                P.copy("act" if hh == 0 else "dve", vst[:, tl % 2, hh * 384:(hh + 1) * 384], bank(hh, 384))
            P.dma("sp", hull(v_loc[g], v_loc[g].full[tl * 128:(tl + 1) * 128, :], 0, v_loc[g].size), vst[:, tl % 2, :])
            for k in range(8):
                P.mm(bank(6, 12), xnv(k, tl * 128, 128), WA[:, k, 1536:1548], start=(k == 0), stop=(k == 7))
            P.tt("dve", st[:, 0:12], bank(6, 12), bfor.all(), ALU.add)
            P.act(tmpn[:, 0, 0:12], st[:, 0:12], AF.Exp, scale=-1.0)
            P.act(tmpn[:, 0, 16:28], tmpn[:, 0, 0:12], AF.Ln, bias=1.0)
            P.ts("dve", lfst[:, tg, :], tmpn[:, 0, 16:28], -1.0, None, ALU.mult)
        gather(kT_loc[g], kT_all[g])
        gather(v_loc[g], v_all[g])
    P.dma("sp", hull(lf_loc, lf_loc.full.rearrange("(i p) h -> p i h", p=128), 0, lf_loc.size), lfst.all(),
          allow_slow_non_contiguous=True)

    gather(lf_loc, lf_all)

    mem_prep(1)
    m32 = AV(47200, [2, 128], F32)
    P.dma("sp", m32[:, 0, :], m0_d.all())
    P.dma("sp", m32[:, 1, :], m1_d.all())
    P.copy("dve", msk.all(), m32[:, :, :])
    LF = AV(40960, [2, NT, 12], F32)
    LFf = AV(40960, [2 * NT * 12], F32)
    CT = AV(42496, [2, NT, 12], F32)
    CTf = AV(42496, [2 * NT * 12], F32)
    TOT = AV(44032, [2, NT, 12], F32)
    TOTf = AV(44032, [2 * NT * 12], F32)
    OFF = AV(45568, [2, NT + 1, 12], F32)
    P.dma("sp", LF[:, :, :, :], hull(lf_all, lf_all.full.rearrange("(r i p) h -> p r i h", p=128, i=NT), 0, lf_all.size),
          allow_slow_non_contiguous=True)
    P.mm(bank(0, 384), tri.all(), LFf[:, :])
    P.mm(bank(1, 384), ones32.all(), LFf[:, :])
    P.copy("dve", CTf[:, :], bank(0, 384))
    P.copy("act", TOTf[:, :], bank(1, 384))
    P.memset("dve", OFF[:, 0, 0, :], 0.0)
    for G in range(2 * NT):
        r, i = G % 2, G // 2
        r2, i2 = (G + 1) % 2, (G + 1) // 2
        P.tt("dve", OFF[:, r2, i2, :], OFF[:, r, i, :], TOT[:, r, i, :], ALU.add)
    for r in range(2):
        P.tt("dve", CT[:, r, :, :], CT[:, r, :, :], OFF[:, r, 0:NT, :], ALU.add)
    for j in range(4):
        for r in range(2):
            ni = 4 * j + 4
            lo_ = OFF.lo + 2 * (4 * j * 12)
            cref = View(arena, OFF.ap[:, 0, 4 * j:4 * j + 1, :].to_broadcast([128, ni, 12]), lo_, lo_ + 24)
            P.tt("dve", BT[:, j, r, 0:ni, :], cref, CT[:, r, 0:ni, :], ALU.subtract)

    for g in range(2):
        if stop == 1 or (stop == 2 and g == 1):
            break
        g0 = g * TG
        P.memset("pool", VP[:, :, :, :, :], 1.0)
        wload(WIN[:, :, :], w_inb_d, w_inb_d.full[0], 0, D)
        wload(WB[:, :, :], w_out_d, w_out_d.full[1], 0, D)
        for s in range(2):
            norm_to_xn(4, g0, s)
        for s in range(2):
            c0 = s * SG
            for m in range(8):
                b = 4 + m % 2
                for k in range(8):
                    P.mm(bank(b), WIN[:, k, m * 128:(m + 1) * 128], xnv(k, c0, SG), start=(k == 0), stop=(k == 7))
                if m < 6:
                    P.copy("act", QT[:, m, c0:c0 + SG], bank(b))
                else:
                    P.copy("act", qmT[:, m - 6, c0:c0 + SG], bank(b))
        KTb = [KTp, AV(12288, [2, T])]
        VPb = [AV(73728 + 8192 * i_, [2, NT, 128]) for i_ in range(2)]

        def load_k(hp):
            for g2 in range(2):
                P.dma("sp", KTb[hp % 2][:, :, g2 * TG:(g2 + 1) * TG],
                      hull(kT_all[g2], kT_all[g2].full.rearrange("(r f) t -> f r t", r=2)[hp * 128:(hp + 1) * 128], 0, kT_all[g2].size))

        def load_v(h):
            cc = h * 64
            for g2 in range(2):
                for r_ in range(2):
                    P.dma("sp", VPb[h % 2][:, r_, g2 * 8:(g2 + 1) * 8, 0:64],
                          hull(v_all[g2], v_all[g2].full[r_ * TG:(r_ + 1) * TG, cc:cc + 64].rearrange("(i p) d -> p i d", p=128), 0, v_all[g2].size))

        pairs = []
        for h in range(12):
            for jj in range(2):
                j = 2 * g + jj
                for i in range(4 * j + 4):
                    pairs.append((h, jj, j, i, i == 0, i == 4 * j + 3))
        LA = 2
        load_k(0)
        load_v(0)
        load_v(1)
        for n in range(len(pairs) + LA):
            if n < len(pairs):
                h, jj, j, i, first, last = pairs[n]
                hp, e_ = h // 2, h % 2
                r0 = 64 * e_
                if first and jj == 0 and e_ == 0 and hp + 1 < 6:
                    load_k(hp + 1)
                c0 = jj * SG
                o_ = max(0, i - 4 * j)
                ncol = SG - 128 * o_
                b0 = 2 * (n % 2)
                both = psF[:, 512 * b0:512 * b0 + 1024]
                for r in range(2):
                    o = P.op("pe", lambda e, out=bank(b0 + r, ncol), l=KTb[hp % 2][r0:r0 + 64, r, i * 128:(i + 1) * 128],
                             rr=QT[r0:r0 + 64, hp, c0 + 128 * o_:c0 + SG]: e.matmul(out.ap, l.ap, rr.ap, start=True, stop=True),
                             reads=[KTb[hp % 2][r0:r0 + 64, r, i * 128:(i + 1) * 128], QT[r0:r0 + 64, hp, c0 + 128 * o_:c0 + SG]],
                             writes=[both if r == 0 else bank(b0 + r, ncol)])
                for r in range(2):
                    pt = PT[:, (2 * n + r) % 6, 0:ncol]
                    P.act(pt, bank(b0 + r, ncol), AF.Exp, bias=BT[:, j, r, i, h:h + 1], scale=0.125)
                    if i >= 4 * j:
                        ptd = View(PT, pt.ap[:, 0:128], pt.lo, pt.lo + 128)
                        P.tt("pool", ptd, ptd, msk[:, r, :], ALU.mult)
            if n >= LA:
                m_ = n - LA
                h, jj, j, i, first, last = pairs[m_]
                hp, e_ = h // 2, h % 2
                r0 = 64 * e_
                c0 = jj * SG
                o_ = max(0, i - 4 * j)
                ncol = SG - 128 * o_
                ob = 4 + (2 * h + jj) % 2
                pts = [PT[:, (2 * m_ + r) % 6, 0:ncol] for r in range(2)]
                for r in range(2):
                    P.op("pe", lambda e, out=bank(ob, ncol, 128 * o_), l=VPb[h % 2][:, r, i, :], rr=pts[r], st_=(first and r == 0),
                         sp_=(last and r == 1): e.matmul(out.ap, l.ap, rr.ap, start=st_, stop=sp_),
                         reads=[VPb[h % 2][:, r, i, :], pts[r]] + ([pts[1]] if r == 0 else []), writes=[bank(ob, ncol, 128 * o_)])
                if last:
                    rv = Rb[64:128, jj, :]
                    ov = psF[64:128, 512 * ob:512 * ob + 512]
                    P.op("dve", lambda e, o=rv, i_=ov: e.reciprocal(o.ap, i_.ap), reads=[ov], writes=[rv])
                    P.tt("dve", mixT[r0:r0 + 64, hp, c0:c0 + SG], psF[0:64, 512 * ob:512 * ob + 512], rv, ALU.mult)
                    if jj == 1 and h + 2 < 12:
                        load_v(h + 2)
        for s in range(2):
            mem_attn(s)
        out_proj_and_ffn(1, g0)
    final = []
    for t in range(NT):
        for c in range(8):
            P.transpose(bank(c // 4, 128, (c % 4) * 128), hT[:, c, t * 128:(t + 1) * 128], ident.all())
        P.copy("act", io[:, t % 2, 0:512], bank(0))
        P.copy("dve", io[:, t % 2, 512:1024], bank(1))
        final.append(P.dma("sp", out_d[t * 128:(t + 1) * 128, :], io[:, t % 2, :]))
    P.finalize(final_waits=final)
    return nc, P


PARAM_NAMES = ["ln_mix_pre", "ln_mix_post", "ln_ffn_pre", "ln_ffn_post", "ln_mem", "ln_shared", "w_mem_kv", "w_out",
               "w_ffn_gate", "w_ffn_up", "w_ffn_down", "w_in_a", "w_spatial", "b_spatial", "ln_v_g", "ln_v_b",
               "w_shared_kv", "b_forget", "w_in_b"]


def kernel(**inp):
    inp = {k: np.asarray(v) for k, v in inp.items()}
    x, mem = inp["x"], inp["mem"]
    B = x.shape[0]
    ident = np.eye(128, dtype=np.float32)
    tri = np.triu(np.ones((128, 128), np.float32))
    ones = np.ones((128, 128), np.float32)
    zeros = np.zeros((128, 128), np.float32)
    cores = [(b, r) for b in range(B) for r in range(2)]
    params = {k: np.ascontiguousarray(inp[k], dtype=np.float32) for k in PARAM_NAMES}
    nc, _ = build(len(cores))
    maps = []
    for (b, r) in cores:
        m = dict(params)
        m.update({"ident": ident, "tri": tri,
                  "m0": tri if r == 0 else ones, "m1": zeros if r == 0 else tri,
                  "x": np.ascontiguousarray(x[b].reshape(NT, 2, 128, D)[:, r].reshape(T, D)),
                  "mem": np.ascontiguousarray(mem[b])})
        maps.append(m)
    res = run_bass_kernel_spmd(nc, maps, core_ids=list(range(len(cores)))).results
    out = np.empty((B, NT, 2, 128, D), np.float32)
    for ci, (b, r) in enumerate(cores):
        out[b, :, r] = np.asarray(res[ci]["out"]).reshape(NT, 128, D)
    return out.reshape(B, NT * 2 * 128, D)
```

```python
import contextlib
import numpy as np
import concourse.bass as bass
import concourse.mybir as mybir
from concourse.bass_utils import run_bass_kernel_spmd

F32 = mybir.dt.float32
BF16 = mybir.dt.bfloat16
ALU = mybir.AluOpType
AF = mybir.ActivationFunctionType
AX = mybir.AxisListType

N_DMA_SEMS = 12


class View:
    __slots__ = ("buf", "ap", "lo", "hi")

    def __init__(self, buf, ap, lo, hi):
        self.buf, self.ap, self.lo, self.hi = buf, ap, lo, hi


class Buf:
    def __init__(self, name, full_ap, shape, kind):
        self.name, self.full, self.shape, self.kind = name, full_ap, list(shape), kind
        first = 1 if kind != "dram" else 0
        self.first = first
        st = [0] * len(shape)
        s = 1
        for i in range(len(shape) - 1, first - 1, -1):
            st[i] = s
            s *= shape[i]
        self.strides = st
        self.size = s
        self.recs = []

    def __getitem__(self, idx):
        if not isinstance(idx, tuple):
            idx = (idx,)
        idx = tuple(idx) + (slice(None),) * (len(self.shape) - len(idx))
        lo, hi = 0, 1
        for d, (i, n) in enumerate(zip(idx, self.shape)):
            if isinstance(i, slice):
                a = 0 if i.start is None else i.start
                b = n if i.stop is None else i.stop
                assert i.step in (None, 1) and 0 <= a < b <= n, (self.name, idx)
            else:
                assert 0 <= i < n, (self.name, idx)
                a, b = i, i + 1
            if d >= self.first:
                lo += a * self.strides[d]
                hi += (b - 1) * self.strides[d]
        return View(self, self.full[idx], lo, hi)

    def all(self):
        return self[tuple(slice(None) for _ in self.shape)]


class Op:
    __slots__ = ("eng", "fn", "deps", "is_dma", "pos", "need_inc", "count", "waits", "sem", "semval", "guard", "idx", "inc")


class Prog:
    def __init__(self, nc):
        self.nc = nc
        self.stack = contextlib.ExitStack()
        self.ops = []
        self.eng_ops = {e: [] for e in ("pe", "act", "dve", "pool", "sp")}
        self.n_dma = {e: 0 for e in self.eng_ops}
        self.nbuf = 0

    def sbuf(self, name, shape, dtype):
        t = self.stack.enter_context(self.nc.sbuf_tensor("sb_" + name, list(shape), dtype))
        return Buf(name, t[:] if not hasattr(t, "ap") else t.ap(), shape, "sbuf")

    def psum(self, name, shape, dtype):
        t = self.stack.enter_context(self.nc.psum_tensor("ps_" + name, list(shape), dtype))
        b = Buf(name, t[:] if not hasattr(t, "ap") else t.ap(), shape, "psum")
        b.bank_elems = 512 if dtype == F32 else 1024
        return b

    def dram(self, name, shape, dtype, kind="Internal"):
        t = self.nc.dram_tensor(name, list(shape), dtype, kind=kind)
        return Buf(name, t.ap(), shape, "dram")

    def op(self, eng, fn, reads=(), writes=(), dma=False, cc=False):
        o = Op()
        o.eng, o.fn, o.is_dma = eng, fn, dma
        o.deps = {}
        o.need_inc = False
        o.idx = len(self.ops)

        def norm(v):
            be = getattr(v.buf, "bank_elems", None)
            if v is None or be is None:
                return v
            return View(v.buf, v.ap, v.lo // be * be, -(-v.hi // be) * be)
        reads = [norm(v) for v in reads if v is not None]
        writes = [norm(v) for v in writes if v is not None]
        for v in reads:
            if v is None:
                continue
            for r in v.buf.recs:
                if r[0] < v.hi and v.lo < r[1]:
                    if r[2] is not None:
                        o.deps.setdefault(r[2], "raw")
                    r[3].append(o)
        for v in writes:
            keep = []
            for r in v.buf.recs:
                if r[0] < v.hi and v.lo < r[1]:
                    if r[2] is not None:
                        if o.deps.get(r[2]) != "raw":
                            o.deps[r[2]] = "waw"
                    for rd in r[3]:
                        if rd is not o:
                            o.deps.setdefault(rd, "war")
                    if v.lo <= r[0] and r[1] <= v.hi:
                        continue
                keep.append(r)
            keep.append([v.lo, v.hi, o, []])
            v.buf.recs = keep
        o.pos = len(self.eng_ops[eng])
        self.eng_ops[eng].append(o)
        self.ops.append(o)
        o.inc = 16
        if cc:
            self.n_cc = getattr(self, "n_cc", 0) + 1
            o.is_dma = True
            o.sem = ("cc", 0)
            o.semval = self.n_cc
            o.guard = 0
            o.inc = 1
        elif dma:
            k = self.n_dma[eng]
            self.n_dma[eng] += 1
            o.sem = (eng, k % N_DMA_SEMS)
            o.semval = 16 * (k // N_DMA_SEMS + 1)
            o.guard = 16 * (k // N_DMA_SEMS) if k >= N_DMA_SEMS else 0
        return o

    def mm(self, out, lhsT, rhs, start=True, stop=True, **kw):
        return self.op("pe", lambda e: e.matmul(out.ap, lhsT.ap, rhs.ap, start=start, stop=stop, **kw),
                       reads=[lhsT, rhs] + ([] if start else []), writes=[out])

    def transpose(self, out, in_, ident):
        return self.op("pe", lambda e: e.transpose(out.ap, in_.ap, ident.ap), reads=[in_, ident], writes=[out])

    def act(self, out, in_, func, bias=None, scale=1.0, accum=None, eng="act"):
        def fn(e):
            kw = {}
            if bias is not None:
                kw["bias"] = bias.ap if isinstance(bias, View) else bias
            if accum is not None:
                kw["accum_out"] = accum.ap
            sc = scale.ap if isinstance(scale, View) else scale
            return e.activation(out.ap, in_.ap, func, scale=sc, **kw)
        rd = [in_] + [x for x in (bias, scale) if isinstance(x, View)]
        wr = [out] + ([accum] if accum is not None else [])
        return self.op(eng, fn, reads=rd, writes=wr)

    def tt(self, eng, out, a, b, op):
        return self.op(eng, lambda e: e.tensor_tensor(out.ap, a.ap, b.ap, op), reads=[a, b], writes=[out])

    def ts(self, eng, out, a, s1, s2, op0, op1=None, accum=None):
        def fn(e):
            x1 = s1.ap if isinstance(s1, View) else s1
            x2 = s2.ap if isinstance(s2, View) else s2
            kw = {}
            if accum is not None:
                kw["accum_out"] = accum.ap
            if op1 is None:
                return e.tensor_scalar(out.ap, a.ap, x1, x2, op0, **kw)
            return e.tensor_scalar(out.ap, a.ap, x1, x2, op0, op1, **kw)
        rd = [a] + [x for x in (s1, s2) if isinstance(x, View)]
        wr = [out] + ([accum] if accum is not None else [])
        return self.op(eng, fn, reads=rd, writes=wr)

    def stt(self, eng, out, a, s, b, op0, op1):
        def fn(e):
            x = s.ap if isinstance(s, View) else s
            return e.scalar_tensor_tensor(out.ap, a.ap, x, b.ap, op0, op1)
        rd = [a, b] + ([s] if isinstance(s, View) else [])
        return self.op(eng, fn, reads=rd, writes=[out])

    def copy(self, eng, out, in_):
        if eng == "act":
            return self.op(eng, lambda e: e.copy(out.ap, in_.ap), reads=[in_], writes=[out])
        return self.op(eng, lambda e: e.tensor_copy(out.ap, in_.ap), reads=[in_], writes=[out])

    def memset(self, eng, out, val):
        return self.op(eng, lambda e: e.memset(out.ap, val), writes=[out])

    def dma(self, eng, out, in_, **kw):
        return self.op(eng, lambda e: e.dma_start(out=out.ap, in_=in_.ap, **kw), reads=[in_], writes=[out], dma=True)

    def finalize(self, final_waits=()):
        nc = self.nc
        seen = {e: {f: -1 for f in self.eng_ops} for e in self.eng_ops}
        seen_dma = {e: {} for e in self.eng_ops}
        for o in self.ops:
            o.waits = []
            E = o.eng
            for y, kind in sorted(o.deps.items(), key=lambda kv: -kv[0].idx):
                if y.is_dma:
                    if seen_dma[E].get(y.sem, 0) >= y.semval:
                        continue
                    seen_dma[E][y.sem] = y.semval
                    o.waits.append(("dma", y.sem, y.semval))
                    continue
                F = y.eng
                if F == E and not o.is_dma:
                    if E == "pe":
                        continue
                if seen[E][F] >= y.pos:
                    continue
                seen[E][F] = y.pos
                y.need_inc = True
                o.waits.append(("eng", y))
            if o.is_dma and o.guard:
                if seen_dma[E].get(o.sem, 0) < o.guard:
                    seen_dma[E][o.sem] = o.guard
                    o.waits.append(("dma", o.sem, o.guard))
        for e, lst in self.eng_ops.items():
            c = 0
            for o in lst:
                if o.need_inc:
                    c += 1
                    o.count = c
        esem = {e: self.stack.enter_context(nc.semaphore("s_" + e)) for e in self.eng_ops}
        dsem = {}
        for e in self.eng_ops:
            if self.n_dma[e]:
                for i in range(min(N_DMA_SEMS, self.n_dma[e])):
                    dsem[(e, i)] = self.stack.enter_context(nc.semaphore("d_%s_%d" % (e, i)))
        if getattr(self, "n_cc", 0):
            dsem[("cc", 0)] = self.stack.enter_context(nc.semaphore("d_cc"))
        self.stats = {e: (len(l), sum(1 for o in l if o.need_inc), sum(len(o.waits) for o in l)) for e, l in self.eng_ops.items()}
        block = self.stack.enter_context(nc.Block())

        def emit(engobj, lst, extra):
            for o in lst:
                for w in o.waits:
                    if w[0] == "dma":
                        engobj.wait_ge(dsem[w[1]], w[2])
                    else:
                        engobj.wait_ge(esem[w[1].eng], w[1].count)
                ins = o.fn(engobj)
                if o.is_dma:
                    ins.then_inc(dsem[o.sem], o.inc)
                elif o.need_inc:
                    ins.then_inc(esem[o.eng], 1)
            for o in extra:
                if o.is_dma:
                    engobj.wait_ge(dsem[o.sem], o.semval)
                else:
                    engobj.wait_ge(esem[o.eng], o.count)

        fw = list(final_waits)
        for o in fw:
            if not o.is_dma and not o.need_inc:
                raise RuntimeError("final wait on op without inc")

        @block.tensor
        def _(e):
            emit(e, self.eng_ops["pe"], [])

        @block.scalar
        def _(e):
            emit(e, self.eng_ops["act"], [])

        @block.vector
        def _(e):
            emit(e, self.eng_ops["dve"], [])

        @block.gpsimd
        def _(e):
            emit(e, self.eng_ops["pool"], [])

        @block.sync
        def _(e):
            emit(e, self.eng_ops["sp"], fw)

        self.stack.close()


D = 1024
T = 2048
TG = 1024
SG = 512
NT = T // 128
DFF = 2816
NJ = DFF // 128
EPS = 1e-6
LN_EPS = 1e-5


def hull(buf, ap, lo, hi):
    return View(buf, ap, lo, hi)


def build(ncores=8, stop=None):
    nc = bass.Bass("TRN2", target_bir_lowering=False)
    P = Prog(nc)

    def din(name, shape, dt=F32):
        return P.dram(name, shape, dt, kind="ExternalInput")

    ident_d = din("ident", [128, 128])
    tri_d = din("tri", [128, 128])
    m0_d = din("m0", [128, 128])
    m1_d = din("m1", [128, 128])
    x_d = din("x", [T, D])
    mem_d = din("mem", [256, D])
    g_d = [din(n, [2, D]) for n in ("ln_mix_pre", "ln_mix_post", "ln_ffn_pre", "ln_ffn_post")]
    g_mem_d = din("ln_mem", [2, D])
    g_sh_d = din("ln_shared", [D])
    w_mkv_d = din("w_mem_kv", [2, D, 512])
    w_out_d = din("w_out", [2, D, D])
    w_g_d = din("w_ffn_gate", [2, D, DFF])
    w_u_d = din("w_ffn_up", [2, D, DFF])
    w_d_d = din("w_ffn_down", [2, DFF, D])
    w_ina_d = din("w_in_a", [1, D, 1792])
    w_sp_d = din("w_spatial", [1, 6, 128, 128])
    b_sp_d = din("b_spatial", [1, 6, 128])
    lnvg_d = din("ln_v_g", [1, 768])
    lnvb_d = din("ln_v_b", [1, 768])
    w_skv_d = din("w_shared_kv", [D, 1548])
    bfor_d = din("b_forget", [12])
    w_inb_d = din("w_in_b", [1, D, D])
    out_d = P.dram("out", [T, D], F32, kind="ExternalOutput")
    kT_loc = [P.dram("kT_loc%d" % g, [768, TG], BF16) for g in range(2)]
    v_loc = [P.dram("v_loc%d" % g, [TG, 768], BF16) for g in range(2)]
    lf_loc = P.dram("lf_loc", [T, 12], F32)
    kT_all = [P.dram("kT_all%d" % g, [2 * 768, TG], BF16) for g in range(2)]
    v_all = [P.dram("v_all%d" % g, [2 * TG, 768], BF16) for g in range(2)]
    lf_all = P.dram("lf_all", [2 * T, 12], F32)
    rg = [[2 * i, 2 * i + 1] for i in range(ncores // 2)]

    def gather(src, dst):
        P.op("pool", lambda e, s_=src, d_=dst: e.collective_compute("AllGather", ALU.bypass, replica_groups=rg,
                                                                     ins=[s_.full.opt()], outs=[d_.full.opt()]),
             reads=[src.all()], writes=[dst.all()], cc=True)

    hT = P.sbuf("hTs", [128, 8, T], F32)
    xn = P.sbuf("xn", [128, 8 * TG], BF16)
    ARENA_B = 88 * 1024
    arena = P.sbuf("arena", [128, ARENA_B // 2], BF16)
    arena32 = arena.full.bitcast(F32)
    xn32 = xn.full.bitcast(F32)
    psF = P.psum("psF", [128, 7 * 512], F32)
    psT = P.psum("psT", [128, 1024], BF16)

    def bank(b, n=512, o=0):
        return psF[:, 512 * b + o:512 * b + o + n]

    class AV:
        def __init__(self, off_b, shape, dt=BF16, base=None, base32=None):
            base = arena if base is None else base
            base32 = arena32 if base32 is None else base32
            self.base = base
            n = 1
            for s_ in shape:
                n *= s_
            if dt == BF16:
                e0 = off_b // 2
                ap = base.full[:, e0:e0 + n]
                self.lo, self.hi, self.esz = e0, e0 + n, 1
            else:
                e0 = off_b // 4
                ap = base32[:, e0:e0 + n]
                self.lo, self.hi, self.esz = off_b // 2, off_b // 2 + 2 * n, 2
            self.shape = list(shape)
            if len(shape) > 1:
                names = " ".join("a%d" % i for i in range(len(shape)))
                kw = {"a%d" % i: s_ for i, s_ in enumerate(shape)}
                ap = ap.rearrange("p (%s) -> p %s" % (names, names), **kw)
            self.ap = ap
            st = [0] * len(shape)
            s_ = 1
            for i in range(len(shape) - 1, -1, -1):
                st[i] = s_
                s_ *= shape[i]
            self.st = st

        def __getitem__(self, idx):
            if not isinstance(idx, tuple):
                idx = (idx,)
            idx = tuple(idx) + (slice(None),) * (1 + len(self.shape) - len(idx))
            lo, hi = 0, 1
            for d, (i, n) in enumerate(zip(idx[1:], self.shape)):
                if isinstance(i, slice):
                    a = 0 if i.start is None else i.start
                    b = n if i.stop is None else i.stop
                else:
                    a, b = i, i + 1
                assert 0 <= a < b <= n
                lo += a * self.st[d]
                hi += (b - 1) * self.st[d]
            return View(self.base, self.ap[idx], self.lo + lo * self.esz, self.lo + hi * self.esz)

    io = AV(32768, [2, D], F32)
    gmem = AV(24576, [D], F32)

    class YV:
        def __getitem__(self, idx):
            p, c, cols = idx
            a = 0 if cols.start is None else cols.start
            b = 512 if cols.stop is None else cols.stop
            e0 = c * 512 + a
            return View(xn, xn32[p, e0:e0 + (b - a)], 2 * e0, 2 * (e0 + b - a))
    y = YV()

    def xnv(c, t0, n, p=slice(None)):
        return xn[p, c * TG + t0:c * TG + t0 + n]

    ident = P.sbuf("ident", [128, 128], F32)
    identb = P.sbuf("identb", [128, 128], BF16)
    tri = P.sbuf("tri", [128, 128], F32)
    onesS = P.sbuf("onesS", [128, 128], BF16)
    ones32 = P.sbuf("ones32", [128, 128], F32)
    gains = P.sbuf("gains", [128, 9, 8], F32)
    KmT = P.sbuf("KmT", [128, 2, 256], BF16)
    Vm = P.sbuf("Vm", [128, 2, 4, 128], BF16)
    sq = P.sbuf("sq", [128, 2, 512], BF16)
    rstd = P.sbuf("rstd", [128, 512], F32)
    tmpn = P.sbuf("tmpn", [128, 2, 512], F32)
    st = P.sbuf("st", [128, 16], F32)
    PT = P.sbuf("PT", [128, 6, 512], BF16)
    Rb = P.sbuf("Rb", [128, 2, 512], F32)
    WmT = P.sbuf("WmT", [128, 6, 128], BF16)
    bsp = P.sbuf("bsp", [128, 6], F32)
    bfor = P.sbuf("bfor", [128, 12], F32)
    lfst = P.sbuf("lfst", [128, NT, 12], F32)
    msk = P.sbuf("msk", [128, 2, 128], BF16)
    lshare = P.sbuf("lshare", [128, 3072], BF16)
    lshare32 = lshare.full.bitcast(F32)
    lnvg = AV(0, [768], F32, lshare, lshare32)
    lnvb = AV(3072, [768], F32, lshare, lshare32)
    BT = AV(0, [4, 2, NT, 12], F32, lshare, lshare32)

    P.dma("sp", ident.all(), ident_d.all())
    P.dma("sp", tri.all(), tri_d.all())
    P.copy("dve", identb.all(), ident.all())
    P.memset("dve", onesS.all(), 1.0 / 1024.0)
    P.memset("dve", ones32.all(), 1.0)
    P.memset("dve", Vm.all(), 1.0)

    def gcol(dst_i, src, ap):
        P.dma("sp", gains[:, dst_i, :], hull(src, ap.rearrange("(c p) -> p c", p=128), 0, src.size),
              allow_slow_non_contiguous=True)
    for L_ in range(2):
        for q in range(4):
            gcol(L_ * 4 + q, g_d[q], g_d[q].full[L_])
    gcol(8, g_sh_d, g_sh_d.full)

    def wload(dst, wd, wap, c0, c1, eng="pool"):
        src = wap[:, c0:c1].rearrange("(k p) c -> p k c", p=128)
        return P.dma(eng, dst, hull(wd, src, 0, wd.size))

    WA = AV(0, [8, 1792])
    WB = AV(28672, [8, 1024])
    mixT = AV(45056, [8, TG])
    qmT = AV(61440, [2, TG])
    TMP0 = 65536
    actb = AV(0, [NJ, TG])
    GU = [AV(45056 + i * 4096, [2, 8, 128]) for i in range(4)]
    DN = [AV(61440 + i * 5632, [NJ, 128]) for i in range(3)]
    QT = AV(0, [6, TG])
    WIN = AV(12288, [8, 1024])
    KTp = AV(65536, [2, T])
    VP = AV(73728, [2, NT, 2, 128])

    def mem_prep(L):
        wmk = AV(0, [8, 512])
        memT = AV(8192, [8, 256])
        memn = AV(8192 + 4096, [2, D])
        P.dma("sp", gmem[:, :], hull(g_mem_d, g_mem_d.full[L].partition_broadcast(128), 0, g_mem_d.size))
        wload(wmk[:, :, :], w_mkv_d, w_mkv_d.full[L], 0, 512)
        for mt in range(2):
            mtile = io[:, mt, :]
            P.dma("sp", mtile, mem_d[mt * 128:(mt + 1) * 128, :])
            P.act(tmpn[:, 0, :], io[:, mt, 0:512], AF.Square, accum=st[:, 0:1])
            P.act(tmpn[:, 1, :], io[:, mt, 512:1024], AF.Square, accum=st[:, 1:2])
            P.tt("dve", st[:, 2:3], st[:, 0:1], st[:, 1:2], ALU.add)
            P.act(st[:, 3:4], st[:, 2:3], AF.Sqrt, bias=EPS, scale=1.0 / D)
            P.op("dve", lambda e, o=st[:, 4:5], i=st[:, 3:4]: e.reciprocal(o.ap, i.ap), reads=[st[:, 3:4]], writes=[st[:, 4:5]])
            P.stt("dve", memn[:, mt, :], mtile, st[:, 4:5], gmem[:, :], ALU.mult, ALU.mult)
            for c in range(8):
                P.transpose(psT[:, c * 128:(c + 1) * 128], memn[:, mt, c * 128:(c + 1) * 128], identb.all())
            P.copy("dve", memT[:, :, mt * 128:(mt + 1) * 128],
                   hull(psT, psT.full.rearrange("p (c t) -> p c t", t=128), 0, 1024))
        for m in range(2):
            for k in range(8):
                P.mm(bank(4 + m, 256), wmk[:, k, m * 128:(m + 1) * 128], memT[:, k, :], start=(k == 0), stop=(k == 7))
            P.copy("act", KmT[:, m, :], bank(4 + m, 256))
        for mt in range(2):
            for k in range(8):
                P.mm(bank(mt, 256), memT[:, k, mt * 128:(mt + 1) * 128], wmk[:, k, 256:512], start=(k == 0), stop=(k == 7))
            P.copy("dve", Vm[:, mt, :, 0:64],
                   hull(psF, psF.full[:, 512 * mt:512 * mt + 256].rearrange("p (h d) -> p h d", d=64), 512 * mt, 512 * mt + 256))

    def rms_stats(src_fn):
        for c in range(8):
            P.act(sq[:, c % 2, :], src_fn(c), AF.Square)
            P.mm(bank(6), onesS.all(), sq[:, c % 2, :], start=(c == 0), stop=(c == 7))
        rstd_from_bank6()

    def rstd_from_bank6():
        P.act(tmpn[:, 0, :], bank(6), AF.Ln, bias=EPS)
        P.act(rstd.all(), tmpn[:, 0, :], AF.Exp, scale=-0.5)

    def norm_to_xn(gi, g0, s):
        t0 = g0 + s * SG
        rms_stats(lambda c: hT[:, c, t0:t0 + SG])
        for c in range(8):
            P.stt("dve", xnv(c, s * SG, SG), hT[:, c, t0:t0 + SG], gains[:, gi, c:c + 1], rstd.all(), ALU.mult, ALU.mult)

    def proj_resid(gi, g0, s, mm_fn):
        t0 = g0 + s * SG
        for m in range(9):
            if m < 8:
                b = 4 + m % 2
                mm_fn(m, bank(b))
                P.act(sq[:, m % 2, :], bank(b), AF.Square)
                P.act(y[:, m, :], bank(b), AF.Copy, scale=gains[:, gi, m:m + 1])
            if m >= 1:
                P.mm(bank(6), onesS.all(), sq[:, (m - 1) % 2, :], start=(m == 1), stop=(m == 8))
        rstd_from_bank6()
        for c in range(8):
            P.tt("dve", tmpn[:, c % 2, :], y[:, c, :], rstd.all(), ALU.mult)
            P.tt("dve", hT[:, c, t0:t0 + SG], hT[:, c, t0:t0 + SG], tmpn[:, c % 2, :], ALU.add)

    def mem_attn(s):
        c0 = s * SG
        for hm in range(4):
            m, r0 = hm // 2, (hm % 2) * 64
            for mt in range(2):
                P.mm(bank(mt), KmT[r0:r0 + 64, m, mt * 128:(mt + 1) * 128], qmT[r0:r0 + 64, m, c0:c0 + SG])
                P.act(PT[:, mt, :], bank(mt), AF.Exp, scale=0.125)
            ob = 2 + hm % 2
            for mt in range(2):
                P.mm(bank(ob), Vm[:, mt, hm, :], PT[:, mt, :], start=(mt == 0), stop=(mt == 1))
            rv = Rb[64:128, hm % 2, :]
            ov = psF[64:128, 512 * ob:512 * ob + 512]
            P.op("dve", lambda e, o=rv, i=ov: e.reciprocal(o.ap, i.ap), reads=[ov], writes=[rv])
            P.tt("dve", mixT[r0:r0 + 64, 6 + m, c0:c0 + SG], psF[0:64, 512 * ob:512 * ob + 512], rv, ALU.mult)

    def out_proj_and_ffn(L, g0):
        for s in range(2):
            c0 = s * SG

            def mm_out(m, bv, c0=c0):
                for k in range(8):
                    P.mm(bv, WB[:, k, m * 128:(m + 1) * 128], mixT[:, k, c0:c0 + SG], start=(k == 0), stop=(k == 7))
            proj_resid(L * 4 + 1, g0, s, mm_out)
        for s in range(2):
            norm_to_xn(L * 4 + 2, g0, s)

        def load_gu(j):
            slot = GU[j % 4]
            wload(slot[:, 0, :, :], w_g_d, w_g_d.full[L], j * 128, (j + 1) * 128)
            wload(slot[:, 1, :, :], w_u_d, w_u_d.full[L], j * 128, (j + 1) * 128)
        for j in range(3):
            load_gu(j)
        for j in range(NJ):
            if j + 3 < NJ:
                load_gu(j + 3)
            slot = GU[j % 4]
            for s in range(2):
                c0 = s * SG
                gb, ub = s, 2 + s
                for k in range(8):
                    P.mm(bank(gb), slot[:, 0, k, :], xnv(k, c0, SG), start=(k == 0), stop=(k == 7))
                for k in range(8):
                    P.mm(bank(ub), slot[:, 1, k, :], xnv(k, c0, SG), start=(k == 0), stop=(k == 7))
                P.act(tmpn[:, s, :], bank(gb), AF.Silu)
                P.tt("dve", actb[:, j, c0:c0 + SG], bank(ub), tmpn[:, s, :], ALU.mult)

        def load_dn(q):
            m = q % 8
            wload(DN[q % 3][:, :, :], w_d_d, w_d_d.full[L], m * 128, (m + 1) * 128)
        load_dn(0)
        load_dn(1)
        for s in range(2):
            c0 = s * SG

            def mm_dn(m, bv, c0=c0, s=s):
                q = s * 8 + m
                if q + 2 < 16:
                    load_dn(q + 2)
                for j in range(NJ):
                    P.mm(bv, DN[q % 3][:, j, :], actb[:, j, c0:c0 + SG], start=(j == 0), stop=(j == NJ - 1))
            proj_resid(L * 4 + 3, g0, s, mm_dn)

    mem_prep(0)
    wsp32 = AV(16384, [6, 128], F32)
    P.dma("sp", wsp32[:, :, :], hull(w_sp_d, w_sp_d.full[0].rearrange("g t s -> t g s"), 0, w_sp_d.size))
    for g in range(6):
        P.transpose(bank(g % 4, 128), wsp32[:, g, :], ident.all())
        P.tt("dve", WmT[:, g, :], bank(g % 4, 128), tri.all(), ALU.mult)
    P.dma("sp", bsp.all(), hull(b_sp_d, b_sp_d.full[0].rearrange("g t -> t g"), 0, b_sp_d.size),
          allow_slow_non_contiguous=True)
    P.dma("sp", lnvg[:, :], hull(lnvg_d, lnvg_d.full[0].partition_broadcast(128), 0, 768))
    P.dma("sp", lnvb[:, :], hull(lnvb_d, lnvb_d.full[0].partition_broadcast(128), 0, 768))
    P.dma("sp", bfor.all(), hull(bfor_d, bfor_d.full.partition_broadcast(128), 0, 12))
    for t in range(NT):
        xt = io[:, t % 2, :]
        P.dma("sp", xt, x_d[t * 128:(t + 1) * 128, :])
        for c in range(8):
            P.transpose(bank(c // 4, 128, (c % 4) * 128), io[:, t % 2, c * 128:(c + 1) * 128], ident.all())
        for hb in range(2):
            P.copy("act" if hb == 0 else "dve", hT[:, 4 * hb:4 * hb + 4, t * 128:(t + 1) * 128],
                   hull(psF, psF.full[:, 512 * hb:512 * hb + 512].rearrange("p (c t) -> p c t", t=128), 512 * hb, 512 * hb + 512))

    for g in range(2):
        g0 = g * TG
        wload(WA[:, :, :], w_ina_d, w_ina_d.full[0], 0, 1792)
        wload(WB[:, :, :], w_out_d, w_out_d.full[0], 0, D)
        for s in range(2):
            norm_to_xn(0, g0, s)
        for s in range(2):
            c0 = s * SG
            for m in range(2):
                for k in range(8):
                    P.mm(bank(4 + m), WA[:, k, 1536 + m * 128:1536 + (m + 1) * 128], xnv(k, c0, SG), start=(k == 0), stop=(k == 7))
                P.copy("act", qmT[:, m, c0:c0 + SG], bank(4 + m))
        def tmp_bufs(tl):
            o = TMP0 + (tl % 2) * 10752
            return (AV(o, [768]), AV(o + 1536, [768], F32), AV(o + 4608, [768], F32), AV(o + 7680, [768]), AV(o + 9216, [768]))

        def mix_a(tl):
            gu, gv, vn, vln, mainb = tmp_bufs(tl)
            for n in range(3):
                for k in range(8):
                    P.mm(bank(n), xnv(k, tl * 128, 128), WA[:, k, n * 512:(n + 1) * 512], start=(k == 0), stop=(k == 7))
            P.act(gu[:, :], psF[:, 0:768], AF.Gelu_apprx_tanh)
            P.act(gv[:, :], psF[:, 768:1536], AF.Gelu_apprx_tanh, accum=st[:, 8:9])
            P.act(vn[:, :], gv[:, :], AF.Square, accum=st[:, 9:10])
            P.ts("dve", st[:, 10:11], st[:, 8:9], -1.0 / 768, None, ALU.mult)
            P.ts("dve", st[:, 11:12], st[:, 9:10], 1.0 / 768, None, ALU.mult)
            P.stt("dve", st[:, 12:13], st[:, 10:11], st[:, 10:11], st[:, 11:12], ALU.mult, ALU.subtract)
            P.act(st[:, 13:14], st[:, 12:13], AF.Sqrt, bias=LN_EPS, scale=-1.0)
            P.op("dve", lambda e, o_=st[:, 14:15], i_=st[:, 13:14]: e.reciprocal(o_.ap, i_.ap), reads=[st[:, 13:14]], writes=[st[:, 14:15]])
            P.ts("dve", vn[:, :], gv[:, :], st[:, 10:11], st[:, 14:15], ALU.add, ALU.mult)
            P.tt("dve", gv[:, :], vn[:, :], lnvg[:, :], ALU.mult)
            P.tt("dve", vln[:, :], gv[:, :], lnvb[:, :], ALU.add)

        def mix_b(tl):
            gu, gv, vn, vln, mainb = tmp_bufs(tl)
            for gg in range(6):
                P.mm(psF[:, 1536 + gg * 128:1536 + (gg + 1) * 128], WmT[:, gg, :], vln[:, gg * 128:(gg + 1) * 128])
            for gg in range(6):
                P.stt("dve", mainb[:, gg * 128:(gg + 1) * 128], psF[:, 1536 + gg * 128:1536 + (gg + 1) * 128],
                      bsp[:, gg:gg + 1], gu[:, gg * 128:(gg + 1) * 128], ALU.add, ALU.mult)
            for gg in range(6):
                P.transpose(psT[:, gg * 128:(gg + 1) * 128], mainb[:, gg * 128:(gg + 1) * 128], identb.all())
            P.copy("act", mixT[:, 0:6, tl * 128:(tl + 1) * 128],
                   hull(psT, psT.full[:, 0:768].rearrange("p (c t) -> p c t", t=128), 0, 768))

        ntl = TG // 128
        for tl in range(ntl + 1):
            if tl < ntl:
                mix_a(tl)
            if tl >= 1:
                mix_b(tl - 1)
        for s in range(2):
            mem_attn(s)
        out_proj_and_ffn(0, g0)
        wload(WA[:, :, 0:1548], w_skv_d, w_skv_d.full, 0, 1548)
        for s in range(2):
            norm_to_xn(8, g0, s)
        kst = AV(45056, [2, 6, SG])
        vst = AV(45056 + 12288, [2, 768])
        for s in range(2):
            c0 = s * SG
            for m in range(6):
                b = 4 + m % 2
                for k in range(8):
                    P.mm(bank(b), WA[:, k, m * 128:(m + 1) * 128], xnv(k, c0, SG), start=(k == 0), stop=(k == 7))
                P.copy("act", kst[:, s, m, :], bank(b))
            P.dma("sp", hull(kT_loc[g], kT_loc[g].full.rearrange("(m p) t -> p m t", p=128)[:, :, c0:c0 + SG], 0, kT_loc[g].size),
                  kst[:, s, :, :])
        for tl in range(TG // 128):
            tg = g * 8 + tl
            for hh in range(2):
                for k in range(8):
                    P.mm(bank(hh, 384), xnv(k, tl * 128, 128), WA[:, k, 768 + hh * 384:768 + (hh + 1) * 384], start=(k == 0), stop=(k == 7))
                P.copy("act" if hh == 0 else "dve", vst[:, tl % 2, hh * 384:(hh + 1) * 384], bank(hh, 384))
            P.dma("sp", hull(v_loc[g], v_loc[g].full[tl * 128:(tl + 1) * 128, :], 0, v_loc[g].size), vst[:, tl % 2, :])
            for k in range(8):
                P.mm(bank(6, 12), xnv(k, tl * 128, 128), WA[:, k, 1536:1548], start=(k == 0), stop=(k == 7))
            P.tt("dve", st[:, 0:12], bank(6, 12), bfor.all(), ALU.add)
            P.act(tmpn[:, 0, 0:12], st[:, 0:12], AF.Exp, scale=-1.0)
            P.act(tmpn[:, 0, 16:28], tmpn[:, 0, 0:12], AF.Ln, bias=1.0)
            P.ts("dve", lfst[:, tg, :], tmpn[:, 0, 16:28], -1.0, None, ALU.mult)
        gather(kT_loc[g], kT_all[g])
        gather(v_loc[g], v_all[g])
    P.dma("sp", hull(lf_loc, lf_loc.full.rearrange("(i p) h -> p i h", p=128), 0, lf_loc.size), lfst.all(),
          allow_slow_non_contiguous=True)

    gather(lf_loc, lf_all)

    mem_prep(1)
    m32 = AV(47200, [2, 128], F32)
    P.dma("sp", m32[:, 0, :], m0_d.all())
    P.dma("sp", m32[:, 1, :], m1_d.all())
    P.copy("dve", msk.all(), m32[:, :, :])
    LF = AV(40960, [2, NT, 12], F32)
    LFf = AV(40960, [2 * NT * 12], F32)
    CT = AV(42496, [2, NT, 12], F32)
    CTf = AV(42496, [2 * NT * 12], F32)
    TOT = AV(44032, [2, NT, 12], F32)
    TOTf = AV(44032, [2 * NT * 12], F32)
    OFF = AV(45568, [2, NT + 1, 12], F32)
    P.dma("sp", LF[:, :, :, :], hull(lf_all, lf_all.full.rearrange("(r i p) h -> p r i h", p=128, i=NT), 0, lf_all.size),
          allow_slow_non_contiguous=True)
    P.mm(bank(0, 384), tri.all(), LFf[:, :])
    P.mm(bank(1, 384), ones32.all(), LFf[:, :])
    P.copy("dve", CTf[:, :], bank(0, 384))
    P.copy("act", TOTf[:, :], bank(1, 384))
    P.memset("dve", OFF[:, 0, 0, :], 0.0)
    for G in range(2 * NT):
        r, i = G % 2, G // 2
        r2, i2 = (G + 1) % 2, (G + 1) // 2
        P.tt("dve", OFF[:, r2, i2, :], OFF[:, r, i, :], TOT[:, r, i, :], ALU.add)
    for r in range(2):
        P.tt("dve", CT[:, r, :, :], CT[:, r, :, :], OFF[:, r, 0:NT, :], ALU.add)
    for j in range(4):
        for r in range(2):
            ni = 4 * j + 4
            lo_ = OFF.lo + 2 * (4 * j * 12)
            cref = View(arena, OFF.ap[:, 0, 4 * j:4 * j + 1, :].to_broadcast([128, ni, 12]), lo_, lo_ + 24)
            P.tt("dve", BT[:, j, r, 0:ni, :], cref, CT[:, r, 0:ni, :], ALU.subtract)

    for g in range(2):
        if stop == 1 or (stop == 2 and g == 1):
            break
        g0 = g * TG
        P.memset("pool", VP[:, :, :, :, :], 1.0)
        wload(WIN[:, :, :], w_inb_d, w_inb_d.full[0], 0, D)
        wload(WB[:, :, :], w_out_d, w_out_d.full[1], 0, D)
        for s in range(2):
            norm_to_xn(4, g0, s)
        for s in range(2):
            c0 = s * SG
            for m in range(8):
                b = 4 + m % 2
                for k in range(8):
                    P.mm(bank(b), WIN[:, k, m * 128:(m + 1) * 128], xnv(k, c0, SG), start=(k == 0), stop=(k == 7))
                if m < 6:
                    P.copy("act", QT[:, m, c0:c0 + SG], bank(b))
                else:
                    P.copy("act", qmT[:, m - 6, c0:c0 + SG], bank(b))
        KTb = [KTp, AV(12288, [2, T])]
        VPb = [AV(73728 + 8192 * i_, [2, NT, 128]) for i_ in range(2)]

        def load_k(hp):
            for g2 in range(2):
                P.dma("sp", KTb[hp % 2][:, :, g2 * TG:(g2 + 1) * TG],
                      hull(kT_all[g2], kT_all[g2].full.rearrange("(r f) t -> f r t", r=2)[hp * 128:(hp + 1) * 128], 0, kT_all[g2].size))

        def load_v(h):
            cc = h * 64
            for g2 in range(2):
                for r_ in range(2):
                    P.dma("sp", VPb[h % 2][:, r_, g2 * 8:(g2 + 1) * 8, 0:64],
                          hull(v_all[g2], v_all[g2].full[r_ * TG:(r_ + 1) * TG, cc:cc + 64].rearrange("(i p) d -> p i d", p=128), 0, v_all[g2].size))

        pairs = []
        for h in range(12):
            for jj in range(2):
                j = 2 * g + jj
                for i in range(4 * j + 4):
                    pairs.append((h, jj, j, i, i == 0, i == 4 * j + 3))
        LA = 2
        load_k(0)
        load_v(0)
        load_v(1)
        for n in range(len(pairs) + LA):
            if n < len(pairs):
                h, jj, j, i, first, last = pairs[n]
                hp, e_ = h // 2, h % 2
                r0 = 64 * e_
                if first and jj == 0 and e_ == 0 and hp + 1 < 6:
                    load_k(hp + 1)
                c0 = jj * SG
                o_ = max(0, i - 4 * j)
                ncol = SG - 128 * o_
                b0 = 2 * (n % 2)
                both = psF[:, 512 * b0:512 * b0 + 1024]
                for r in range(2):
                    o = P.op("pe", lambda e, out=bank(b0 + r, ncol), l=KTb[hp % 2][r0:r0 + 64, r, i * 128:(i + 1) * 128],
                             rr=QT[r0:r0 + 64, hp, c0 + 128 * o_:c0 + SG]: e.matmul(out.ap, l.ap, rr.ap, start=True, stop=True),
                             reads=[KTb[hp % 2][r0:r0 + 64, r, i * 128:(i + 1) * 128], QT[r0:r0 + 64, hp, c0 + 128 * o_:c0 + SG]],
                             writes=[both if r == 0 else bank(b0 + r, ncol)])
                for r in range(2):
                    pt = PT[:, (2 * n + r) % 6, 0:ncol]
                    P.act(pt, bank(b0 + r, ncol), AF.Exp, bias=BT[:, j, r, i, h:h + 1], scale=0.125)
                    if i >= 4 * j:
                        ptd = View(PT, pt.ap[:, 0:128], pt.lo, pt.lo + 128)
                        P.tt("pool", ptd, ptd, msk[:, r, :], ALU.mult)
            if n >= LA:
                m_ = n - LA
                h, jj, j, i, first, last = pairs[m_]
                hp, e_ = h // 2, h % 2
                r0 = 64 * e_
                c0 = jj * SG
                o_ = max(0, i - 4 * j)
                ncol = SG - 128 * o_
                ob = 4 + (2 * h + jj) % 2
                pts = [PT[:, (2 * m_ + r) % 6, 0:ncol] for r in range(2)]
                for r in range(2):
                    P.op("pe", lambda e, out=bank(ob, ncol, 128 * o_), l=VPb[h % 2][:, r, i, :], rr=pts[r], st_=(first and r == 0),
                         sp_=(last and r == 1): e.matmul(out.ap, l.ap, rr.ap, start=st_, stop=sp_),
                         reads=[VPb[h % 2][:, r, i, :], pts[r]] + ([pts[1]] if r == 0 else []), writes=[bank(ob, ncol, 128 * o_)])
                if last:
                    rv = Rb[64:128, jj, :]
                    ov = psF[64:128, 512 * ob:512 * ob + 512]
                    P.op("dve", lambda e, o=rv, i_=ov: e.reciprocal(o.ap, i_.ap), reads=[ov], writes=[rv])
                    P.tt("dve", mixT[r0:r0 + 64, hp, c0:c0 + SG], psF[0:64, 512 * ob:512 * ob + 512], rv, ALU.mult)
                    if jj == 1 and h + 2 < 12:
                        load_v(h + 2)
        for s in range(2):
            mem_attn(s)
        out_proj_and_ffn(1, g0)
    final = []
    for t in range(NT):
        for c in range(8):
            P.transpose(bank(c // 4, 128, (c % 4) * 128), hT[:, c, t * 128:(t + 1) * 128], ident.all())
        P.copy("act", io[:, t % 2, 0:512], bank(0))
        P.copy("dve", io[:, t % 2, 512:1024], bank(1))
        final.append(P.dma("sp", out_d[t * 128:(t + 1) * 128, :], io[:, t % 2, :]))
    P.finalize(final_waits=final)
    return nc, P


PARAM_NAMES = ["ln_mix_pre", "ln_mix_post", "ln_ffn_pre", "ln_ffn_post", "ln_mem", "ln_shared", "w_mem_kv", "w_out",
               "w_ffn_gate", "w_ffn_up", "w_ffn_down", "w_in_a", "w_spatial", "b_spatial", "ln_v_g", "ln_v_b",
               "w_shared_kv", "b_forget", "w_in_b"]


def kernel(**inp):
    inp = {k: np.asarray(v) for k, v in inp.items()}
    x, mem = inp["x"], inp["mem"]
    B = x.shape[0]
    ident = np.eye(128, dtype=np.float32)
    tri = np.triu(np.ones((128, 128), np.float32))
    ones = np.ones((128, 128), np.float32)
    zeros = np.zeros((128, 128), np.float32)
    cores = [(b, r) for b in range(B) for r in range(2)]
    params = {k: np.ascontiguousarray(inp[k], dtype=np.float32) for k in PARAM_NAMES}
    nc, _ = build(len(cores))
    maps = []
    for (b, r) in cores:
        m = dict(params)
        m.update({"ident": ident, "tri": tri,
                  "m0": tri if r == 0 else ones, "m1": zeros if r == 0 else tri,
                  "x": np.ascontiguousarray(x[b].reshape(NT, 2, 128, D)[:, r].reshape(T, D)),
                  "mem": np.ascontiguousarray(mem[b])})
        maps.append(m)
    res = run_bass_kernel_spmd(nc, maps, core_ids=list(range(len(cores)))).results
    out = np.empty((B, NT, 2, 128, D), np.float32)
    for ci, (b, r) in enumerate(cores):
        out[b, :, r] = np.asarray(res[ci]["out"]).reshape(NT, 128, D)
    return out.reshape(B, NT * 2 * 128, D)
```

```python
import contextlib
import numpy as np
import concourse.bass as bass
import concourse.mybir as mybir
from concourse.bass_utils import run_bass_kernel_spmd

F32 = mybir.dt.float32
BF16 = mybir.dt.bfloat16
ALU = mybir.AluOpType
AF = mybir.ActivationFunctionType
AX = mybir.AxisListType

N_DMA_SEMS = 12


class View:
    __slots__ = ("buf", "ap", "lo", "hi")

    def __init__(self, buf, ap, lo, hi):
        self.buf, self.ap, self.lo, self.hi = buf, ap, lo, hi


class Buf:
    def __init__(self, name, full_ap, shape, kind):
        self.name, self.full, self.shape, self.kind = name, full_ap, list(shape), kind
        first = 1 if kind != "dram" else 0
        self.first = first
        st = [0] * len(shape)
        s = 1
        for i in range(len(shape) - 1, first - 1, -1):
            st[i] = s
            s *= shape[i]
        self.strides = st
        self.size = s
        self.recs = []

    def __getitem__(self, idx):
        if not isinstance(idx, tuple):
            idx = (idx,)
        idx = tuple(idx) + (slice(None),) * (len(self.shape) - len(idx))
        lo, hi = 0, 1
        for d, (i, n) in enumerate(zip(idx, self.shape)):
            if isinstance(i, slice):
                a = 0 if i.start is None else i.start
                b = n if i.stop is None else i.stop
                assert i.step in (None, 1) and 0 <= a < b <= n, (self.name, idx)
            else:
                assert 0 <= i < n, (self.name, idx)
                a, b = i, i + 1
            if d >= self.first:
                lo += a * self.strides[d]
                hi += (b - 1) * self.strides[d]
        return View(self, self.full[idx], lo, hi)

    def all(self):
        return self[tuple(slice(None) for _ in self.shape)]


class Op:
    __slots__ = ("eng", "fn", "deps", "is_dma", "pos", "need_inc", "count", "waits", "sem", "semval", "guard", "idx", "inc")


class Prog:
    def __init__(self, nc):
        self.nc = nc
        self.stack = contextlib.ExitStack()
        self.ops = []
        self.eng_ops = {e: [] for e in ("pe", "act", "dve", "pool", "sp")}
        self.n_dma = {e: 0 for e in self.eng_ops}
        self.nbuf = 0

    def sbuf(self, name, shape, dtype):
        t = self.stack.enter_context(self.nc.sbuf_tensor("sb_" + name, list(shape), dtype))
        return Buf(name, t[:] if not hasattr(t, "ap") else t.ap(), shape, "sbuf")

    def psum(self, name, shape, dtype):
        t = self.stack.enter_context(self.nc.psum_tensor("ps_" + name, list(shape), dtype))
        b = Buf(name, t[:] if not hasattr(t, "ap") else t.ap(), shape, "psum")
        b.bank_elems = 512 if dtype == F32 else 1024
        return b

    def dram(self, name, shape, dtype, kind="Internal"):
        t = self.nc.dram_tensor(name, list(shape), dtype, kind=kind)
        return Buf(name, t.ap(), shape, "dram")

    def op(self, eng, fn, reads=(), writes=(), dma=False, cc=False):
        o = Op()
        o.eng, o.fn, o.is_dma = eng, fn, dma
        o.deps = {}
        o.need_inc = False
        o.idx = len(self.ops)

        def norm(v):
            be = getattr(v.buf, "bank_elems", None)
            if v is None or be is None:
                return v
            return View(v.buf, v.ap, v.lo // be * be, -(-v.hi // be) * be)
        reads = [norm(v) for v in reads if v is not None]
        writes = [norm(v) for v in writes if v is not None]
        for v in reads:
            if v is None:
                continue
            for r in v.buf.recs:
                if r[0] < v.hi and v.lo < r[1]:
                    if r[2] is not None:
                        o.deps.setdefault(r[2], "raw")
                    r[3].append(o)
        for v in writes:
            keep = []
            for r in v.buf.recs:
                if r[0] < v.hi and v.lo < r[1]:
                    if r[2] is not None:
                        if o.deps.get(r[2]) != "raw":
                            o.deps[r[2]] = "waw"
                    for rd in r[3]:
                        if rd is not o:
                            o.deps.setdefault(rd, "war")
                    if v.lo <= r[0] and r[1] <= v.hi:
                        continue
                keep.append(r)
            keep.append([v.lo, v.hi, o, []])
            v.buf.recs = keep
        o.pos = len(self.eng_ops[eng])
        self.eng_ops[eng].append(o)
        self.ops.append(o)
        o.inc = 16
        if cc:
            self.n_cc = getattr(self, "n_cc", 0) + 1
            o.is_dma = True
            o.sem = ("cc", 0)
            o.semval = self.n_cc
            o.guard = 0
            o.inc = 1
        elif dma:
            k = self.n_dma[eng]
            self.n_dma[eng] += 1
            o.sem = (eng, k % N_DMA_SEMS)
            o.semval = 16 * (k // N_DMA_SEMS + 1)
            o.guard = 16 * (k // N_DMA_SEMS) if k >= N_DMA_SEMS else 0
        return o

    def mm(self, out, lhsT, rhs, start=True, stop=True, **kw):
        return self.op("pe", lambda e: e.matmul(out.ap, lhsT.ap, rhs.ap, start=start, stop=stop, **kw),
                       reads=[lhsT, rhs] + ([] if start else []), writes=[out])

    def transpose(self, out, in_, ident):
        return self.op("pe", lambda e: e.transpose(out.ap, in_.ap, ident.ap), reads=[in_, ident], writes=[out])

    def act(self, out, in_, func, bias=None, scale=1.0, accum=None, eng="act"):
        def fn(e):
            kw = {}
            if bias is not None:
                kw["bias"] = bias.ap if isinstance(bias, View) else bias
            if accum is not None:
                kw["accum_out"] = accum.ap
            sc = scale.ap if isinstance(scale, View) else scale
            return e.activation(out.ap, in_.ap, func, scale=sc, **kw)
        rd = [in_] + [x for x in (bias, scale) if isinstance(x, View)]
        wr = [out] + ([accum] if accum is not None else [])
        return self.op(eng, fn, reads=rd, writes=wr)

    def tt(self, eng, out, a, b, op):
        return self.op(eng, lambda e: e.tensor_tensor(out.ap, a.ap, b.ap, op), reads=[a, b], writes=[out])

    def ts(self, eng, out, a, s1, s2, op0, op1=None, accum=None):
        def fn(e):
            x1 = s1.ap if isinstance(s1, View) else s1
            x2 = s2.ap if isinstance(s2, View) else s2
            kw = {}
            if accum is not None:
                kw["accum_out"] = accum.ap
            if op1 is None:
                return e.tensor_scalar(out.ap, a.ap, x1, x2, op0, **kw)
            return e.tensor_scalar(out.ap, a.ap, x1, x2, op0, op1, **kw)
        rd = [a] + [x for x in (s1, s2) if isinstance(x, View)]
        wr = [out] + ([accum] if accum is not None else [])
        return self.op(eng, fn, reads=rd, writes=wr)

    def stt(self, eng, out, a, s, b, op0, op1):
        def fn(e):
            x = s.ap if isinstance(s, View) else s
            return e.scalar_tensor_tensor(out.ap, a.ap, x, b.ap, op0, op1)
        rd = [a, b] + ([s] if isinstance(s, View) else [])
        return self.op(eng, fn, reads=rd, writes=[out])

    def copy(self, eng, out, in_):
        if eng == "act":
            return self.op(eng, lambda e: e.copy(out.ap, in_.ap), reads=[in_], writes=[out])
        return self.op(eng, lambda e: e.tensor_copy(out.ap, in_.ap), reads=[in_], writes=[out])

    def memset(self, eng, out, val):
        return self.op(eng, lambda e: e.memset(out.ap, val), writes=[out])

    def dma(self, eng, out, in_, **kw):
        return self.op(eng, lambda e: e.dma_start(out=out.ap, in_=in_.ap, **kw), reads=[in_], writes=[out], dma=True)

    def finalize(self, final_waits=()):
        nc = self.nc
        seen = {e: {f: -1 for f in self.eng_ops} for e in self.eng_ops}
        seen_dma = {e: {} for e in self.eng_ops}
        for o in self.ops:
            o.waits = []
            E = o.eng
            for y, kind in sorted(o.deps.items(), key=lambda kv: -kv[0].idx):
                if y.is_dma:
                    if seen_dma[E].get(y.sem, 0) >= y.semval:
                        continue
                    seen_dma[E][y.sem] = y.semval
                    o.waits.append(("dma", y.sem, y.semval))
                    continue
                F = y.eng
                if F == E and not o.is_dma:
                    if E == "pe":
                        continue
                if seen[E][F] >= y.pos:
                    continue
                seen[E][F] = y.pos
                y.need_inc = True
                o.waits.append(("eng", y))
            if o.is_dma and o.guard:
                if seen_dma[E].get(o.sem, 0) < o.guard:
                    seen_dma[E][o.sem] = o.guard
                    o.waits.append(("dma", o.sem, o.guard))
        for e, lst in self.eng_ops.items():
            c = 0
            for o in lst:
                if o.need_inc:
                    c += 1
                    o.count = c
        esem = {e: self.stack.enter_context(nc.semaphore("s_" + e)) for e in self.eng_ops}
        dsem = {}
        for e in self.eng_ops:
            if self.n_dma[e]:
                for i in range(min(N_DMA_SEMS, self.n_dma[e])):
                    dsem[(e, i)] = self.stack.enter_context(nc.semaphore("d_%s_%d" % (e, i)))
        if getattr(self, "n_cc", 0):
            dsem[("cc", 0)] = self.stack.enter_context(nc.semaphore("d_cc"))
        self.stats = {e: (len(l), sum(1 for o in l if o.need_inc), sum(len(o.waits) for o in l)) for e, l in self.eng_ops.items()}
        block = self.stack.enter_context(nc.Block())

        def emit(engobj, lst, extra):
            for o in lst:
                for w in o.waits:
                    if w[0] == "dma":
                        engobj.wait_ge(dsem[w[1]], w[2])
                    else:
                        engobj.wait_ge(esem[w[1].eng], w[1].count)
                ins = o.fn(engobj)
                if o.is_dma:
                    ins.then_inc(dsem[o.sem], o.inc)
                elif o.need_inc:
                    ins.then_inc(esem[o.eng], 1)
            for o in extra:
                if o.is_dma:
                    engobj.wait_ge(dsem[o.sem], o.semval)
                else:
                    engobj.wait_ge(esem[o.eng], o.count)

        fw = list(final_waits)
        for o in fw:
            if not o.is_dma and not o.need_inc:
                raise RuntimeError("final wait on op without inc")

        @block.tensor
        def _(e):
            emit(e, self.eng_ops["pe"], [])

        @block.scalar
        def _(e):
            emit(e, self.eng_ops["act"], [])

        @block.vector
        def _(e):
            emit(e, self.eng_ops["dve"], [])

        @block.gpsimd
        def _(e):
            emit(e, self.eng_ops["pool"], [])

        @block.sync
        def _(e):
            emit(e, self.eng_ops["sp"], fw)

        self.stack.close()


D = 1024
T = 2048
TG = 1024
SG = 512
NT = T // 128
DFF = 2816
NJ = DFF // 128
EPS = 1e-6
LN_EPS = 1e-5


def hull(buf, ap, lo, hi):
    return View(buf, ap, lo, hi)


def build(ncores=8, stop=None):
    nc = bass.Bass("TRN2", target_bir_lowering=False)
    P = Prog(nc)

    def din(name, shape, dt=F32):
        return P.dram(name, shape, dt, kind="ExternalInput")

    ident_d = din("ident", [128, 128])
    tri_d = din("tri", [128, 128])
    m0_d = din("m0", [128, 128])
    m1_d = din("m1", [128, 128])
    x_d = din("x", [T, D])
    mem_d = din("mem", [256, D])
    g_d = [din(n, [2, D]) for n in ("ln_mix_pre", "ln_mix_post", "ln_ffn_pre", "ln_ffn_post")]
    g_mem_d = din("ln_mem", [2, D])
    g_sh_d = din("ln_shared", [D])
    w_mkv_d = din("w_mem_kv", [2, D, 512])
    w_out_d = din("w_out", [2, D, D])
    w_g_d = din("w_ffn_gate", [2, D, DFF])
    w_u_d = din("w_ffn_up", [2, D, DFF])
    w_d_d = din("w_ffn_down", [2, DFF, D])
    w_ina_d = din("w_in_a", [1, D, 1792])
    w_sp_d = din("w_spatial", [1, 6, 128, 128])
    b_sp_d = din("b_spatial", [1, 6, 128])
    lnvg_d = din("ln_v_g", [1, 768])
    lnvb_d = din("ln_v_b", [1, 768])
    w_skv_d = din("w_shared_kv", [D, 1548])
    bfor_d = din("b_forget", [12])
    w_inb_d = din("w_in_b", [1, D, D])
    out_d = P.dram("out", [T, D], F32, kind="ExternalOutput")
    kT_loc = [P.dram("kT_loc%d" % g, [768, TG], BF16) for g in range(2)]
    v_loc = [P.dram("v_loc%d" % g, [TG, 768], BF16) for g in range(2)]
    lf_loc = P.dram("lf_loc", [T, 12], F32)
    kT_all = [P.dram("kT_all%d" % g, [2 * 768, TG], BF16) for g in range(2)]
    v_all = [P.dram("v_all%d" % g, [2 * TG, 768], BF16) for g in range(2)]
    lf_all = P.dram("lf_all", [2 * T, 12], F32)
    rg = [[2 * i, 2 * i + 1] for i in range(ncores // 2)]

    def gather(src, dst):
        P.op("pool", lambda e, s_=src, d_=dst: e.collective_compute("AllGather", ALU.bypass, replica_groups=rg,
                                                                     ins=[s_.full.opt()], outs=[d_.full.opt()]),
             reads=[src.all()], writes=[dst.all()], cc=True)

    hT = P.sbuf("hTs", [128, 8, T], F32)
    xn = P.sbuf("xn", [128, 8 * TG], BF16)
    ARENA_B = 88 * 1024
    arena = P.sbuf("arena", [128, ARENA_B // 2], BF16)
    arena32 = arena.full.bitcast(F32)
    xn32 = xn.full.bitcast(F32)
    psF = P.psum("psF", [128, 7 * 512], F32)
    psT = P.psum("psT", [128, 1024], BF16)

    def bank(b, n=512, o=0):
        return psF[:, 512 * b + o:512 * b + o + n]

    class AV:
        def __init__(self, off_b, shape, dt=BF16, base=None, base32=None):
            base = arena if base is None else base
            base32 = arena32 if base32 is None else base32
            self.base = base
            n = 1
            for s_ in shape:
                n *= s_
            if dt == BF16:
                e0 = off_b // 2
                ap = base.full[:, e0:e0 + n]
                self.lo, self.hi, self.esz = e0, e0 + n, 1
            else:
                e0 = off_b // 4
                ap = base32[:, e0:e0 + n]
                self.lo, self.hi, self.esz = off_b // 2, off_b // 2 + 2 * n, 2
            self.shape = list(shape)
            if len(shape) > 1:
                names = " ".join("a%d" % i for i in range(len(shape)))
                kw = {"a%d" % i: s_ for i, s_ in enumerate(shape)}
                ap = ap.rearrange("p (%s) -> p %s" % (names, names), **kw)
            self.ap = ap
            st = [0] * len(shape)
            s_ = 1
            for i in range(len(shape) - 1, -1, -1):
                st[i] = s_
                s_ *= shape[i]
            self.st = st

        def __getitem__(self, idx):
            if not isinstance(idx, tuple):
                idx = (idx,)
            idx = tuple(idx) + (slice(None),) * (1 + len(self.shape) - len(idx))
            lo, hi = 0, 1
            for d, (i, n) in enumerate(zip(idx[1:], self.shape)):
                if isinstance(i, slice):
                    a = 0 if i.start is None else i.start
                    b = n if i.stop is None else i.stop
                else:
                    a, b = i, i + 1
                assert 0 <= a < b <= n
                lo += a * self.st[d]
                hi += (b - 1) * self.st[d]
            return View(self.base, self.ap[idx], self.lo + lo * self.esz, self.lo + hi * self.esz)

    io = AV(32768, [2, D], F32)
    gmem = AV(24576, [D], F32)

    class YV:
        def __getitem__(self, idx):
            p, c, cols = idx
            a = 0 if cols.start is None else cols.start
            b = 512 if cols.stop is None else cols.stop
            e0 = c * 512 + a
            return View(xn, xn32[p, e0:e0 + (b - a)], 2 * e0, 2 * (e0 + b - a))
    y = YV()

    def xnv(c, t0, n, p=slice(None)):
        return xn[p, c * TG + t0:c * TG + t0 + n]

    ident = P.sbuf("ident", [128, 128], F32)
    identb = P.sbuf("identb", [128, 128], BF16)
    tri = P.sbuf("tri", [128, 128], F32)
    onesS = P.sbuf("onesS", [128, 128], BF16)
    ones32 = P.sbuf("ones32", [128, 128], F32)
    gains = P.sbuf("gains", [128, 9, 8], F32)
    KmT = P.sbuf("KmT", [128, 2, 256], BF16)
    Vm = P.sbuf("Vm", [128, 2, 4, 128], BF16)
    sq = P.sbuf("sq", [128, 2, 512], BF16)
    rstd = P.sbuf("rstd", [128, 512], F32)
    tmpn = P.sbuf("tmpn", [128, 2, 512], F32)
    st = P.sbuf("st", [128, 16], F32)
    PT = P.sbuf("PT", [128, 6, 512], BF16)
    Rb = P.sbuf("Rb", [128, 2, 512], F32)
    WmT = P.sbuf("WmT", [128, 6, 128], BF16)
    bsp = P.sbuf("bsp", [128, 6], F32)
    bfor = P.sbuf("bfor", [128, 12], F32)
    lfst = P.sbuf("lfst", [128, NT, 12], F32)
    msk = P.sbuf("msk", [128, 2, 128], BF16)
    lshare = P.sbuf("lshare", [128, 3072], BF16)
    lshare32 = lshare.full.bitcast(F32)
    lnvg = AV(0, [768], F32, lshare, lshare32)
    lnvb = AV(3072, [768], F32, lshare, lshare32)
    BT = AV(0, [4, 2, NT, 12], F32, lshare, lshare32)

    P.dma("sp", ident.all(), ident_d.all())
    P.dma("sp", tri.all(), tri_d.all())
    P.copy("dve", identb.all(), ident.all())
    P.memset("dve", onesS.all(), 1.0 / 1024.0)
    P.memset("dve", ones32.all(), 1.0)
    P.memset("dve", Vm.all(), 1.0)

    def gcol(dst_i, src, ap):
        P.dma("sp", gains[:, dst_i, :], hull(src, ap.rearrange("(c p) -> p c", p=128), 0, src.size),
              allow_slow_non_contiguous=True)
    for L_ in range(2):
        for q in range(4):
            gcol(L_ * 4 + q, g_d[q], g_d[q].full[L_])
    gcol(8, g_sh_d, g_sh_d.full)

    def wload(dst, wd, wap, c0, c1, eng="pool"):
        src = wap[:, c0:c1].rearrange("(k p) c -> p k c", p=128)
        return P.dma(eng, dst, hull(wd, src, 0, wd.size))

    WA = AV(0, [8, 1792])
    WB = AV(28672, [8, 1024])
    mixT = AV(45056, [8, TG])
    qmT = AV(61440, [2, TG])
    TMP0 = 65536
    actb = AV(0, [NJ, TG])
    GU = [AV(45056 + i * 4096, [2, 8, 128]) for i in range(4)]
    DN = [AV(61440 + i * 5632, [NJ, 128]) for i in range(3)]
    QT = AV(0, [6, TG])
    WIN = AV(12288, [8, 1024])
    KTp = AV(65536, [2, T])
    VP = AV(73728, [2, NT, 2, 128])

    def mem_prep(L):
        wmk = AV(0, [8, 512])
        memT = AV(8192, [8, 256])
        memn = AV(8192 + 4096, [2, D])
        P.dma("sp", gmem[:, :], hull(g_mem_d, g_mem_d.full[L].partition_broadcast(128), 0, g_mem_d.size))
        wload(wmk[:, :, :], w_mkv_d, w_mkv_d.full[L], 0, 512)
        for mt in range(2):
            mtile = io[:, mt, :]
            P.dma("sp", mtile, mem_d[mt * 128:(mt + 1) * 128, :])
            P.act(tmpn[:, 0, :], io[:, mt, 0:512], AF.Square, accum=st[:, 0:1])
            P.act(tmpn[:, 1, :], io[:, mt, 512:1024], AF.Square, accum=st[:, 1:2])
            P.tt("dve", st[:, 2:3], st[:, 0:1], st[:, 1:2], ALU.add)
            P.act(st[:, 3:4], st[:, 2:3], AF.Sqrt, bias=EPS, scale=1.0 / D)
            P.op("dve", lambda e, o=st[:, 4:5], i=st[:, 3:4]: e.reciprocal(o.ap, i.ap), reads=[st[:, 3:4]], writes=[st[:, 4:5]])
            P.stt("dve", memn[:, mt, :], mtile, st[:, 4:5], gmem[:, :], ALU.mult, ALU.mult)
            for c in range(8):
                P.transpose(psT[:, c * 128:(c + 1) * 128], memn[:, mt, c * 128:(c + 1) * 128], identb.all())
            P.copy("dve", memT[:, :, mt * 128:(mt + 1) * 128],
                   hull(psT, psT.full.rearrange("p (c t) -> p c t", t=128), 0, 1024))
        for m in range(2):
            for k in range(8):
                P.mm(bank(4 + m, 256), wmk[:, k, m * 128:(m + 1) * 128], memT[:, k, :], start=(k == 0), stop=(k == 7))
            P.copy("act", KmT[:, m, :], bank(4 + m, 256))
        for mt in range(2):
            for k in range(8):
                P.mm(bank(mt, 256), memT[:, k, mt * 128:(mt + 1) * 128], wmk[:, k, 256:512], start=(k == 0), stop=(k == 7))
            P.copy("dve", Vm[:, mt, :, 0:64],
                   hull(psF, psF.full[:, 512 * mt:512 * mt + 256].rearrange("p (h d) -> p h d", d=64), 512 * mt, 512 * mt + 256))

    def rms_stats(src_fn):
        for c in range(8):
            P.act(sq[:, c % 2, :], src_fn(c), AF.Square)
            P.mm(bank(6), onesS.all(), sq[:, c % 2, :], start=(c == 0), stop=(c == 7))
        rstd_from_bank6()

    def rstd_from_bank6():
        P.act(tmpn[:, 0, :], bank(6), AF.Ln, bias=EPS)
        P.act(rstd.all(), tmpn[:, 0, :], AF.Exp, scale=-0.5)

    def norm_to_xn(gi, g0, s):
        t0 = g0 + s * SG
        rms_stats(lambda c: hT[:, c, t0:t0 + SG])
        for c in range(8):
            P.stt("dve", xnv(c, s * SG, SG), hT[:, c, t0:t0 + SG], gains[:, gi, c:c + 1], rstd.all(), ALU.mult, ALU.mult)

    def proj_resid(gi, g0, s, mm_fn):
        t0 = g0 + s * SG
        for m in range(9):
            if m < 8:
                b = 4 + m % 2
                mm_fn(m, bank(b))
                P.act(sq[:, m % 2, :], bank(b), AF.Square)
                P.act(y[:, m, :], bank(b), AF.Copy, scale=gains[:, gi, m:m + 1])
            if m >= 1:
                P.mm(bank(6), onesS.all(), sq[:, (m - 1) % 2, :], start=(m == 1), stop=(m == 8))
        rstd_from_bank6()
        for c in range(8):
            P.tt("dve", tmpn[:, c % 2, :], y[:, c, :], rstd.all(), ALU.mult)
            P.tt("dve", hT[:, c, t0:t0 + SG], hT[:, c, t0:t0 + SG], tmpn[:, c % 2, :], ALU.add)

    def mem_attn(s):
        c0 = s * SG
        for hm in range(4):
            m, r0 = hm // 2, (hm % 2) * 64
            for mt in range(2):
                P.mm(bank(mt), KmT[r0:r0 + 64, m, mt * 128:(mt + 1) * 128], qmT[r0:r0 + 64, m, c0:c0 + SG])
                P.act(PT[:, mt, :], bank(mt), AF.Exp, scale=0.125)
            ob = 2 + hm % 2
            for mt in range(2):
                P.mm(bank(ob), Vm[:, mt, hm, :], PT[:, mt, :], start=(mt == 0), stop=(mt == 1))
            rv = Rb[64:128, hm % 2, :]
            ov = psF[64:128, 512 * ob:512 * ob + 512]
            P.op("dve", lambda e, o=rv, i=ov: e.reciprocal(o.ap, i.ap), reads=[ov], writes=[rv])
            P.tt("dve", mixT[r0:r0 + 64, 6 + m, c0:c0 + SG], psF[0:64, 512 * ob:512 * ob + 512], rv, ALU.mult)

    def out_proj_and_ffn(L, g0):
        for s in range(2):
            c0 = s * SG

            def mm_out(m, bv, c0=c0):
                for k in range(8):
                    P.mm(bv, WB[:, k, m * 128:(m + 1) * 128], mixT[:, k, c0:c0 + SG], start=(k == 0), stop=(k == 7))
            proj_resid(L * 4 + 1, g0, s, mm_out)
        for s in range(2):
            norm_to_xn(L * 4 + 2, g0, s)

        def load_gu(j):
            slot = GU[j % 4]
            wload(slot[:, 0, :, :], w_g_d, w_g_d.full[L], j * 128, (j + 1) * 128)
            wload(slot[:, 1, :, :], w_u_d, w_u_d.full[L], j * 128, (j + 1) * 128)
        for j in range(3):
            load_gu(j)
        for j in range(NJ):
            if j + 3 < NJ:
                load_gu(j + 3)
            slot = GU[j % 4]
            for s in range(2):
                c0 = s * SG
                gb, ub = s, 2 + s
                for k in range(8):
                    P.mm(bank(gb), slot[:, 0, k, :], xnv(k, c0, SG), start=(k == 0), stop=(k == 7))
                for k in range(8):
                    P.mm(bank(ub), slot[:, 1, k, :], xnv(k, c0, SG), start=(k == 0), stop=(k == 7))
                P.act(tmpn[:, s, :], bank(gb), AF.Silu)
                P.tt("dve", actb[:, j, c0:c0 + SG], bank(ub), tmpn[:, s, :], ALU.mult)

        def load_dn(q):
            m = q % 8
            wload(DN[q % 3][:, :, :], w_d_d, w_d_d.full[L], m * 128, (m + 1) * 128)
        load_dn(0)
        load_dn(1)
        for s in range(2):
            c0 = s * SG

            def mm_dn(m, bv, c0=c0, s=s):
                q = s * 8 + m
                if q + 2 < 16:
                    load_dn(q + 2)
                for j in range(NJ):
                    P.mm(bv, DN[q % 3][:, j, :], actb[:, j, c0:c0 + SG], start=(j == 0), stop=(j == NJ - 1))
            proj_resid(L * 4 + 3, g0, s, mm_dn)

    mem_prep(0)
    wsp32 = AV(16384, [6, 128], F32)
    P.dma("sp", wsp32[:, :, :], hull(w_sp_d, w_sp_d.full[0].rearrange("g t s -> t g s"), 0, w_sp_d.size))
    for g in range(6):
        P.transpose(bank(g % 4, 128), wsp32[:, g, :], ident.all())
        P.tt("dve", WmT[:, g, :], bank(g % 4, 128), tri.all(), ALU.mult)
    P.dma("sp", bsp.all(), hull(b_sp_d, b_sp_d.full[0].rearrange("g t -> t g"), 0, b_sp_d.size),
          allow_slow_non_contiguous=True)
    P.dma("sp", lnvg[:, :], hull(lnvg_d, lnvg_d.full[0].partition_broadcast(128), 0, 768))
    P.dma("sp", lnvb[:, :], hull(lnvb_d, lnvb_d.full[0].partition_broadcast(128), 0, 768))
    P.dma("sp", bfor.all(), hull(bfor_d, bfor_d.full.partition_broadcast(128), 0, 12))
    for t in range(NT):
        xt = io[:, t % 2, :]
        P.dma("sp", xt, x_d[t * 128:(t + 1) * 128, :])
        for c in range(8):
            P.transpose(bank(c // 4, 128, (c % 4) * 128), io[:, t % 2, c * 128:(c + 1) * 128], ident.all())
        for hb in range(2):
            P.copy("act" if hb == 0 else "dve", hT[:, 4 * hb:4 * hb + 4, t * 128:(t + 1) * 128],
                   hull(psF, psF.full[:, 512 * hb:512 * hb + 512].rearrange("p (c t) -> p c t", t=128), 512 * hb, 512 * hb + 512))

    for g in range(2):
        g0 = g * TG
        wload(WA[:, :, :], w_ina_d, w_ina_d.full[0], 0, 1792)
        wload(WB[:, :, :], w_out_d, w_out_d.full[0], 0, D)
        for s in range(2):
            norm_to_xn(0, g0, s)
        for s in range(2):
            c0 = s * SG
            for m in range(2):
                for k in range(8):
                    P.mm(bank(4 + m), WA[:, k, 1536 + m * 128:1536 + (m + 1) * 128], xnv(k, c0, SG), start=(k == 0), stop=(k == 7))
                P.copy("act", qmT[:, m, c0:c0 + SG], bank(4 + m))
        def tmp_bufs(tl):
            o = TMP0 + (tl % 2) * 10752
            return (AV(o, [768]), AV(o + 1536, [768], F32), AV(o + 4608, [768], F32), AV(o + 7680, [768]), AV(o + 9216, [768]))

        def mix_proj(tl):
            gu, gv, vn, vln, mainb = tmp_bufs(tl)
            for n in range(3):
                for k in range(8):
                    P.mm(bank(n), xnv(k, tl * 128, 128), WA[:, k, n * 512:(n + 1) * 512], start=(k == 0), stop=(k == 7))
            P.act(gu[:, :], psF[:, 0:768], AF.Gelu_apprx_tanh)
            P.act(gv[:, :], psF[:, 768:1536], AF.Gelu_apprx_tanh, accum=st[:, 8:9])
            P.act(vn[:, :], gv[:, :], AF.Square, accum=st[:, 9:10])

        def mix_ln(tl):
            gu, gv, vn, vln, mainb = tmp_bufs(tl)
            P.ts("dve", st[:, 10:11], st[:, 8:9], -1.0 / 768, None, ALU.mult)
            P.ts("dve", st[:, 11:12], st[:, 9:10], 1.0 / 768, None, ALU.mult)
            P.stt("dve", st[:, 12:13], st[:, 10:11], st[:, 10:11], st[:, 11:12], ALU.mult, ALU.subtract)
            P.act(st[:, 13:14], st[:, 12:13], AF.Sqrt, bias=LN_EPS, scale=-1.0)
            P.op("dve", lambda e, o_=st[:, 14:15], i_=st[:, 13:14]: e.reciprocal(o_.ap, i_.ap), reads=[st[:, 13:14]], writes=[st[:, 14:15]])
            P.ts("dve", vn[:, :], gv[:, :], st[:, 10:11], st[:, 14:15], ALU.add, ALU.mult)
            P.tt("dve", gv[:, :], vn[:, :], lnvg[:, :], ALU.mult)
            P.tt("dve", vln[:, :], gv[:, :], lnvb[:, :], ALU.add)

        def mix_spatial(tl):
            gu, gv, vn, vln, mainb = tmp_bufs(tl)
            for gg in range(6):
                P.mm(psF[:, 1536 + gg * 128:1536 + (gg + 1) * 128], WmT[:, gg, :], vln[:, gg * 128:(gg + 1) * 128])
            for gg in range(6):
                P.stt("dve", mainb[:, gg * 128:(gg + 1) * 128], psF[:, 1536 + gg * 128:1536 + (gg + 1) * 128],
                      bsp[:, gg:gg + 1], gu[:, gg * 128:(gg + 1) * 128], ALU.add, ALU.mult)

        def mix_tr(tl):
            gu, gv, vn, vln, mainb = tmp_bufs(tl)
            for gg in range(6):
                P.transpose(psT[:, gg * 128:(gg + 1) * 128], mainb[:, gg * 128:(gg + 1) * 128], identb.all())
            P.copy("act", mixT[:, 0:6, tl * 128:(tl + 1) * 128],
                   hull(psT, psT.full[:, 0:768].rearrange("p (c t) -> p c t", t=128), 0, 768))

        ntl = TG // 128
        mix_proj(0)
        mix_ln(0)
        for tl in range(ntl):
            if tl + 1 < ntl:
                mix_proj(tl + 1)
            mix_spatial(tl)
            if tl + 1 < ntl:
                mix_ln(tl + 1)
            if tl >= 1:
                mix_tr(tl - 1)
        mix_tr(ntl - 1)
        for s in range(2):
            mem_attn(s)
        out_proj_and_ffn(0, g0)
        wload(WA[:, :, 0:1548], w_skv_d, w_skv_d.full, 0, 1548)
        for s in range(2):
            norm_to_xn(8, g0, s)
        kst = AV(45056, [2, 6, SG])
        vst = AV(45056 + 12288, [2, 768])
        for s in range(2):
            c0 = s * SG
            for m in range(6):
                b = 4 + m % 2
                for k in range(8):
                    P.mm(bank(b), WA[:, k, m * 128:(m + 1) * 128], xnv(k, c0, SG), start=(k == 0), stop=(k == 7))
                P.copy("act", kst[:, s, m, :], bank(b))
            P.dma("sp", hull(kT_loc[g], kT_loc[g].full.rearrange("(m p) t -> p m t", p=128)[:, :, c0:c0 + SG], 0, kT_loc[g].size),
                  kst[:, s, :, :])
        for tl in range(TG // 128):
            tg = g * 8 + tl
            for hh in range(2):
                for k in range(8):
                    P.mm(bank(hh, 384), xnv(k, tl * 128, 128), WA[:, k, 768 + hh * 384:768 + (hh + 1) * 384], start=(k == 0), stop=(k == 7))
                P.copy("act" if hh == 0 else "dve", vst[:, tl % 2, hh * 384:(hh + 1) * 384], bank(hh, 384))
            P.dma("sp", hull(v_loc[g], v_loc[g].full[tl * 128:(tl + 1) * 128, :], 0, v_loc[g].size), vst[:, tl % 2, :])
            for k in range(8):
                P.mm(bank(6, 12), xnv(k, tl * 128, 128), WA[:, k, 1536:1548], start=(k == 0), stop=(k == 7))
            P.tt("dve", st[:, 0:12], bank(6, 12), bfor.all(), ALU.add)
            P.act(tmpn[:, 0, 0:12], st[:, 0:12], AF.Exp, scale=-1.0)
            P.act(tmpn[:, 0, 16:28], tmpn[:, 0, 0:12], AF.Ln, bias=1.0)
            P.ts("dve", lfst[:, tg, :], tmpn[:, 0, 16:28], -1.0, None, ALU.mult)
        gather(kT_loc[g], kT_all[g])
        gather(v_loc[g], v_all[g])
    P.dma("sp", hull(lf_loc, lf_loc.full.rearrange("(i p) h -> p i h", p=128), 0, lf_loc.size), lfst.all(),
          allow_slow_non_contiguous=True)

    gather(lf_loc, lf_all)

    mem_prep(1)
    m32 = AV(47200, [2, 128], F32)
    P.dma("sp", m32[:, 0, :], m0_d.all())
    P.dma("sp", m32[:, 1, :], m1_d.all())
    P.copy("dve", msk.all(), m32[:, :, :])
    LF = AV(40960, [2, NT, 12], F32)
    LFf = AV(40960, [2 * NT * 12], F32)
    CT = AV(42496, [2, NT, 12], F32)
    CTf = AV(42496, [2 * NT * 12], F32)
    TOT = AV(44032, [2, NT, 12], F32)
    TOTf = AV(44032, [2 * NT * 12], F32)
    OFF = AV(45568, [2, NT + 1, 12], F32)
    P.dma("sp", LF[:, :, :, :], hull(lf_all, lf_all.full.rearrange("(r i p) h -> p r i h", p=128, i=NT), 0, lf_all.size),
          allow_slow_non_contiguous=True)
    P.mm(bank(0, 384), tri.all(), LFf[:, :])
    P.mm(bank(1, 384), ones32.all(), LFf[:, :])
    P.copy("dve", CTf[:, :], bank(0, 384))
    P.copy("act", TOTf[:, :], bank(1, 384))
    P.memset("dve", OFF[:, 0, 0, :], 0.0)
    for G in range(2 * NT):
        r, i = G % 2, G // 2
        r2, i2 = (G + 1) % 2, (G + 1) // 2
        P.tt("dve", OFF[:, r2, i2, :], OFF[:, r, i, :], TOT[:, r, i, :], ALU.add)
    for r in range(2):
        P.tt("dve", CT[:, r, :, :], CT[:, r, :, :], OFF[:, r, 0:NT, :], ALU.add)
    for j in range(4):
        for r in range(2):
            ni = 4 * j + 4
            lo_ = OFF.lo + 2 * (4 * j * 12)
            cref = View(arena, OFF.ap[:, 0, 4 * j:4 * j + 1, :].to_broadcast([128, ni, 12]), lo_, lo_ + 24)
            P.tt("dve", BT[:, j, r, 0:ni, :], cref, CT[:, r, 0:ni, :], ALU.subtract)

    for g in range(2):
        if stop == 1 or (stop == 2 and g == 1):
            break
        g0 = g * TG
        P.memset("pool", VP[:, :, :, :, :], 1.0)
        wload(WIN[:, :, :], w_inb_d, w_inb_d.full[0], 0, D)
        wload(WB[:, :, :], w_out_d, w_out_d.full[1], 0, D)
        for s in range(2):
            norm_to_xn(4, g0, s)
        for s in range(2):
            c0 = s * SG
            for m in range(8):
                b = 4 + m % 2
                for k in range(8):
                    P.mm(bank(b), WIN[:, k, m * 128:(m + 1) * 128], xnv(k, c0, SG), start=(k == 0), stop=(k == 7))
                if m < 6:
                    P.copy("act", QT[:, m, c0:c0 + SG], bank(b))
                else:
                    P.copy("act", qmT[:, m - 6, c0:c0 + SG], bank(b))
        KTb = [KTp, AV(12288, [2, T])]
        VPb = [AV(73728 + 8192 * i_, [2, NT, 128]) for i_ in range(2)]

        def load_k(hp):
            for g2 in range(2):
                P.dma("sp", KTb[hp % 2][:, :, g2 * TG:(g2 + 1) * TG],
                      hull(kT_all[g2], kT_all[g2].full.rearrange("(r f) t -> f r t", r=2)[hp * 128:(hp + 1) * 128], 0, kT_all[g2].size))

        def load_v(h):
            cc = h * 64
            for g2 in range(2):
                for r_ in range(2):
                    P.dma("sp", VPb[h % 2][:, r_, g2 * 8:(g2 + 1) * 8, 0:64],
                          hull(v_all[g2], v_all[g2].full[r_ * TG:(r_ + 1) * TG, cc:cc + 64].rearrange("(i p) d -> p i d", p=128), 0, v_all[g2].size))

        pairs = []
        for h in range(12):
            for jj in range(2):
                j = 2 * g + jj
                for i in range(4 * j + 4):
                    pairs.append((h, jj, j, i, i == 0, i == 4 * j + 3))
        LA = 2
        load_k(0)
        load_v(0)
        load_v(1)
        for n in range(len(pairs) + LA):
            if n < len(pairs):
                h, jj, j, i, first, last = pairs[n]
                hp, e_ = h // 2, h % 2
                r0 = 64 * e_
                if first and jj == 0 and e_ == 0 and hp + 1 < 6:
                    load_k(hp + 1)
                c0 = jj * SG
                o_ = max(0, i - 4 * j)
                ncol = SG - 128 * o_
                b0 = 2 * (n % 2)
                both = psF[:, 512 * b0:512 * b0 + 1024]
                for r in range(2):
                    o = P.op("pe", lambda e, out=bank(b0 + r, ncol), l=KTb[hp % 2][r0:r0 + 64, r, i * 128:(i + 1) * 128],
                             rr=QT[r0:r0 + 64, hp, c0 + 128 * o_:c0 + SG]: e.matmul(out.ap, l.ap, rr.ap, start=True, stop=True),
                             reads=[KTb[hp % 2][r0:r0 + 64, r, i * 128:(i + 1) * 128], QT[r0:r0 + 64, hp, c0 + 128 * o_:c0 + SG]],
                             writes=[both if r == 0 else bank(b0 + r, ncol)])
                for r in range(2):
                    pt = PT[:, (2 * n + r) % 6, 0:ncol]
                    P.act(pt, bank(b0 + r, ncol), AF.Exp, bias=BT[:, j, r, i, h:h + 1], scale=0.125)
                    if i >= 4 * j:
                        ptd = View(PT, pt.ap[:, 0:128], pt.lo, pt.lo + 128)
                        P.tt("pool", ptd, ptd, msk[:, r, :], ALU.mult)
            if n >= LA:
                m_ = n - LA
                h, jj, j, i, first, last = pairs[m_]
                hp, e_ = h // 2, h % 2
                r0 = 64 * e_
                c0 = jj * SG
                o_ = max(0, i - 4 * j)
                ncol = SG - 128 * o_
                ob = 4 + (2 * h + jj) % 2
                pts = [PT[:, (2 * m_ + r) % 6, 0:ncol] for r in range(2)]
                for r in range(2):
                    P.op("pe", lambda e, out=bank(ob, ncol, 128 * o_), l=VPb[h % 2][:, r, i, :], rr=pts[r], st_=(first and r == 0),
                         sp_=(last and r == 1): e.matmul(out.ap, l.ap, rr.ap, start=st_, stop=sp_),
                         reads=[VPb[h % 2][:, r, i, :], pts[r]] + ([pts[1]] if r == 0 else []), writes=[bank(ob, ncol, 128 * o_)])
                if last:
                    rv = Rb[64:128, jj, :]
                    ov = psF[64:128, 512 * ob:512 * ob + 512]
                    P.op("dve", lambda e, o=rv, i_=ov: e.reciprocal(o.ap, i_.ap), reads=[ov], writes=[rv])
                    P.tt("dve", mixT[r0:r0 + 64, hp, c0:c0 + SG], psF[0:64, 512 * ob:512 * ob + 512], rv, ALU.mult)
                    if jj == 1 and h + 2 < 12:
                        load_v(h + 2)
        for s in range(2):
            mem_attn(s)
        out_proj_and_ffn(1, g0)
    final = []
    for t in range(NT):
        for c in range(8):
            P.transpose(bank(c // 4, 128, (c % 4) * 128), hT[:, c, t * 128:(t + 1) * 128], ident.all())
        P.copy("act", io[:, t % 2, 0:512], bank(0))
        P.copy("dve", io[:, t % 2, 512:1024], bank(1))
        final.append(P.dma("sp", out_d[t * 128:(t + 1) * 128, :], io[:, t % 2, :]))
    P.finalize(final_waits=final)
    return nc, P


PARAM_NAMES = ["ln_mix_pre", "ln_mix_post", "ln_ffn_pre", "ln_ffn_post", "ln_mem", "ln_shared", "w_mem_kv", "w_out",
               "w_ffn_gate", "w_ffn_up", "w_ffn_down", "w_in_a", "w_spatial", "b_spatial", "ln_v_g", "ln_v_b",
               "w_shared_kv", "b_forget", "w_in_b"]


def kernel(**inp):
    inp = {k: np.asarray(v) for k, v in inp.items()}
    x, mem = inp["x"], inp["mem"]
    B = x.shape[0]
    ident = np.eye(128, dtype=np.float32)
    tri = np.triu(np.ones((128, 128), np.float32))
    ones = np.ones((128, 128), np.float32)
    zeros = np.zeros((128, 128), np.float32)
    cores = [(b, r) for b in range(B) for r in range(2)]
    params = {k: np.ascontiguousarray(inp[k], dtype=np.float32) for k in PARAM_NAMES}
    nc, _ = build(len(cores))
    maps = []
    for (b, r) in cores:
        m = dict(params)
        m.update({"ident": ident, "tri": tri,
                  "m0": tri if r == 0 else ones, "m1": zeros if r == 0 else tri,
                  "x": np.ascontiguousarray(x[b].reshape(NT, 2, 128, D)[:, r].reshape(T, D)),
                  "mem": np.ascontiguousarray(mem[b])})
        maps.append(m)
    res = run_bass_kernel_spmd(nc, maps, core_ids=list(range(len(cores)))).results
    out = np.empty((B, NT, 2, 128, D), np.float32)
    for ci, (b, r) in enumerate(cores):
        out[b, :, r] = np.asarray(res[ci]["out"]).reshape(NT, 128, D)
    return out.reshape(B, NT * 2 * 128, D)
```
